# Optimizing a Trainium2 kernel written in Bass

```python
import jax
import jax.numpy as jnp
from jax import lax
import numpy as np


D_MODEL = 1024
BATCH = 16
SEQ = 2048
DEPTH = 2

CHUNK = 64
HEAD_DIM = 64
N_HEADS_A = 8
N_IDX_HEADS = 8
D_IDX = 64
TOPK_MAX = 256
Q_BLOCK = 128
N_HEADS_B = 8
N_PAST_CHUNKS = 8
BAND = (N_PAST_CHUNKS + 1) * CHUNK
REL_LO = -(CHUNK - 1)
REL_HI = 256
N_REL = REL_HI - REL_LO + 1
ROT_DIM = HEAD_DIM // 4
ROPE_THETA = 500000.0
D_FF = -(-8 * D_MODEL // (3 * 256)) * 256
EPS = 1e-6
W_A = N_HEADS_A * HEAD_DIM
W_B = N_HEADS_B * HEAD_DIM
INDEX_SCALE = (D_IDX ** -0.5) * (N_IDX_HEADS ** -0.5)
IN_SIZES = (W_A, HEAD_DIM, HEAD_DIM, N_IDX_HEADS * D_IDX, D_IDX, N_IDX_HEADS,
            W_B, W_B, W_B, D_MODEL, D_MODEL)
D_IN = sum(IN_SIZES)

kernel_name = 'chunk_causal_hybrid_dsa_band_attention_block'


def rms_norm(x, g):
    xf = x.astype(jnp.float32)
    y = xf * lax.rsqrt(jnp.mean(xf * xf, axis=-1, keepdims=True) + EPS)
    return (y * g.astype(jnp.float32)).astype(x.dtype)


def rope_tables(positions):
    inv_freq = ROPE_THETA ** (-jnp.arange(0, ROT_DIM, 2, dtype=jnp.float32) / ROT_DIM)
    ang = positions.astype(jnp.float32)[..., None] * inv_freq
    return jnp.cos(ang)[:, :, None, :], jnp.sin(ang)[:, :, None, :]


def partial_rope(x, cos, sin):
    half = ROT_DIM // 2
    xr = x[..., :ROT_DIM].astype(jnp.float32)
    x1, x2 = xr[..., :half], xr[..., half:]
    rot = jnp.concatenate([x1 * cos - x2 * sin, x2 * cos + x1 * sin], axis=-1)
    return jnp.concatenate([rot.astype(x.dtype), x[..., ROT_DIM:]], axis=-1)


def to_blocks(a, size):
    b, s = a.shape[:2]
    return a.reshape(b, s // size, size, *a.shape[2:]).swapaxes(0, 1)


def from_blocks(a):
    n, b, size = a.shape[:3]
    return a.swapaxes(0, 1).reshape(b, n * size, *a.shape[3:])


def split_cols(z):
    offs, acc = [], 0
    for w in IN_SIZES[:-1]:
        acc += w
        offs.append(acc)
    return jnp.split(z, offs, axis=-1)


def sparse_indexed_attention(q, k, v, q_idx, k_idx, w_idx):
    s_len = q.shape[1]
    top_k = min(TOPK_MAX, s_len // 4)
    key_pos = jnp.arange(s_len)

    def block(args):
        qb, qib, wb, bid = args
        t = bid * Q_BLOCK + jnp.arange(Q_BLOCK)
        limit = (t // CHUNK + 1) * CHUNK
        admissible = key_pos[None, :] < limit[:, None]
        logits = jnp.einsum('bqhd,bsd->bqhs', qib, k_idx).astype(jnp.float32)
        score = jnp.einsum('bqh,bqhs->bqs', wb.astype(jnp.float32), jax.nn.relu(logits)) * INDEX_SCALE
        score = jnp.where(admissible[None], score, -jnp.inf)
        _, sel = lax.top_k(score, top_k)
        valid = sel < limit[None, :, None]
        kg = jax.vmap(lambda kb, ib: kb[ib])(k, sel)
        vg = jax.vmap(lambda vb, ib: vb[ib])(v, sel)
        s = jnp.einsum('bqhd,bqkd->bqhk', qb, kg).astype(jnp.float32) * (HEAD_DIM ** -0.5)
        s = jnp.where(valid[:, :, None, :], s, -jnp.inf)
        p = jax.nn.softmax(s, axis=-1).astype(vg.dtype)
        return jnp.einsum('bqhk,bqkd->bqhd', p, vg)

    n_blk = s_len // Q_BLOCK
    out = lax.map(block, (to_blocks(q, Q_BLOCK), to_blocks(q_idx, Q_BLOCK),
                          to_blocks(w_idx, Q_BLOCK), jnp.arange(n_blk)))
    return from_blocks(out)


def chunked_band_attention(q, k, v, rel_bias):
    s_len = q.shape[1]
    pad = N_PAST_CHUNKS * CHUNK
    kp = jnp.pad(k, ((0, 0), (pad, 0), (0, 0), (0, 0)))
    vp = jnp.pad(v, ((0, 0), (pad, 0), (0, 0), (0, 0)))
    i = jnp.arange(CHUNK)[:, None]
    j = jnp.arange(BAND)
    rel = jnp.clip(i - j[None, :] + pad, REL_LO, REL_HI) - REL_LO
    bias = rel_bias[:, rel].astype(jnp.float32)

    def chunk(args):
        qc, c = args
        kb = lax.dynamic_slice_in_dim(kp, c * CHUNK, BAND, axis=1)
        vb = lax.dynamic_slice_in_dim(vp, c * CHUNK, BAND, axis=1)
        valid = j >= (N_PAST_CHUNKS - c) * CHUNK
        s = jnp.einsum('bqhd,bkhd->bhqk', qc, kb).astype(jnp.float32) * (HEAD_DIM ** -0.5) + bias[None]
        s = jnp.where(valid[None, None, None, :], s, -jnp.inf)
        p = jax.nn.softmax(s, axis=-1).astype(vb.dtype)
        return jnp.einsum('bhqk,bkhd->bqhd', p, vb)

    out = lax.map(chunk, (to_blocks(q, CHUNK), jnp.arange(s_len // CHUNK)))
    return from_blocks(out)


def setup_inputs(seed: int = 0) -> dict:
    key = jax.random.key(seed)
    ks = jax.random.split(key, 20)
    f32 = jnp.float32
    D = D_MODEL
    nrm = lambda k, shape, scale: jax.random.normal(k, shape, f32) * scale
    x = nrm(ks[0], (BATCH, SEQ, D), 1.0)
    c = nrm(ks[1], (BATCH, D), 1.0)
    offset = jax.random.randint(ks[2], (BATCH, 1), 0, 4096, dtype=jnp.int32)
    positions = (offset + jnp.arange(SEQ, dtype=jnp.int32)[None, :]).astype(jnp.int32)
    return {
        'x': x,
        'c': c,
        'positions': positions,
        'w_ada': nrm(ks[3], (DEPTH, D, 6 * D), 0.5 * D ** -0.5),
        'b_ada': nrm(ks[4], (DEPTH, 6 * D), 0.02),
        'g_mix': 1.0 + nrm(ks[5], (DEPTH, D), 0.02),
        'w_in': nrm(ks[6], (DEPTH, D, D_IN), D ** -0.5),
        'b_in': nrm(ks[7], (DEPTH, D_IN), 0.02),
        'qn_a': 1.0 + nrm(ks[8], (DEPTH, HEAD_DIM), 0.02),
        'kn_a': 1.0 + nrm(ks[9], (DEPTH, HEAD_DIM), 0.02),
        'qn_b': 1.0 + nrm(ks[10], (DEPTH, HEAD_DIM), 0.02),
        'kn_b': 1.0 + nrm(ks[11], (DEPTH, HEAD_DIM), 0.02),
        'rel_bias': nrm(ks[12], (DEPTH, N_HEADS_B, N_REL), 0.1),
        'w_oa': nrm(ks[13], (DEPTH, W_A, D), W_A ** -0.5),
        'w_ob': nrm(ks[14], (DEPTH, W_B, D), W_B ** -0.5),
        'w_out': nrm(ks[15], (DEPTH, D, D), D ** -0.5),
        'g_ffn': 1.0 + nrm(ks[16], (DEPTH, D), 0.02),
        'w_gu': nrm(ks[17], (DEPTH, D, 2 * D_FF), D ** -0.5),
        'w_down': nrm(ks[18], (DEPTH, D_FF, D), D_FF ** -0.5),
    }


def reference(x, c, positions, w_ada, b_ada, g_mix, w_in, b_in, qn_a, kn_a, qn_b, kn_b,
              rel_bias, w_oa, w_ob, w_out, g_ffn, w_gu, w_down):
    b, s_len, _ = x.shape
    cos, sin = rope_tables(positions)
    c_act = jax.nn.silu(c)
    for l in range(DEPTH):
        mod = c_act @ w_ada[l] + b_ada[l]
        sh_m, sc_m, gt_m, sh_f, sc_f, gt_f = jnp.split(mod, 6, axis=-1)

        h = rms_norm(x, g_mix[l]) * (1.0 + sc_m[:, None]) + sh_m[:, None]
        z = h @ w_in[l] + b_in[l]
        qa, ka, va, qi, ki, wi, qb, kb, vb, ga, gb = split_cols(z)

        qa = partial_rope(rms_norm(qa.reshape(b, s_len, N_HEADS_A, HEAD_DIM), qn_a[l]), cos, sin)
        ka = partial_rope(rms_norm(ka.reshape(b, s_len, 1, HEAD_DIM), kn_a[l]), cos, sin)[:, :, 0]
        qi = partial_rope(qi.reshape(b, s_len, N_IDX_HEADS, D_IDX), cos, sin)
        ki = partial_rope(ki.reshape(b, s_len, 1, D_IDX), cos, sin)[:, :, 0]
        y_a = sparse_indexed_attention(qa, ka, va, qi, ki, wi).reshape(b, s_len, W_A) @ w_oa[l]

        qb = rms_norm(qb.reshape(b, s_len, N_HEADS_B, HEAD_DIM), qn_b[l])
        kb = rms_norm(kb.reshape(b, s_len, N_HEADS_B, HEAD_DIM), kn_b[l])
        vb = vb.reshape(b, s_len, N_HEADS_B, HEAD_DIM)
        y_b = chunked_band_attention(qb, kb, vb, rel_bias[l]).reshape(b, s_len, W_B) @ w_ob[l]

        merged = jax.nn.sigmoid(ga) * y_a + jax.nn.sigmoid(gb) * y_b
        x = x + gt_m[:, None] * (merged @ w_out[l])

        h2 = rms_norm(x, g_ffn[l]) * (1.0 + sc_f[:, None]) + sh_f[:, None]
        gate, up = jnp.split(h2 @ w_gu[l], 2, axis=-1)
        x = x + gt_f[:, None] * ((jax.nn.silu(gate) * up) @ w_down[l])
    return x
```

```python
import contextlib
import numpy as np
import concourse.bass as bass
import concourse.mybir as mybir
from concourse.bass_utils import run_bass_kernel_spmd

F32 = mybir.dt.float32
BF16 = mybir.dt.bfloat16
I32 = mybir.dt.int32
I8 = mybir.dt.int8
ALU = mybir.AluOpType
AF = mybir.ActivationFunctionType
AX = mybir.AxisListType

D_MODEL = 1024
SEQ = 2048
DEPTH = 2
N_CORES = 8
NB = 2
HD = 64
D_FF = 2816
IN_SIZES = (512, 64, 64, 512, 64, 8, 512, 512, 512, 1024, 1024)
TOPK = 256
EPS = 1e-6
INDEX_SCALE = (64 ** -0.5) * (8 ** -0.5)
ROPE_THETA = 500000.0
NT = SEQ // 128
NG = 4
NWB = 31
NADA = 12
NSLOT = 3
NIT = 22
NEG = -30000.0
KSLOTS = 8

SEG = 30000
DBG_STAGE = 99
import os
DBG_SKIP = set(os.environ.get('DBG_SKIP', '').split(','))


class Op:
    __slots__ = ("eng", "fn", "deps", "dsem", "signal", "sig")

    def __init__(self, eng, fn, dsem=None):
        self.eng = eng
        self.fn = fn
        self.deps = []
        self.dsem = dsem
        self.signal = False
        self.sig = None


class Sched:
    ENGS = ("pe", "act", "dve", "pool", "sp")

    def __init__(self, nc):
        self.nc = nc
        self.ops = []
        self.last_w = {}
        self.readers = {}

    def op(self, eng, fn, reads=(), writes=(), dsem=None):
        o = Op(eng, fn, dsem)
        deps = set()
        for k in reads:
            w = self.last_w.get(k)
            if w is not None:
                deps.add(w)
        for k in writes:
            w = self.last_w.get(k)
            if w is not None:
                deps.add(w)
            r = self.readers.get(k)
            if r:
                for x in r[0].values():
                    deps.add(x)
                for x in r[1]:
                    deps.add(x)
        for k in reads:
            r = self.readers.get(k)
            if r is None:
                r = self.readers[k] = ({}, [])
            if dsem is None:
                r[0][eng] = o
            else:
                r[1].append(o)
        for k in writes:
            self.last_w[k] = o
            self.readers[k] = ({}, [])
        deps.discard(o)
        for d in deps:
            if d.eng == "pe" and eng == "pe" and d.dsem is None and dsem is None:
                continue
            o.deps.append(d)
            d.signal = True
        self.ops.append(o)
        return o

    def emit(self, final_wait_ops=()):
        nc = self.nc
        for o in final_wait_ops:
            o.signal = True
        for o in self.ops:
            if o.dsem is not None:
                o.signal = True
        print("sched ops:", len(self.ops), {e: sum(1 for o in self.ops if o.eng == e) for e in self.ENGS})
        cnt = {}
        for o in self.ops:
            if not o.signal:
                continue
            key = ("d", o.dsem) if o.dsem is not None else ("e", o.eng)
            n = cnt.get(key, 0) + 1
            cnt[key] = n
            o.sig = (key, n)
        sems = {}
        with contextlib.ExitStack() as stack:
            def getsem(key, seg):
                k = (key, seg)
                if k not in sems:
                    nm = "s%d" % len(sems)
                    sems[k] = stack.enter_context(nc.semaphore(nm))
                return sems[k]

            for key, n in cnt.items():
                for seg in range((n - 1) // SEG + 1):
                    getsem(key, seg)

            def sem_and_val(sig):
                key, n = sig
                if key[0] == "d" and key[1].startswith("all:"):
                    n = cnt[key]
                seg = (n - 1) // SEG
                v = n - seg * SEG
                if key[0] == "d":
                    v *= 16
                return getsem(key, seg), v, (key, seg)

            by_eng = {e: [] for e in self.ENGS}
            for o in self.ops:
                by_eng[o.eng].append(o)
            block = stack.enter_context(nc.Block())

            def make_section(eng_name, final=False):
                def section(eng):
                    waited = {}
                    for o in by_eng[eng_name]:
                        for d in o.deps:
                            s, v, sk = sem_and_val(d.sig)
                            if waited.get(sk, 0) >= v:
                                continue
                            waited[sk] = v
                            eng.wait_ge(s, v)
                        ins = o.fn(eng)
                        if o.signal:
                            key, n = o.sig
                            seg = (n - 1) // SEG
                            ins.then_inc(getsem(key, seg), 16 if o.dsem is not None else 1)
                    if final:
                        for o in final_wait_ops:
                            s, v, sk = sem_and_val(o.sig)
                            if waited.get(sk, 0) >= v:
                                continue
                            waited[sk] = v
                            eng.wait_ge(s, v)
                return section

            block.sync(make_section("sp", final=True))
            block.scalar(make_section("act"))
            block.vector(make_section("dve"))
            block.gpsimd(make_section("pool"))
            block.tensor(make_section("pe"))


def build_program(layers):
    nc = bass.Bass("TRN2", target_bir_lowering=False)
    L = DEPTH

    def dram(name, shape, dt=F32, kind="ExternalInput"):
        return nc.dram_tensor(name, shape, dt, kind=kind).ap()

    d_xT = dram("xT", [NB, 8, 128, SEQ])
    d_out = dram("oT", [NB, 8, 128, SEQ], kind="ExternalOutput")
    d_cT = dram("cT", [128, 8 * NB])
    d_pos = dram("pos", [NB, 128, NT], I32)
    d_invf = dram("invf", [1, 8])
    d_pw = dram("pw", [1, NIT + 1])
    d_wada = dram("wada", [L, NADA, 128, 4096])
    d_badaT = dram("badaT", [L, 128, 48])
    d_gT = dram("gT", [L, 128, 16])
    d_wst = dram("wst", [L, NWB, 128, 4096])
    d_brow = dram("brow", [L, 128, 3072])
    d_bgT = dram("bgT", [L, 128, 16])
    d_gains = dram("gains", [L, 1, 256])
    d_biasT = dram("biasT", [L, 128, 5120])
    d_dbg = dram("dbg", [128, 16384], kind="ExternalOutput") if DBG_STAGE < 99 else None
    d_dbgb = dram("dbgb", [128, 16384], BF16, kind="ExternalOutput") if DBG_STAGE < 99 else None

    with contextlib.ExitStack() as st:
        def sb(name, shape, dt):
            return st.enter_context(nc.sbuf_tensor(name, shape, dt))

        pb = [st.enter_context(nc.psum_tensor("pb%d" % i, [128, 512], F32)) for i in range(8)]
        pbk = ["pb%d" % i for i in range(8)]

        xT = sb("xT_sb", [128, 8, SEQ], F32)
        KAT = sb("KAT", [128, SEQ], BF16)
        KIT = sb("KIT", [128, SEQ], BF16)
        VA = sb("VA", [128, NT, 65], BF16)
        KBT = sb("KBT", [128, 4, KSLOTS, 128], BF16)
        VB = sb("VB", [128, KSLOTS, 8, 65], BF16)
        wsl = sb("wsl", [128, NSLOT, 4096], BF16)
        hT = sb("hT", [128, 8, 512], BF16)
        rstd = sb("rstd", [128, 512], F32)
        tmpA = sb("tmpA", [128, 2, 512], F32)
        sqc = sb("sqc", [128, 2, 512], BF16)
        attnT = sb("attnT", [128, 2, 4, 512], BF16)
        brow = sb("brow_sb", [128, 3072], BF16)
        biasT = sb("biasT_sb", [128, 5, 2, 512], BF16)
        big = sb("big", [128, 43008], I8)
        ident = sb("ident", [128, 128], BF16)
        I4 = sb("I4", [128, 512], BF16)
        ones_s = sb("ones_s", [128, 128], BF16)
        ones_row = sb("ones_row", [128, 128], BF16)
        cTf = sb("cTf", [128, 8 * NB], F32)
        cact = sb("cact", [128, 8 * NB], BF16)
        modT = sb("modT", [128, L, 48, NB], F32)
        badaT = sb("badaT_sb", [128, L, 48], F32)
        gT = sb("gT_sb", [128, L, 16], F32)
        coef = sb("coef", [128, 2, 6, 8], F32)
        bgT = sb("bgT_sb", [128, 2, 16], F32)
        gains = sb("gains_sb", [128, 2, 256], F32)
        invf = sb("invf_sb", [128, 8], F32)
        pw = sb("pw_sb", [128, NIT + 1], F32)
        posi = sb("posi", [128, NT], I32)
        posf = sb("posf", [128, NT], F32)
        ang = sb("ang", [128, NT * 8], F32)
        rr = sb("rr", [128, 2, NT * 8], F32)
        rki = sb("rki", [128, NT * 8], I32)
        rkf = sb("rkf", [128, NT * 8], F32)
        cosT = sb("cosT", [128, NT, 8], F32)
        sinT = sb("sinT", [128, NT, 8], F32)
        ss = sb("ss", [128, 2, 8], F32)
        rs = sb("rs", [128, 2, 8], F32)
        rp = sb("rp", [128, 4, 80], F32)
        wsc = sb("wsc", [128, 4, 8], F32)
        bs = sb("bs", [128, 8], F32)
        rw = sb("rw", [128, NIT + 1], F32)
        tauc = sb("tauc", [128, 1], F32)
        rc = sb("rc", [128, 4, 4], F32)

        def carve(off, nbytes, dt):
            return big[:, off:off + nbytes].bitcast(dt)

        ztok = [carve(0 + i * 2048, 2048, F32) for i in range(2)]
        zscr = [carve(4096 + i * 2048, 2048, F32) for i in range(2)]
        zb = [carve(8192 + i * 1024, 1024, BF16) for i in range(2)]
        QT = carve(10240, 12288, BF16).rearrange("p (b t) -> p b t", t=512)
        score = carve(22528, 8192, F32)
        negm = carve(30720, 4096, BF16)
        rl = [carve(34816 + i * 1024, 1024, BF16) for i in range(2)]
        PT = [carve(36864 + i * 1024, 1024, BF16) for i in range(4)]
        atok = [carve(40960 + i * 1024, 1024, BF16) for i in range(2)]
        gates = carve(0, 16384, BF16).rearrange("p (c t) -> p c t", t=512)
        merged = carve(16384, 8192, BF16).rearrange("p (c t) -> p c t", t=512)
        hidden = carve(0, 22528, BF16).rearrange("p (c t) -> p c t", t=512)

        def gk(off, nbytes):
            return [("big", i) for i in range(off // 1024, (off + nbytes + 1023) // 1024)]

        K_ztok = [gk(0 + i * 2048, 2048) for i in range(2)]
        K_zscr = [gk(4096 + i * 2048, 2048) for i in range(2)]
        K_zb = [gk(8192 + i * 1024, 1024) for i in range(2)]
        K_QT = lambda blk, tl: gk(10240 + blk * 1024 + tl * 256, 256)
        K_QT_blk = lambda b0, b1, tl: sum([K_QT(bb, tl) for bb in range(b0, b1)], [])
        K_score = gk(22528, 8192)
        K_negm = gk(30720, 4096)
        K_rl = [gk(34816 + i * 1024, 1024) for i in range(2)]
        K_PT = [gk(36864 + i * 1024, 1024) for i in range(4)]
        K_atok = [gk(40960 + i * 1024, 1024) for i in range(2)]
        K_gates = lambda c: gk(c * 1024, 1024)
        K_merged = lambda c: gk(16384 + c * 1024, 1024)
        K_hidden = lambda c: gk(c * 1024, 1024)

        S = Sched(nc)
        dbg_ops = []

        def dump(ap, col0, ncols, keys):
            if d_dbg is None:
                return
            dst = d_dbgb if ap.dtype == BF16 else d_dbg
            o = S.op("sp", lambda e: e.dma_start(out=dst[:, col0:col0 + ncols], in_=ap), reads=keys, dsem="dbg%d" % len(dbg_ops))
            dbg_ops.append(o)

        blocks = []
        for l in range(L):
            if l in layers:
                for i in range(NADA):
                    blocks.append(d_wada[l, i])
        for b in range(NB):
            for l in layers:
                for g in range(NG):
                    for i in range(NWB):
                        blocks.append(d_wst[l, i])
        wstate = {"cur": 0, "issued": 0}

        def wnext(keep=0):
            i = wstate["cur"]
            wstate["cur"] += 1
            while wstate["issued"] < min(len(blocks), i - keep + NSLOT):
                j = wstate["issued"]
                sl = j % NSLOT
                S.op("pool", lambda e, j=j, sl=sl: e.dma_start(out=wsl[:, sl, :], in_=blocks[j]),
                     writes=[("w", sl)], dsem="w%d" % sl)
                wstate["issued"] += 1
            return i % NSLOT

        gpstate = {"i": 0}

        def gp():
            gpstate["i"] ^= 1
            return gpstate["i"]

        def mm(out, lhsT, rhs, start, stop, reads, bank, skip=False):
            if skip:
                S.op("pe", lambda e: e.matmul(out, lhsT, rhs, start=start, stop=stop, skip_group_check=True),
                     reads=reads, writes=[pbk[bank]])
            else:
                S.op("pe", lambda e: e.matmul(out, lhsT, rhs, start=start, stop=stop),
                     reads=reads, writes=[pbk[bank]])

        S.op("pool", lambda e: e.memset(ident[:], 1.0), writes=["ident"])
        S.op("pool", lambda e: e.affine_select(out=ident[:], in_=ident[:], pattern=[[-1, 128]],
                                               compare_op=ALU.is_equal, fill=0.0, base=0, channel_multiplier=1),
             reads=["ident"], writes=["ident"])
        for i in range(4):
            S.op("pool", lambda e, i=i: e.tensor_copy(I4[:, i * 128:(i + 1) * 128], ident[:]),
                 reads=["ident"], writes=[("I4", i)])
        K_I4 = [("I4", i) for i in range(4)]
        S.op("pool", lambda e: e.memset(ones_s[:], 1.0 / 1024.0), writes=["ones_s"])
        S.op("pool", lambda e: e.memset(ones_row[:], 1.0), writes=["ones_row"])
        S.op("pool", lambda e: e.memset(VA[:, :, 64:65], 1.0), writes=["VA1"])
        S.op("pool", lambda e: e.memset(VB[:, :, :, 64:65], 1.0), writes=["VB1"])
        S.op("pool", lambda e: e.memset(tauc[:], -1e29), writes=["tauc"])
        pre = "all:pre"
        S.op("sp", lambda e: e.dma_start(out=cTf[:], in_=d_cT), writes=["cTf"], dsem=pre)
        S.op("sp", lambda e: e.dma_start(out=invf[:], in_=d_invf.partition_broadcast(128)), writes=["invf"], dsem=pre)
        S.op("sp", lambda e: e.dma_start(out=pw[:], in_=d_pw.partition_broadcast(128)), writes=["pw"], dsem=pre)
        for l in range(L):
            S.op("sp", lambda e, l=l: e.dma_start(out=badaT[:, l, :], in_=d_badaT[l]), writes=[("badaT", l)], dsem=pre)
            S.op("sp", lambda e, l=l: e.dma_start(out=gT[:, l, :], in_=d_gT[l]), writes=[("gT", l)], dsem=pre)
        S.op("act", lambda e: e.activation(out=cact[:], in_=cTf[:], func=AF.Silu), reads=["cTf"], writes=["cact"])

        for l in range(L):
            if l not in layers:
                continue
            if 'mod' in DBG_SKIP:
                for blk in range(NADA):
                    wnext()
                continue
            for blk in range(NADA):
                sl = wnext()
                for jj in range(4):
                    j = blk * 4 + jj
                    for k in range(8):
                        mm(pb[0][:, j * NB:(j + 1) * NB], wsl[:, sl, k * 512 + jj * 128:k * 512 + (jj + 1) * 128],
                           cact[:, k * NB:(k + 1) * NB], k == 0, k == 7, [("w", sl), "cact"], 0)
            S.op("dve", lambda e, l=l: e.tensor_tensor(
                out=modT[:, l, :, :], in0=pb[0][:, 0:48 * NB].rearrange("p (j b) -> p j b", b=NB),
                in1=badaT[:, l, :].unsqueeze(2).to_broadcast([128, 48, NB]), op=ALU.add),
                reads=[("badaT", l)], writes=[pbk[0], ("modT", l)])

        def norm_mod(b, l, g, ci, which):
            gs = slice(g * 512, (g + 1) * 512)
            bank = gp()
            for c in range(8):
                q = c % 2
                S.op("act", lambda e, c=c, q=q: e.activation(out=sqc[:, q, :], in_=xT[:, c, gs], func=AF.Square),
                     reads=[("xT", c, g)], writes=[("sqc", q)])
                mm(pb[bank][:, :], ones_s[:, :], sqc[:, q, :], c == 0, c == 7, [("sqc", q), "ones_s"], bank)
            S.op("act", lambda e: e.activation(out=rstd[:], in_=pb[bank][:, :], func=AF.Sqrt, bias=epsT[:, 0:1]),
                 reads=["epsT"], writes=[pbk[bank], "rstd"])
            S.op("dve", lambda e: e.reciprocal(out=rstd[:], in_=rstd[:]), reads=["rstd"], writes=["rstd"])
            a0 = 3 * which
            for c in range(8):
                q = c % 2
                S.op("dve", lambda e, c=c, q=q: e.scalar_tensor_tensor(
                    out=tmpA[:, q, :], in0=xT[:, c, gs], scalar=coef[:, ci, a0, c:c + 1], in1=rstd[:],
                    op0=ALU.mult, op1=ALU.mult),
                    reads=[("xT", c, g), "rstd", ("coef", ci)], writes=[("tmpA", q)])
                S.op("act", lambda e, c=c, q=q: e.activation(out=hT[:, c, :], in_=tmpA[:, q, :], func=AF.Identity,
                                                             bias=coef[:, ci, a0 + 1, c:c + 1]),
                     reads=[("tmpA", q), ("coef", ci)], writes=[("hT", c)])

        epsT = sb("epsT", [128, 2], F32)
        S.op("pool", lambda e: e.memset(epsT[:], EPS), writes=["epsT"])

        def rope(z, nh, t, zkeys):
            zv = z[:, 0:nh * 64].rearrange("p (h d) -> p h d", d=64)
            x1 = zv[:, :, 0:8]
            x2 = zv[:, :, 8:16]
            cb = cosT[:, t, :].unsqueeze(1).to_broadcast([128, nh, 8])
            sn = sinT[:, t, :].unsqueeze(1).to_broadcast([128, nh, 8])
            tv = [rp[:, i, 0:nh * 8].rearrange("p (h d) -> p h d", d=8) for i in range(4)]
            S.op("dve", lambda e: e.tensor_tensor(out=tv[0], in0=x1, in1=cb, op=ALU.mult), reads=zkeys + ["rope_tab"], writes=[("rp", 0)])
            S.op("dve", lambda e: e.tensor_tensor(out=tv[1], in0=x2, in1=sn, op=ALU.mult), reads=zkeys + ["rope_tab"], writes=[("rp", 1)])
            S.op("dve", lambda e: e.tensor_tensor(out=tv[2], in0=x2, in1=cb, op=ALU.mult), reads=zkeys + ["rope_tab"], writes=[("rp", 2)])
            S.op("dve", lambda e: e.tensor_tensor(out=tv[3], in0=x1, in1=sn, op=ALU.mult), reads=zkeys + ["rope_tab"], writes=[("rp", 3)])
            S.op("dve", lambda e: e.tensor_tensor(out=x1, in0=tv[0], in1=tv[1], op=ALU.subtract),
                 reads=[("rp", 0), ("rp", 1)], writes=zkeys)
            S.op("dve", lambda e: e.tensor_tensor(out=x2, in0=tv[2], in1=tv[3], op=ALU.add),
                 reads=[("rp", 2), ("rp", 3)], writes=zkeys)

        def zproj_block(blk, sl, tl, t, li):
            ncols = 512 if blk < 5 else 328
            bank = gp()
            for k in range(8):
                mm(pb[bank][:, 0:ncols], hT[:, k, tl * 128:(tl + 1) * 128], wsl[:, sl, k * 512:k * 512 + ncols],
                   k == 0, False, [("hT", k), ("w", sl)], bank)
            mm(pb[bank][:, 0:ncols], ones_row[:, :], brow[:, blk * 512:blk * 512 + ncols], False, True,
               ["ones_row", "brow"], bank)
            zi = (blk * 4 + tl) % 2
            z = ztok[zi]
            zk = K_ztok[zi]
            S.op("act", lambda e: e.activation(out=z[:, 0:ncols], in_=pb[bank][:, 0:ncols], func=AF.Copy),
                 writes=[pbk[bank]] + zk)
            nh_norm = {0: 8, 1: 8, 2: 8, 5: 2}.get(blk, 0)
            gidx = {0: 0, 1: 2, 2: 3, 5: 1}.get(blk, 0)
            if nh_norm:
                w = nh_norm * 64
                S.op("act", lambda e: e.activation(out=zscr[zi][:, 0:w], in_=z[:, 0:w], func=AF.Square),
                     reads=zk, writes=K_zscr[zi])
                S.op("dve", lambda e: e.tensor_reduce(out=ss[:, zi, 0:nh_norm],
                                                      in_=zscr[zi][:, 0:w].rearrange("p (h d) -> p h d", d=64),
                                                      axis=AX.X, op=ALU.add),
                     reads=K_zscr[zi], writes=[("ss", zi)])
                S.op("act", lambda e: e.activation(out=rs[:, zi, 0:nh_norm], in_=ss[:, zi, 0:nh_norm], func=AF.Sqrt,
                                                   scale=1.0 / 64.0, bias=epsT[:, 0:1]),
                     reads=[("ss", zi), "epsT"], writes=[("rs", zi)])
                S.op("dve", lambda e: e.reciprocal(out=rs[:, zi, 0:nh_norm], in_=rs[:, zi, 0:nh_norm]),
                     reads=[("rs", zi)], writes=[("rs", zi)])
                zv = z[:, 0:w].rearrange("p (h d) -> p h d", d=64)
                S.op("dve", lambda e: e.tensor_tensor(out=zv, in0=zv,
                                                      in1=rs[:, zi, 0:nh_norm].unsqueeze(2).to_broadcast([128, nh_norm, 64]),
                                                      op=ALU.mult),
                     reads=zk + [("rs", zi)], writes=zk)
                S.op("dve", lambda e: e.tensor_tensor(out=zv, in0=zv,
                                                      in1=gains[:, li, gidx * 64:(gidx + 1) * 64].unsqueeze(1).to_broadcast([128, nh_norm, 64]),
                                                      op=ALU.mult),
                     reads=zk + [("gains", li)], writes=zk)
            if blk in (0, 3):
                rope(z, 8, t, zk)
            if blk == 5:
                rope(z, 4, t, zk)
            if blk in (0, 1, 2, 3):
                S.op("pool", lambda e: e.tensor_copy(zb[zi][:, :], z[:, :]), reads=zk, writes=K_zb[zi])
                tb = gp()
                tbv = pb[tb][:, :].bitcast(BF16)
                for i in range(4):
                    S.op("pe", lambda e, i=i: e.transpose(tbv[:, i * 128:(i + 1) * 128], zb[zi][:, i * 128:(i + 1) * 128], ident[:]),
                         reads=K_zb[zi] + ["ident"], writes=[pbk[tb]])
                src = tbv[:, 0:512].rearrange("p (i q) -> p i q", q=128)
                if blk == 2:
                    slot = t % KSLOTS
                    S.op("dve", lambda e: e.tensor_copy(KBT[:, :, slot, :], src), writes=[pbk[tb], ("KBT", slot)])
                else:
                    b0 = {0: 0, 1: 4, 3: 8}[blk]
                    S.op("dve", lambda e: e.tensor_copy(QT[:, b0:b0 + 4, tl * 128:(tl + 1) * 128], src),
                         writes=[pbk[tb]] + K_QT_blk(b0, b0 + 4, tl))
            if blk == 4:
                slot = t % KSLOTS
                S.op("pool", lambda e: e.tensor_copy(VB[:, slot, :, 0:64], z[:, :].rearrange("p (h d) -> p h d", d=64)),
                     reads=zk + ["VB1"], writes=[("VB", slot)])
            if blk == 5:
                S.op("pool", lambda e: e.tensor_copy(zb[zi][:, 0:256], z[:, 0:256]), reads=zk, writes=K_zb[zi])
                tb = gp()
                tbv = pb[tb][:, :].bitcast(BF16)
                for i in range(2):
                    S.op("pe", lambda e, i=i: e.transpose(tbv[:, i * 128:(i + 1) * 128], zb[zi][:, i * 128:(i + 1) * 128], ident[:]),
                         reads=K_zb[zi] + ["ident"], writes=[pbk[tb]])
                S.op("dve", lambda e: e.tensor_copy(KAT[:, t * 128:(t + 1) * 128], tbv[:, 0:128]),
                     writes=[pbk[tb], ("KAT", t)])
                S.op("dve", lambda e: e.tensor_copy(KIT[:, t * 128:(t + 1) * 128], tbv[:, 128:256]),
                     writes=[pbk[tb], ("KIT", t)])
                S.op("pool", lambda e: e.tensor_copy(VA[:, t, 0:64], z[:, 256:320]), reads=zk + ["VA1"], writes=[("VA", t)])
                S.op("dve", lambda e: e.tensor_scalar(out=wsc[:, tl, :], in0=z[:, 320:328], scalar1=INDEX_SCALE,
                                                       scalar2=None, op0=ALU.mult),
                     reads=zk, writes=[("wsc", tl)])

        def indexer(tl, t):
            nkeys = (t + 1) * 128
            for c0 in range(0, nkeys, 512):
                w = min(512, nkeys - c0)
                kit_keys = [("KIT", j) for j in range(c0 // 128, (c0 + w) // 128)]
                sc_keys = gk(22528 + c0 * 4, w * 4)
                for h in range(8):
                    half = h % 2
                    ps_ = slice(half * 64, (half + 1) * 64)
                    bank = gp()
                    mm(pb[bank][:, 0:w], QT[ps_, 8 + h // 2, tl * 128:(tl + 1) * 128], KIT[ps_, c0:c0 + w], True, True,
                       K_QT(8 + h // 2, tl) + kit_keys, bank)
                    q = h % 2
                    S.op("act", lambda e, bank=bank, q=q, w=w: e.activation(out=rl[q][:, 0:w], in_=pb[bank][:, 0:w], func=AF.Relu),
                         writes=[pbk[bank]] + K_rl[q])
                    if h == 0:
                        S.op("dve", lambda e, q=q, w=w, c0=c0: e.tensor_scalar(
                            out=score[:, c0:c0 + w], in0=rl[q][:, 0:w], scalar1=wsc[:, tl, 0:1], scalar2=None, op0=ALU.mult),
                            reads=K_rl[q] + [("wsc", tl)], writes=sc_keys)
                    else:
                        S.op("dve", lambda e, q=q, w=w, c0=c0, h=h: e.scalar_tensor_tensor(
                            out=score[:, c0:c0 + w], in0=rl[q][:, 0:w], scalar=wsc[:, tl, h:h + 1], in1=score[:, c0:c0 + w],
                            op0=ALU.mult, op1=ALU.add),
                            reads=K_rl[q] + [("wsc", tl)], writes=sc_keys)
            sk = K_score
            if t >= 2:
                S.op("dve", lambda e: e.tensor_reduce(out=bs[:, 0:1], in_=score[:, 0:nkeys], axis=AX.X, op=ALU.min),
                     reads=sk, writes=["bs0"])
            S.op("dve", lambda e: e.memset(score[0:64, t * 128 + 64:(t + 1) * 128], -1e30), writes=sk)
            if t >= 2:
                S.op("dve", lambda e: e.tensor_reduce(out=bs[:, 1:2], in_=score[:, 0:nkeys], axis=AX.X, op=ALU.max),
                     reads=sk, writes=["bs1"])
                S.op("dve", lambda e: e.tensor_tensor(out=bs[:, 2:3], in0=bs[:, 1:2], in1=bs[:, 0:1], op=ALU.subtract),
                     reads=["bs0", "bs1"], writes=["bs2"])
                S.op("dve", lambda e: e.tensor_scalar(out=rw[:, :], in0=pw[:, :], scalar1=bs[:, 2:3], scalar2=None, op0=ALU.mult),
                     reads=["pw", "bs2"], writes=["rw"])
                S.op("dve", lambda e: e.tensor_tensor(out=bs[:, 3:4], in0=bs[:, 0:1], in1=rw[:, 0:1], op=ALU.add),
                     reads=["bs0", "rw"], writes=["mid"])
                for it in range(NIT):
                    S.op("dve", lambda e: e.tensor_scalar(out=negm[:, 0:nkeys], in0=score[:, 0:nkeys], scalar1=bs[:, 3:4],
                                                          scalar2=None, op0=ALU.is_gt, op1=ALU.add, accum_out=bs[:, 4:5]),
                         reads=sk + ["mid"], writes=K_negm + ["cnt"])
                    S.op("dve", lambda e: e.tensor_scalar(out=bs[:, 5:6], in0=bs[:, 4:5], scalar1=TOPK - 0.5, scalar2=0.5,
                                                          op0=ALU.is_ge, op1=ALU.subtract),
                         reads=["cnt"], writes=["tt"])
                    S.op("dve", lambda e, it=it: e.scalar_tensor_tensor(out=bs[:, 3:4], in0=bs[:, 5:6], scalar=rw[:, it:it + 1],
                                                                        in1=bs[:, 3:4], op0=ALU.mult, op1=ALU.add),
                         reads=["tt", "rw", "mid"], writes=["mid"])
                S.op("dve", lambda e: e.tensor_tensor(out=bs[:, 6:7], in0=bs[:, 3:4], in1=rw[:, NIT:NIT + 1], op=ALU.subtract),
                     reads=["mid", "rw"], writes=["tau"])
                tau = bs[:, 6:7]
                tk = ["tau"]
            else:
                tau = tauc[:, 0:1]
                tk = ["tauc"]
            S.op("dve", lambda e: e.tensor_scalar(out=negm[:, 0:nkeys], in0=score[:, 0:nkeys], scalar1=tau, scalar2=NEG,
                                                  op0=ALU.is_le, op1=ALU.mult),
                 reads=sk + tk, writes=K_negm)

        ptc = {"i": 0}

        def attn_finish(br, tl, obase):
            av = atok[br][:, :].rearrange("p (i two d) -> p i two d", two=2, d=64)
            for half in range(2):
                ob = obase + half
                ov = pb[ob][:, 0:260].rearrange("p (i d) -> p i d", d=65)
                S.op("dve", lambda e, ov=ov, half=half: e.reciprocal(out=rc[:, br * 2 + half, :], in_=ov[:, :, 64]),
                     writes=[pbk[ob], ("rc", br, half)])
                S.op("dve", lambda e, ov=ov, half=half: e.tensor_tensor(
                    out=av[:, :, half, :], in0=ov[:, :, 0:64],
                    in1=rc[:, br * 2 + half, :].unsqueeze(2).to_broadcast([128, 4, 64]), op=ALU.mult),
                    reads=[("rc", br, half)], writes=[pbk[ob]] + K_atok[br])
            tb = gp()
            tbv = pb[tb][:, :].bitcast(BF16)
            for i in range(4):
                S.op("pe", lambda e, i=i: e.transpose(tbv[:, i * 128:(i + 1) * 128], atok[br][:, i * 128:(i + 1) * 128], ident[:]),
                     reads=K_atok[br] + ["ident"], writes=[pbk[tb]])
            S.op("act", lambda e: e.activation(out=attnT[:, br, :, tl * 128:(tl + 1) * 128],
                                               in_=tbv[:, 0:512].rearrange("p (i q) -> p i q", q=128), func=AF.Copy),
                 writes=[pbk[tb], ("attnT", br, tl)])

        def attn_A(tl, t):
            for j in range(t + 1):
                for half in range(2):
                    ps_ = slice(half * 64, (half + 1) * 64)
                    bank = 2 + half
                    mm(pb[bank][:, :], KAT[ps_, j * 128:(j + 1) * 128], QT[ps_, 0:4, tl * 128:(tl + 1) * 128], True, False,
                       [("KAT", j)] + K_QT_blk(0, 4, tl), bank)
                    mm(pb[bank][:, :], negm[:, j * 128:(j + 1) * 128], I4[:, :], False, True, K_negm + K_I4, bank)
                    pi = ptc["i"] % 4
                    ptc["i"] += 1
                    S.op("act", lambda e, bank=bank, pi=pi: e.activation(out=PT[pi][:, :], in_=pb[bank][:, :], func=AF.Exp),
                         writes=[pbk[bank]] + K_PT[pi])
                    for i in range(4):
                        mm(pb[4 + half][:, i * 65:(i + 1) * 65], PT[pi][:, i * 128:(i + 1) * 128], VA[:, j, :],
                           (j == 0 and i == 0), j == t, K_PT[pi] + [("VA", j), "VA1"], 4 + half, skip=True)
            attn_finish(0, tl, 4)

        def attn_B(tl, t):
            j0 = max(0, t - 4)
            for j in range(j0, t + 1):
                jrel = j - (t - 4)
                slot = j % KSLOTS
                for half in range(2):
                    ps_ = slice(half * 64, (half + 1) * 64)
                    bank = 2 + half
                    for i in range(4):
                        mm(pb[bank][:, i * 128:(i + 1) * 128], KBT[ps_, i, slot, :], QT[ps_, 4 + i, tl * 128:(tl + 1) * 128],
                           i == 0, False, [("KBT", slot)] + K_QT(4 + i, tl), bank, skip=True)
                    mm(pb[bank][:, :], ident[:, :], biasT[:, jrel, half, :], False, True, ["ident", "biasT"], bank, skip=True)
                    pi = ptc["i"] % 4
                    ptc["i"] += 1
                    S.op("act", lambda e, bank=bank, pi=pi: e.activation(out=PT[pi][:, :], in_=pb[bank][:, :], func=AF.Exp),
                         writes=[pbk[bank]] + K_PT[pi])
                    for i in range(4):
                        mm(pb[6 + half][:, i * 65:(i + 1) * 65], PT[pi][:, i * 128:(i + 1) * 128], VB[:, slot, 2 * i + half, :],
                           (j == j0 and i == 0), j == t, K_PT[pi] + [("VB", slot), "VB1"], 6 + half, skip=True)
            attn_finish(1, tl, 6)

        def mixer_group(b, l, g, ci, li):
            gs = slice(g * 512, (g + 1) * 512)
            if DBG_STAGE < 1:
                return
            norm_mod(b, l, g, ci, 0)
            if DBG_STAGE < 6:
                dump(hT[:, :, :].rearrange("p c t -> p (c t)"), 0, 4096, [("hT", c) for c in range(8)])
            if DBG_STAGE < 2:
                return
            for blk in range(6):
                sl = wnext()
                for tl in range(4):
                    zproj_block(blk, sl, tl, g * 4 + tl, li)
            if DBG_STAGE < 6:
              dump(QT[:, :, :].rearrange("p c t -> p (c t)"), 4096, 6144, K_QT_blk(0, 12, 0) + K_QT_blk(0, 12, 1) + K_QT_blk(0, 12, 2) + K_QT_blk(0, 12, 3))
            if DBG_STAGE < 6:
              dump(KAT[:, 0:512], 10240, 512, [("KAT", j) for j in range(4)])
            dump(KIT[:, 0:512], 10752, 512, [("KIT", j) for j in range(4)])
            if DBG_STAGE < 3:
                return
            for tl in range(4):
                t = g * 4 + tl
                indexer(tl, t)
                if tl == 3:
                    dump(score[:, 0:512], 11264, 512, K_score)
                    dump(negm[:, 0:512], 11776, 512, K_negm)
                if DBG_STAGE >= 4:
                    attn_A(tl, t)
                if DBG_STAGE >= 5:
                    attn_B(tl, t)
            dump(attnT[:, :, :, :].rearrange("p a c t -> p (a c t)"), 12288, 4096, [("attnT", br, x) for br in range(2) for x in range(4)])
            if DBG_STAGE < 6:
                return
            for blk in range(4):
                sl = wnext()
                for jj in range(4):
                    c = blk * 4 + jj
                    bank = gp()
                    for k in range(8):
                        mm(pb[bank][:, :], wsl[:, sl, k * 512 + jj * 128:k * 512 + (jj + 1) * 128], hT[:, k, :], k == 0, k == 7,
                           [("w", sl), ("hT", k)], bank)
                    S.op("act", lambda e, c=c, bank=bank: e.activation(out=gates[:, c, :], in_=pb[bank][:, :], func=AF.Sigmoid,
                                                                       bias=bgT[:, li, c:c + 1]),
                         reads=[("bgT", li)], writes=[pbk[bank]] + K_gates(c))
            sla = wnext()
            slb = wnext(keep=1)
            for c in range(8):
                for br, slx in ((0, sla), (1, slb)):
                    bank = gp()
                    for k in range(4):
                        mm(pb[bank][:, :], wsl[:, slx, k * 1024 + c * 128:k * 1024 + (c + 1) * 128], attnT[:, br, k, :], k == 0, k == 3,
                           [("w", slx)] + [("attnT", br, x) for x in range(4)], bank)
                    S.op("dve", lambda e, c=c, br=br, bank=bank: e.tensor_tensor(out=tmpA[:, br, :], in0=pb[bank][:, :],
                                                                                 in1=gates[:, br * 8 + c, :], op=ALU.mult),
                         reads=K_gates(br * 8 + c), writes=[pbk[bank], ("tmpA", br)])
                S.op("pool", lambda e, c=c: e.tensor_tensor(out=merged[:, c, :], in0=tmpA[:, 0, :], in1=tmpA[:, 1, :], op=ALU.add),
                     reads=[("tmpA", 0), ("tmpA", 1)], writes=K_merged(c))
            for i in range(2):
                sl = wnext()
                for cc in range(4):
                    c = i * 4 + cc
                    bank = gp()
                    for k in range(8):
                        mm(pb[bank][:, :], wsl[:, sl, k * 512 + cc * 128:k * 512 + (cc + 1) * 128], merged[:, k, :], k == 0, k == 7,
                           [("w", sl)] + K_merged(k), bank)
                    S.op("dve", lambda e, c=c, bank=bank: e.scalar_tensor_tensor(
                        out=xT[:, c, gs], in0=pb[bank][:, :], scalar=coef[:, ci, 2, c:c + 1], in1=xT[:, c, gs],
                        op0=ALU.mult, op1=ALU.add),
                        reads=[("coef", ci)], writes=[pbk[bank], ("xT", c, g)])
            if 'late' in DBG_SKIP:
                late_dumps.append((gs, g))
            elif DBG_STAGE >= 6:
                if 'mdump' not in DBG_SKIP:
                    dump(merged[:, :, :].rearrange("p c t -> p (c t)"), 0, 4096, sum([K_merged(c) for c in range(8)], []))
                if 'gdump' not in DBG_SKIP:
                    dump(gates[:, 0:8, :].rearrange("p c t -> p (c t)"), 8192, 4096, sum([K_gates(c) for c in range(8)], []))
                for c in range(8):
                    if 'xdump' not in DBG_SKIP:
                        dump(xT[:, c, gs], 4096 + c * 512, 512, [("xT", c, g)])

        late_dumps = []

        def ffn_group(b, l, g, ci):
            gs = slice(g * 512, (g + 1) * 512)
            norm_mod(b, l, g, ci, 1)
            if late_dumps:
                late_dumps.pop()
                for c in range(8):
                    dump(xT[:, c, gs], 4096 + c * 512, 512, [("xT", c, g)])
            pr_i = 0
            for i in range(11):
                sl = wnext()
                for pr in range(2):
                    j = 2 * i + pr
                    bg = 2 * (pr_i % 4)
                    bu = bg + 1
                    pr_i += 1
                    for k in range(8):
                        mm(pb[bg][:, :], wsl[:, sl, k * 512 + (2 * pr) * 128:k * 512 + (2 * pr + 1) * 128], hT[:, k, :], k == 0, k == 7,
                           [("w", sl), ("hT", k)], bg)
                    for k in range(8):
                        mm(pb[bu][:, :], wsl[:, sl, k * 512 + (2 * pr + 1) * 128:k * 512 + (2 * pr + 2) * 128], hT[:, k, :], k == 0, k == 7,
                           [("w", sl), ("hT", k)], bu)
                    q = j % 2
                    S.op("act", lambda e, bg=bg, q=q: e.activation(out=sqc[:, q, :], in_=pb[bg][:, :], func=AF.Silu),
                         writes=[pbk[bg], ("sqc", q)])
                    S.op("dve", lambda e, bu=bu, q=q, j=j: e.tensor_tensor(out=hidden[:, j, :], in0=pb[bu][:, :], in1=sqc[:, q, :], op=ALU.mult),
                         reads=[("sqc", q)], writes=[pbk[bu]] + K_hidden(j))
            for hf in range(2):
                base = 4 if hf == 0 else 0
                for kb in range(3):
                    sl = wnext()
                    nk = 8 if kb < 2 else 6
                    for kk in range(nk):
                        kt = kb * 8 + kk
                        for cc in range(4):
                            mm(pb[base + cc][:, :], wsl[:, sl, kk * 512 + cc * 128:kk * 512 + (cc + 1) * 128], hidden[:, kt, :],
                               kt == 0, kt == 21, [("w", sl)] + K_hidden(kt), base + cc)
                for cc in range(4):
                    c = hf * 4 + cc
                    S.op("dve", lambda e, c=c, bank=base + cc: e.scalar_tensor_tensor(
                        out=xT[:, c, gs], in0=pb[bank][:, :], scalar=coef[:, ci, 5, c:c + 1], in1=xT[:, c, gs],
                        op0=ALU.mult, op1=ALU.add),
                        reads=[("coef", ci)], writes=[pbk[base + cc], ("xT", c, g)])

        out_ops = []
        inst = 0
        for b in range(NB):
            xk_all = [("xT", c, g) for c in range(8) for g in range(NG)]
            for c in range(8):
                S.op("sp", lambda e, b=b, c=c: e.dma_start(out=xT[:, c, :], in_=d_xT[b, c]),
                     writes=[("xT", c, g) for g in range(NG)], dsem="all:x%d" % b)
            S.op("sp", lambda e, b=b: e.dma_start(out=posi[:], in_=d_pos[b]), writes=["posi"], dsem="all:pos%d" % b)
            S.op("dve", lambda e: e.tensor_copy(posf[:], posi[:]), reads=["posi"], writes=["posf"])
            angv = ang[:, :].rearrange("p (t f) -> p t f", f=8)
            S.op("dve", lambda e: e.tensor_tensor(out=angv, in0=posf[:, :].unsqueeze(2).to_broadcast([128, NT, 8]),
                                                  in1=invf[:, :].unsqueeze(1).to_broadcast([128, NT, 8]), op=ALU.mult),
                 reads=["posf", "invf"], writes=["ang"])
            for which, dst in ((0, sinT), (1, cosT)):
                if 'rope' in DBG_SKIP:
                    continue
                r_ = rr[:, which, :]
                wk = ("rr", which)
                S.op("dve", lambda e, which=which, r_=r_: e.tensor_scalar(out=r_, in0=ang[:, :], scalar1=1.0 / (2 * np.pi),
                                                                         scalar2=0.25 * which, op0=ALU.mult, op1=ALU.add),
                     reads=["ang"], writes=[wk])
                S.op("dve", lambda e, r_=r_: e.tensor_copy(rki[:, :], r_), reads=[wk], writes=["rki"])
                S.op("dve", lambda e: e.tensor_copy(rkf[:, :], rki[:, :]), reads=["rki"], writes=["rkf"])
                S.op("dve", lambda e, r_=r_: e.scalar_tensor_tensor(out=r_, in0=rkf[:, :], scalar=-6.28125, in1=ang[:, :],
                                                                    op0=ALU.mult, op1=ALU.add),
                     reads=["rkf", "ang"], writes=[wk])
                S.op("dve", lambda e, r_=r_: e.scalar_tensor_tensor(out=r_, in0=rkf[:, :], scalar=-(2 * np.pi - 6.28125), in1=r_,
                                                                    op0=ALU.mult, op1=ALU.add),
                     reads=["rkf", wk], writes=[wk])
                S.op("dve", lambda e, r_=r_, which=which: e.tensor_scalar(out=r_, in0=r_, scalar1=(np.pi / 2) * which, scalar2=3.1415925,
                                                                         op0=ALU.add, op1=ALU.min),
                     reads=[wk], writes=[wk])
                S.op("dve", lambda e, r_=r_: e.tensor_scalar(out=r_, in0=r_, scalar1=-3.1415925, scalar2=None, op0=ALU.max),
                     reads=[wk], writes=[wk])
                S.op("act", lambda e, r_=r_, dst=dst: e.activation(out=dst[:, :, :].rearrange("p t f -> p (t f)"), in_=r_, func=AF.Sin),
                     reads=[wk], writes=["rope_tab"])
            for l in layers:
                ci = inst % 2
                li = inst % 2
                inst += 1
                ld = "all:ld%d_%d" % (b, l)
                ldp = "all:ldp%d_%d" % (b, l)
                S.op("sp", lambda e, l=l, li=li: e.dma_start(out=bgT[:, li, :], in_=d_bgT[l]), writes=[("bgT", li)], dsem=ld)
                S.op("sp", lambda e, l=l, li=li: e.dma_start(out=gains[:, li, :], in_=d_gains[l].partition_broadcast(128)),
                     writes=[("gains", li)], dsem=ld)
                if 'brow' not in DBG_SKIP:
                    S.op("pool", lambda e, l=l: e.dma_start(out=brow[:, :], in_=d_brow[l]), writes=["brow"], dsem=ldp)
                S.op("pool", lambda e, l=l: e.dma_start(out=biasT[:, :, :, :].rearrange("p a b c -> p (a b c)"), in_=d_biasT[l]),
                     writes=["biasT"], dsem=ldp)
                S.op("dve", lambda e, li=li: e.tensor_scalar(
                    out=gains[:, li, :].rearrange("p (a b d) -> p a b d", b=2, d=64)[:, :, 0, :],
                    in0=gains[:, li, :].rearrange("p (a b d) -> p a b d", b=2, d=64)[:, :, 0, :],
                    scalar1=0.125, scalar2=None, op0=ALU.mult),
                    reads=[("gains", li)], writes=[("gains", li)])
                for which in range(2):
                    if 'coef' in DBG_SKIP:
                        continue
                    o3 = 24 * which
                    S.op("dve", lambda e, l=l, b=b, o3=o3, which=which, ci=ci: e.scalar_tensor_tensor(
                        out=coef[:, ci, 3 * which, :], in0=modT[:, l, o3 + 8:o3 + 16, b], scalar=1.0,
                        in1=gT[:, l, 8 * which:8 * which + 8], op0=ALU.add, op1=ALU.mult),
                        reads=[("modT", l), ("gT", l)], writes=[("coef", ci)])
                    S.op("dve", lambda e, l=l, b=b, o3=o3, which=which, ci=ci: e.tensor_copy(coef[:, ci, 3 * which + 1, :], modT[:, l, o3:o3 + 8, b]),
                         reads=[("modT", l)], writes=[("coef", ci)])
                    S.op("dve", lambda e, l=l, b=b, o3=o3, which=which, ci=ci: e.tensor_copy(coef[:, ci, 3 * which + 2, :], modT[:, l, o3 + 16:o3 + 24, b]),
                         reads=[("modT", l)], writes=[("coef", ci)])
                for g in range(NG):
                    if DBG_STAGE < 99 and (b > 0 or g > 0):
                        continue
                    mixer_group(b, l, g, ci, li)
                    if DBG_STAGE >= 7:
                        ffn_group(b, l, g, ci)
            for c in range(8):
                o = S.op("sp", lambda e, b=b, c=c: e.dma_start(out=d_out[b, c], in_=xT[:, c, :]),
                         reads=[("xT", c, g) for g in range(NG)], dsem="all:o%d" % b)
                out_ops.append(o)
        assert DBG_STAGE < 99 or wstate["cur"] == len(blocks), (wstate, len(blocks))
        S.emit(final_wait_ops=out_ops + dbg_ops)
    return nc


def _blockify(W, ncols_blk=512):
    K, N = W.shape
    assert N % ncols_blk == 0
    kc = K // 128
    out = np.zeros((N // ncols_blk, 128, 8, ncols_blk), np.float32)
    Wr = W.reshape(kc, 128, N // ncols_blk, ncols_blk)
    out[:, :, :kc, :] = Wr.transpose(2, 1, 0, 3)
    return out.reshape(N // ncols_blk, 128, 8 * ncols_blk)


def _prep_shared(inp):
    L = DEPTH
    offs = np.concatenate([[0], np.cumsum(IN_SIZES)])
    rng_ = lambda i: np.arange(offs[i], offs[i + 1])
    qa, ka, va, qi, ki, wi, qb, kb, vb, ga, gb = [rng_(i) for i in range(11)]
    tok_cols = np.concatenate([qa, qb, kb, qi, vb, ka, ka, ki, ki, va, wi])
    gate_cols = np.concatenate([ga, gb])
    gu_cols = []
    for i in range(11):
        for pr in range(2):
            j = 2 * i + pr
            gu_cols.append(np.arange(j * 128, (j + 1) * 128))
            gu_cols.append(D_FF + np.arange(j * 128, (j + 1) * 128))
    gu_cols = np.concatenate(gu_cols)
    wst = np.zeros((L, NWB, 128, 4096), np.float32)
    wada = np.zeros((L, NADA, 128, 4096), np.float32)
    brow = np.zeros((L, 128, 3072), np.float32)
    bgT = np.zeros((L, 128, 16), np.float32)
    badaT = np.zeros((L, 128, 48), np.float32)
    gT = np.zeros((L, 128, 16), np.float32)
    gains = np.zeros((L, 1, 256), np.float32)
    biasT = np.zeros((L, 128, 5120), np.float32)
    k_ = np.arange(128)[:, None, None]
    j_ = np.arange(5)[None, :, None]
    q_ = np.arange(128)[None, None, :]
    tdiff = q_ - k_ + 128 * (4 - j_)
    ridx = np.clip(tdiff, -63, 256) + 63
    cq = q_ // 64
    ck = 2 * j_ + k_ // 64
    vis = (ck >= cq) & (ck <= 8 + cq)
    for l in range(L):
        w_in = np.asarray(inp["w_in"][l])
        wtok = np.zeros((1024, 3072), np.float32)
        wtok[:, :2888] = w_in[:, tok_cols]
        wst[l, 0:6] = _blockify(wtok)
        wst[l, 6:10] = _blockify(w_in[:, gate_cols])
        woa = np.asarray(inp["w_oa"][l]).reshape(4, 128, 1024).transpose(1, 0, 2).reshape(128, 4096)
        wob = np.asarray(inp["w_ob"][l]).reshape(4, 128, 1024).transpose(1, 0, 2).reshape(128, 4096)
        wst[l, 10] = woa
        wst[l, 11] = wob
        wst[l, 12:14] = _blockify(np.asarray(inp["w_out"][l]))
        wst[l, 14:25] = _blockify(np.asarray(inp["w_gu"][l])[:, gu_cols])
        wd = np.zeros((3072, 1024), np.float32)
        wd[:D_FF] = np.asarray(inp["w_down"][l])
        for hf in range(2):
            for kbk in range(3):
                sub = wd[kbk * 1024:(kbk + 1) * 1024, hf * 512:(hf + 1) * 512]
                wst[l, 25 + hf * 3 + kbk] = _blockify(sub)[0]
        wada[l] = _blockify(np.asarray(inp["w_ada"][l]))
        b_in = np.asarray(inp["b_in"][l])
        brow[l, 0, :2888] = b_in[tok_cols]
        bgT[l] = b_in[gate_cols].reshape(16, 128).T
        badaT[l] = np.asarray(inp["b_ada"][l]).reshape(48, 128).T
        gT[l, :, 0:8] = np.asarray(inp["g_mix"][l]).reshape(8, 128).T
        gT[l, :, 8:16] = np.asarray(inp["g_ffn"][l]).reshape(8, 128).T
        gains[l, 0] = np.concatenate([np.asarray(inp[k][l]) for k in ("qn_a", "kn_a", "qn_b", "kn_b")])
        rb = np.asarray(inp["rel_bias"][l])
        g_ = rb[:, ridx]
        g_ = np.where(vis[None], g_, np.float32(NEG)).astype(np.float32)
        g_ = g_.reshape(4, 2, 128, 5, 128)
        biasT[l] = g_.transpose(2, 3, 1, 0, 4).reshape(128, 5120)
    invf = (np.float32(ROPE_THETA) ** (-np.arange(0, 16, 2, dtype=np.float32) / np.float32(16))).astype(np.float32).reshape(1, 8)
    pw = (0.5 ** np.arange(1, NIT + 2)).astype(np.float32).reshape(1, NIT + 1)
    return dict(wst=wst, wada=wada, brow=brow, bgT=bgT, badaT=badaT, gT=gT, gains=gains, biasT=biasT, invf=invf, pw=pw)


def _prep_core(x_t, c, positions, core):
    bsl = slice(core * NB, (core + 1) * NB)
    xT = np.ascontiguousarray(x_t[bsl].transpose(0, 2, 1)).reshape(NB, 8, 128, SEQ)
    cT = np.ascontiguousarray(np.asarray(c)[bsl].reshape(NB, 8, 128).transpose(2, 1, 0)).reshape(128, 8 * NB)
    pos = np.ascontiguousarray(np.asarray(positions)[bsl].reshape(NB, NT, 128).transpose(0, 2, 1)).astype(np.int32)
    return dict(xT=xT, cT=cT, pos=pos)


_PROG_CACHE = {}


def _get_prog(layers):
    key = tuple(layers)
    if key not in _PROG_CACHE:
        _PROG_CACHE[key] = build_program(list(layers))
    return _PROG_CACHE[key]


def _run(x_cur, c, positions, shared, layers):
    nc = _get_prog(layers)
    in_maps = []
    for core in range(N_CORES):
        m = dict(shared)
        m.update(_prep_core(x_cur, c, positions, core))
        in_maps.append(m)
    res = run_bass_kernel_spmd(nc, in_maps, core_ids=list(range(N_CORES)))
    outs = []
    for core in range(N_CORES):
        oT = np.asarray(res.results[core]["oT"]).reshape(NB, 1024, SEQ)
        outs.append(oT.transpose(0, 2, 1))
    return np.ascontiguousarray(np.concatenate(outs, axis=0)).astype(np.float32)


FUSED = False


def kernel(**inputs):
    inp = {k: np.asarray(v) for k, v in inputs.items()}
    shared = _prep_shared(inp)
    x = inp["x"].astype(np.float32)
    if FUSED:
        return _run(x, inp["c"], inp["positions"], shared, list(range(DEPTH)))
    for l in range(DEPTH):
        x = _run(x, inp["c"], inp["positions"], shared, [l])
    return x
```

```python
import contextlib
import numpy as np
import concourse.bass as bass
import concourse.mybir as mybir
from concourse.bass_utils import run_bass_kernel_spmd

F32 = mybir.dt.float32
BF16 = mybir.dt.bfloat16
I32 = mybir.dt.int32
I8 = mybir.dt.int8
ALU = mybir.AluOpType
AF = mybir.ActivationFunctionType
AX = mybir.AxisListType

D_MODEL = 1024
SEQ = 2048
DEPTH = 2
N_CORES = 8
NB = 2
HD = 64
D_FF = 2816
IN_SIZES = (512, 64, 64, 512, 64, 8, 512, 512, 512, 1024, 1024)
TOPK = 256
EPS = 1e-6
INDEX_SCALE = (64 ** -0.5) * (8 ** -0.5)
ROPE_THETA = 500000.0
NT = SEQ // 128
NG = 4
NWB = 31
NADA = 12
NSLOT = 3
NIT = 22
NEG = -30000.0
KSLOTS = 8

SEG = 30000
DBG_STAGE = 99
import os
DBG_SKIP = set(os.environ.get('DBG_SKIP', '').split(','))


class Op:
    __slots__ = ("eng", "fn", "deps", "dsem", "signal", "sig")

    def __init__(self, eng, fn, dsem=None):
        self.eng = eng
        self.fn = fn
        self.deps = []
        self.dsem = dsem
        self.signal = False
        self.sig = None


class Sched:
    ENGS = ("pe", "act", "dve", "pool", "sp")

    def __init__(self, nc):
        self.nc = nc
        self.ops = []
        self.last_w = {}
        self.readers = {}

    def op(self, eng, fn, reads=(), writes=(), dsem=None):
        o = Op(eng, fn, dsem)
        deps = set()
        for k in reads:
            w = self.last_w.get(k)
            if w is not None:
                deps.add(w)
        for k in writes:
            w = self.last_w.get(k)
            if w is not None:
                deps.add(w)
            r = self.readers.get(k)
            if r:
                for x in r[0].values():
                    deps.add(x)
                for x in r[1]:
                    deps.add(x)
        for k in reads:
            r = self.readers.get(k)
            if r is None:
                r = self.readers[k] = ({}, [])
            if dsem is None:
                r[0][eng] = o
            else:
                r[1].append(o)
        for k in writes:
            self.last_w[k] = o
            self.readers[k] = ({}, [])
        deps.discard(o)
        for d in deps:
            if d.eng == "pe" and eng == "pe" and d.dsem is None and dsem is None:
                continue
            o.deps.append(d)
            d.signal = True
        self.ops.append(o)
        return o

    def emit(self, final_wait_ops=()):
        nc = self.nc
        for o in final_wait_ops:
            o.signal = True
        for o in self.ops:
            if o.dsem is not None:
                o.signal = True
        print("sched ops:", len(self.ops), {e: sum(1 for o in self.ops if o.eng == e) for e in self.ENGS})
        cnt = {}
        for o in self.ops:
            if not o.signal:
                continue
            key = ("d", o.dsem) if o.dsem is not None else ("e", o.eng)
            n = cnt.get(key, 0) + 1
            cnt[key] = n
            o.sig = (key, n)
        sems = {}
        with contextlib.ExitStack() as stack:
            def getsem(key, seg):
                k = (key, seg)
                if k not in sems:
                    nm = "s%d" % len(sems)
                    sems[k] = stack.enter_context(nc.semaphore(nm))
                return sems[k]

            for key, n in cnt.items():
                for seg in range((n - 1) // SEG + 1):
                    getsem(key, seg)

            def sem_and_val(sig):
                key, n = sig
                if key[0] == "d" and key[1].startswith("all:"):
                    n = cnt[key]
                seg = (n - 1) // SEG
                v = n - seg * SEG
                if key[0] == "d":
                    v *= 16
                return getsem(key, seg), v, (key, seg)

            by_eng = {e: [] for e in self.ENGS}
            for o in self.ops:
                by_eng[o.eng].append(o)
            block = stack.enter_context(nc.Block())

            def make_section(eng_name, final=False):
                def section(eng):
                    waited = {}
                    for o in by_eng[eng_name]:
                        for d in o.deps:
                            s, v, sk = sem_and_val(d.sig)
                            if waited.get(sk, 0) >= v:
                                continue
                            waited[sk] = v
                            eng.wait_ge(s, v)
                        ins = o.fn(eng)
                        if o.signal:
                            key, n = o.sig
                            seg = (n - 1) // SEG
                            ins.then_inc(getsem(key, seg), 16 if o.dsem is not None else 1)
                    if final:
                        for o in final_wait_ops:
                            s, v, sk = sem_and_val(o.sig)
                            if waited.get(sk, 0) >= v:
                                continue
                            waited[sk] = v
                            eng.wait_ge(s, v)
                return section

            block.sync(make_section("sp", final=True))
            block.scalar(make_section("act"))
            block.vector(make_section("dve"))
            block.gpsimd(make_section("pool"))
            block.tensor(make_section("pe"))


def build_program(layers):
    nc = bass.Bass("TRN2", target_bir_lowering=False)
    L = DEPTH

    def dram(name, shape, dt=F32, kind="ExternalInput"):
        return nc.dram_tensor(name, shape, dt, kind=kind).ap()

    d_xT = dram("xT", [NB, 8, 128, SEQ])
    d_out = dram("oT", [NB, 8, 128, SEQ], kind="ExternalOutput")
    d_cT = dram("cT", [128, 8 * NB])
    d_pos = dram("pos", [NB, 128, NT], I32)
    d_invf = dram("invf", [1, 8])
    d_pw = dram("pw", [1, NIT + 1])
    d_wada = dram("wada", [L, NADA, 128, 4096])
    d_badaT = dram("badaT", [L, 128, 48])
    d_gT = dram("gT", [L, 128, 16])
    d_wst = dram("wst", [L, NWB, 128, 4096])
    d_brow = dram("brow", [L, 128, 3072])
    d_bgT = dram("bgT", [L, 128, 16])
    d_gains = dram("gains", [L, 1, 256])
    d_biasT = dram("biasT", [L, 128, 5120])
    d_dbg = dram("dbg", [128, 16384], kind="ExternalOutput") if DBG_STAGE < 99 else None
    d_dbgb = dram("dbgb", [128, 16384], BF16, kind="ExternalOutput") if DBG_STAGE < 99 else None

    with contextlib.ExitStack() as st:
        def sb(name, shape, dt):
            return st.enter_context(nc.sbuf_tensor(name, shape, dt))

        pb = [st.enter_context(nc.psum_tensor("pb%d" % i, [128, 512], F32)) for i in range(8)]
        pbk = ["pb%d" % i for i in range(8)]

        xT = sb("xT_sb", [128, 8, SEQ], F32)
        KAT = sb("KAT", [128, SEQ], BF16)
        KIT = sb("KIT", [128, SEQ], BF16)
        VA = sb("VA", [128, NT, 65], BF16)
        KBT = sb("KBT", [128, 4, KSLOTS, 128], BF16)
        VB = sb("VB", [128, KSLOTS, 8, 65], BF16)
        wsl = sb("wsl", [128, NSLOT, 4096], BF16)
        hT = sb("hT", [128, 8, 512], BF16)
        rstd = sb("rstd", [128, 512], F32)
        tmpA = sb("tmpA", [128, 2, 512], F32)
        sqc = sb("sqc", [128, 2, 512], BF16)
        attnT = sb("attnT", [128, 2, 4, 512], BF16)
        brow = sb("brow_sb", [128, 3072], BF16)
        biasT = sb("biasT_sb", [128, 5, 2, 512], BF16)
        big = sb("big", [128, 43008], I8)
        ident = sb("ident", [128, 128], BF16)
        I4 = sb("I4", [128, 512], BF16)
        ones_s = sb("ones_s", [128, 128], BF16)
        ones_row = sb("ones_row", [128, 128], BF16)
        cTf = sb("cTf", [128, 8 * NB], F32)
        cact = sb("cact", [128, 8 * NB], BF16)
        modT = sb("modT", [128, L, 48, NB], F32)
        badaT = sb("badaT_sb", [128, L, 48], F32)
        gT = sb("gT_sb", [128, L, 16], F32)
        coef = sb("coef", [128, 2, 6, 8], F32)
        bgT = sb("bgT_sb", [128, 2, 16], F32)
        gains = sb("gains_sb", [128, 2, 256], F32)
        invf = sb("invf_sb", [128, 8], F32)
        pw = sb("pw_sb", [128, NIT + 1], F32)
        posi = sb("posi", [128, NT], I32)
        posf = sb("posf", [128, NT], F32)
        ang = sb("ang", [128, NT * 8], F32)
        rr = sb("rr", [128, 2, NT * 8], F32)
        rki = sb("rki", [128, NT * 8], I32)
        rkf = sb("rkf", [128, NT * 8], F32)
        cosT = sb("cosT", [128, NT, 8], F32)
        sinT = sb("sinT", [128, NT, 8], F32)
        ss = sb("ss", [128, 2, 8], F32)
        rs = sb("rs", [128, 2, 8], F32)
        rp = sb("rp", [128, 4, 80], F32)
        wsc = sb("wsc", [128, 4, 8], F32)
        bs = sb("bs", [128, 8], F32)
        rw = sb("rw", [128, NIT + 1], F32)
        tauc = sb("tauc", [128, 1], F32)
        rc = sb("rc", [128, 4, 4], F32)

        def carve(off, nbytes, dt):
            return big[:, off:off + nbytes].bitcast(dt)

        ztok = [carve(0 + i * 2048, 2048, F32) for i in range(2)]
        zscr = [carve(4096 + i * 2048, 2048, F32) for i in range(2)]
        zb = [carve(8192 + i * 1024, 1024, BF16) for i in range(2)]
        QT = carve(10240, 12288, BF16).rearrange("p (b t) -> p b t", t=512)
        score = carve(22528, 8192, F32)
        negm = carve(30720, 4096, BF16)
        rl = [carve(34816 + i * 1024, 1024, BF16) for i in range(2)]
        PT = [carve(36864 + i * 1024, 1024, BF16) for i in range(4)]
        atok = [carve(40960 + i * 1024, 1024, BF16) for i in range(2)]
        gates = carve(0, 16384, BF16).rearrange("p (c t) -> p c t", t=512)
        merged = carve(16384, 8192, BF16).rearrange("p (c t) -> p c t", t=512)
        hidden = carve(0, 22528, BF16).rearrange("p (c t) -> p c t", t=512)

        def gk(off, nbytes):
            return [("big", i) for i in range(off // 1024, (off + nbytes + 1023) // 1024)]

        K_ztok = [gk(0 + i * 2048, 2048) for i in range(2)]
        K_zscr = [gk(4096 + i * 2048, 2048) for i in range(2)]
        K_zb = [gk(8192 + i * 1024, 1024) for i in range(2)]
        K_QT = lambda blk, tl: gk(10240 + blk * 1024 + tl * 256, 256)
        K_QT_blk = lambda b0, b1, tl: sum([K_QT(bb, tl) for bb in range(b0, b1)], [])
        K_score = gk(22528, 8192)
        K_negm = gk(30720, 4096)
        K_rl = [gk(34816 + i * 1024, 1024) for i in range(2)]
        K_PT = [gk(36864 + i * 1024, 1024) for i in range(4)]
        K_atok = [gk(40960 + i * 1024, 1024) for i in range(2)]
        K_gates = lambda c: gk(c * 1024, 1024)
        K_merged = lambda c: gk(16384 + c * 1024, 1024)
        K_hidden = lambda c: gk(c * 1024, 1024)

        S = Sched(nc)
        dbg_ops = []

        def dump(ap, col0, ncols, keys):
            if d_dbg is None:
                return
            dst = d_dbgb if ap.dtype == BF16 else d_dbg
            o = S.op("sp", lambda e: e.dma_start(out=dst[:, col0:col0 + ncols], in_=ap), reads=keys, dsem="dbg%d" % len(dbg_ops))
            dbg_ops.append(o)

        blocks = []
        for l in range(L):
            if l in layers:
                for i in range(NADA):
                    blocks.append(d_wada[l, i])
        for b in range(NB):
            for l in layers:
                for g in range(NG):
                    for i in range(NWB):
                        blocks.append(d_wst[l, i])
        wstate = {"cur": 0, "issued": 0}

        def wnext(keep=0):
            i = wstate["cur"]
            wstate["cur"] += 1
            while wstate["issued"] < min(len(blocks), i - keep + NSLOT):
                j = wstate["issued"]
                sl = j % NSLOT
                S.op("pool", lambda e, j=j, sl=sl: e.dma_start(out=wsl[:, sl, :], in_=blocks[j]),
                     writes=[("w", sl)], dsem="w%d" % sl)
                wstate["issued"] += 1
            return i % NSLOT

        gpstate = {"i": 0}

        def gp():
            gpstate["i"] ^= 1
            return gpstate["i"]

        def mm(out, lhsT, rhs, start, stop, reads, bank, skip=False):
            if skip:
                S.op("pe", lambda e: e.matmul(out, lhsT, rhs, start=start, stop=stop, skip_group_check=True),
                     reads=reads, writes=[pbk[bank]])
            else:
                S.op("pe", lambda e: e.matmul(out, lhsT, rhs, start=start, stop=stop),
                     reads=reads, writes=[pbk[bank]])

        S.op("pool", lambda e: e.memset(ident[:], 1.0), writes=["ident"])
        S.op("pool", lambda e: e.affine_select(out=ident[:], in_=ident[:], pattern=[[-1, 128]],
                                               compare_op=ALU.is_equal, fill=0.0, base=0, channel_multiplier=1),
             reads=["ident"], writes=["ident"])
        for i in range(4):
            S.op("pool", lambda e, i=i: e.tensor_copy(I4[:, i * 128:(i + 1) * 128], ident[:]),
                 reads=["ident"], writes=[("I4", i)])
        K_I4 = [("I4", i) for i in range(4)]
        S.op("pool", lambda e: e.memset(ones_s[:], 1.0 / 1024.0), writes=["ones_s"])
        S.op("pool", lambda e: e.memset(ones_row[:], 1.0), writes=["ones_row"])
        S.op("pool", lambda e: e.memset(VA[:, :, 64:65], 1.0), writes=["VA1"])
        S.op("pool", lambda e: e.memset(VB[:, :, :, 64:65], 1.0), writes=["VB1"])
        S.op("pool", lambda e: e.memset(tauc[:], -1e29), writes=["tauc"])
        pre = "all:pre"
        S.op("sp", lambda e: e.dma_start(out=cTf[:], in_=d_cT), writes=["cTf"], dsem=pre)
        S.op("sp", lambda e: e.dma_start(out=invf[:], in_=d_invf.partition_broadcast(128)), writes=["invf"], dsem=pre)
        S.op("sp", lambda e: e.dma_start(out=pw[:], in_=d_pw.partition_broadcast(128)), writes=["pw"], dsem=pre)
        for l in range(L):
            S.op("sp", lambda e, l=l: e.dma_start(out=badaT[:, l, :], in_=d_badaT[l]), writes=[("badaT", l)], dsem=pre)
            S.op("sp", lambda e, l=l: e.dma_start(out=gT[:, l, :], in_=d_gT[l]), writes=[("gT", l)], dsem=pre)
        S.op("act", lambda e: e.activation(out=cact[:], in_=cTf[:], func=AF.Silu), reads=["cTf"], writes=["cact"])

        for l in range(L):
            if l not in layers:
                continue
            if 'mod' in DBG_SKIP:
                for blk in range(NADA):
                    wnext()
                continue
            for blk in range(NADA):
                sl = wnext()
                for jj in range(4):
                    j = blk * 4 + jj
                    for k in range(8):
                        mm(pb[0][:, j * NB:(j + 1) * NB], wsl[:, sl, k * 512 + jj * 128:k * 512 + (jj + 1) * 128],
                           cact[:, k * NB:(k + 1) * NB], k == 0, k == 7, [("w", sl), "cact"], 0)
            S.op("dve", lambda e, l=l: e.tensor_tensor(
                out=modT[:, l, :, :], in0=pb[0][:, 0:48 * NB].rearrange("p (j b) -> p j b", b=NB),
                in1=badaT[:, l, :].unsqueeze(2).to_broadcast([128, 48, NB]), op=ALU.add),
                reads=[("badaT", l)], writes=[pbk[0], ("modT", l)])

        def norm_mod(b, l, g, ci, which):
            gs = slice(g * 512, (g + 1) * 512)
            bank = gp()
            for c in range(8):
                q = c % 2
                S.op("act", lambda e, c=c, q=q: e.activation(out=sqc[:, q, :], in_=xT[:, c, gs], func=AF.Square),
                     reads=[("xT", c, g)], writes=[("sqc", q)])
                mm(pb[bank][:, :], ones_s[:, :], sqc[:, q, :], c == 0, c == 7, [("sqc", q), "ones_s"], bank)
            S.op("act", lambda e: e.activation(out=rstd[:], in_=pb[bank][:, :], func=AF.Sqrt, bias=epsT[:, 0:1]),
                 reads=["epsT"], writes=[pbk[bank], "rstd"])
            S.op("dve", lambda e: e.reciprocal(out=rstd[:], in_=rstd[:]), reads=["rstd"], writes=["rstd"])
            a0 = 3 * which
            for c in range(8):
                q = c % 2
                S.op("dve", lambda e, c=c, q=q: e.scalar_tensor_tensor(
                    out=tmpA[:, q, :], in0=xT[:, c, gs], scalar=coef[:, ci, a0, c:c + 1], in1=rstd[:],
                    op0=ALU.mult, op1=ALU.mult),
                    reads=[("xT", c, g), "rstd", ("coef", ci)], writes=[("tmpA", q)])
                S.op("act", lambda e, c=c, q=q: e.activation(out=hT[:, c, :], in_=tmpA[:, q, :], func=AF.Identity,
                                                             bias=coef[:, ci, a0 + 1, c:c + 1]),
                     reads=[("tmpA", q), ("coef", ci)], writes=[("hT", c)])

        epsT = sb("epsT", [128, 2], F32)
        S.op("pool", lambda e: e.memset(epsT[:], EPS), writes=["epsT"])

        def rope(z, nh, t, zkeys):
            zv = z[:, 0:nh * 64].rearrange("p (h d) -> p h d", d=64)
            x1 = zv[:, :, 0:8]
            x2 = zv[:, :, 8:16]
            cb = cosT[:, t, :].unsqueeze(1).to_broadcast([128, nh, 8])
            sn = sinT[:, t, :].unsqueeze(1).to_broadcast([128, nh, 8])
            tv = [rp[:, i, 0:nh * 8].rearrange("p (h d) -> p h d", d=8) for i in range(4)]
            S.op("dve", lambda e: e.tensor_tensor(out=tv[0], in0=x1, in1=cb, op=ALU.mult), reads=zkeys + ["rope_tab"], writes=[("rp", 0)])
            S.op("dve", lambda e: e.tensor_tensor(out=tv[1], in0=x2, in1=sn, op=ALU.mult), reads=zkeys + ["rope_tab"], writes=[("rp", 1)])
            S.op("dve", lambda e: e.tensor_tensor(out=tv[2], in0=x2, in1=cb, op=ALU.mult), reads=zkeys + ["rope_tab"], writes=[("rp", 2)])
            S.op("dve", lambda e: e.tensor_tensor(out=tv[3], in0=x1, in1=sn, op=ALU.mult), reads=zkeys + ["rope_tab"], writes=[("rp", 3)])
            S.op("dve", lambda e: e.tensor_tensor(out=x1, in0=tv[0], in1=tv[1], op=ALU.subtract),
                 reads=[("rp", 0), ("rp", 1)], writes=zkeys)
            S.op("dve", lambda e: e.tensor_tensor(out=x2, in0=tv[2], in1=tv[3], op=ALU.add),
                 reads=[("rp", 2), ("rp", 3)], writes=zkeys)

        def zproj_block(blk, sl, tl, t, li):
            ncols = 512 if blk < 5 else 328
            bank = gp()
            for k in range(8):
                mm(pb[bank][:, 0:ncols], hT[:, k, tl * 128:(tl + 1) * 128], wsl[:, sl, k * 512:k * 512 + ncols],
                   k == 0, False, [("hT", k), ("w", sl)], bank)
            mm(pb[bank][:, 0:ncols], ones_row[:, :], brow[:, blk * 512:blk * 512 + ncols], False, True,
               ["ones_row", "brow"], bank)
            zi = (blk * 4 + tl) % 2
            z = ztok[zi]
            zk = K_ztok[zi]
            S.op("act", lambda e: e.activation(out=z[:, 0:ncols], in_=pb[bank][:, 0:ncols], func=AF.Copy),
                 writes=[pbk[bank]] + zk)
            nh_norm = {0: 8, 1: 8, 2: 8, 5: 2}.get(blk, 0)
            gidx = {0: 0, 1: 2, 2: 3, 5: 1}.get(blk, 0)
            if nh_norm:
                w = nh_norm * 64
                S.op("act", lambda e: e.activation(out=zscr[zi][:, 0:w], in_=z[:, 0:w], func=AF.Square),
                     reads=zk, writes=K_zscr[zi])
                S.op("dve", lambda e: e.tensor_reduce(out=ss[:, zi, 0:nh_norm],
                                                      in_=zscr[zi][:, 0:w].rearrange("p (h d) -> p h d", d=64),
                                                      axis=AX.X, op=ALU.add),
                     reads=K_zscr[zi], writes=[("ss", zi)])
                S.op("act", lambda e: e.activation(out=rs[:, zi, 0:nh_norm], in_=ss[:, zi, 0:nh_norm], func=AF.Sqrt,
                                                   scale=1.0 / 64.0, bias=epsT[:, 0:1]),
                     reads=[("ss", zi), "epsT"], writes=[("rs", zi)])
                S.op("dve", lambda e: e.reciprocal(out=rs[:, zi, 0:nh_norm], in_=rs[:, zi, 0:nh_norm]),
                     reads=[("rs", zi)], writes=[("rs", zi)])
                zv = z[:, 0:w].rearrange("p (h d) -> p h d", d=64)
                S.op("dve", lambda e: e.tensor_tensor(out=zv, in0=zv,
                                                      in1=rs[:, zi, 0:nh_norm].unsqueeze(2).to_broadcast([128, nh_norm, 64]),
                                                      op=ALU.mult),
                     reads=zk + [("rs", zi)], writes=zk)
                S.op("dve", lambda e: e.tensor_tensor(out=zv, in0=zv,
                                                      in1=gains[:, li, gidx * 64:(gidx + 1) * 64].unsqueeze(1).to_broadcast([128, nh_norm, 64]),
                                                      op=ALU.mult),
                     reads=zk + [("gains", li)], writes=zk)
            if blk in (0, 3):
                rope(z, 8, t, zk)
            if blk == 5:
                rope(z, 4, t, zk)
            if blk in (0, 1, 2, 3):
                S.op("pool", lambda e: e.tensor_copy(zb[zi][:, :], z[:, :]), reads=zk, writes=K_zb[zi])
                tb = gp()
                tbv = pb[tb][:, :].bitcast(BF16)
                for i in range(4):
                    S.op("pe", lambda e, i=i: e.transpose(tbv[:, i * 128:(i + 1) * 128], zb[zi][:, i * 128:(i + 1) * 128], ident[:]),
                         reads=K_zb[zi] + ["ident"], writes=[pbk[tb]])
                src = tbv[:, 0:512].rearrange("p (i q) -> p i q", q=128)
                if blk == 2:
                    slot = t % KSLOTS
                    S.op("dve", lambda e: e.tensor_copy(KBT[:, :, slot, :], src), writes=[pbk[tb], ("KBT", slot)])
                else:
                    b0 = {0: 0, 1: 4, 3: 8}[blk]
                    S.op("dve", lambda e: e.tensor_copy(QT[:, b0:b0 + 4, tl * 128:(tl + 1) * 128], src),
                         writes=[pbk[tb]] + K_QT_blk(b0, b0 + 4, tl))
            if blk == 4:
                slot = t % KSLOTS
                S.op("pool", lambda e: e.tensor_copy(VB[:, slot, :, 0:64], z[:, :].rearrange("p (h d) -> p h d", d=64)),
                     reads=zk + ["VB1"], writes=[("VB", slot)])
            if blk == 5:
                S.op("pool", lambda e: e.tensor_copy(zb[zi][:, 0:256], z[:, 0:256]), reads=zk, writes=K_zb[zi])
                tb = gp()
                tbv = pb[tb][:, :].bitcast(BF16)
                for i in range(2):
                    S.op("pe", lambda e, i=i: e.transpose(tbv[:, i * 128:(i + 1) * 128], zb[zi][:, i * 128:(i + 1) * 128], ident[:]),
                         reads=K_zb[zi] + ["ident"], writes=[pbk[tb]])
                S.op("dve", lambda e: e.tensor_copy(KAT[:, t * 128:(t + 1) * 128], tbv[:, 0:128]),
                     writes=[pbk[tb], ("KAT", t)])
                S.op("dve", lambda e: e.tensor_copy(KIT[:, t * 128:(t + 1) * 128], tbv[:, 128:256]),
                     writes=[pbk[tb], ("KIT", t)])
                S.op("pool", lambda e: e.tensor_copy(VA[:, t, 0:64], z[:, 256:320]), reads=zk + ["VA1"], writes=[("VA", t)])
                S.op("dve", lambda e: e.tensor_scalar(out=wsc[:, tl, :], in0=z[:, 320:328], scalar1=INDEX_SCALE,
                                                       scalar2=None, op0=ALU.mult),
                     reads=zk, writes=[("wsc", tl)])

        def indexer(tl, t):
            nkeys = (t + 1) * 128
            for c0 in range(0, nkeys, 512):
                w = min(512, nkeys - c0)
                kit_keys = [("KIT", j) for j in range(c0 // 128, (c0 + w) // 128)]
                sc_keys = gk(22528 + c0 * 4, w * 4)
                for h in range(8):
                    half = h % 2
                    ps_ = slice(half * 64, (half + 1) * 64)
                    bank = gp()
                    mm(pb[bank][:, 0:w], QT[ps_, 8 + h // 2, tl * 128:(tl + 1) * 128], KIT[ps_, c0:c0 + w], True, True,
                       K_QT(8 + h // 2, tl) + kit_keys, bank)
                    q = h % 2
                    S.op("act", lambda e, bank=bank, q=q, w=w: e.activation(out=rl[q][:, 0:w], in_=pb[bank][:, 0:w], func=AF.Relu),
                         writes=[pbk[bank]] + K_rl[q])
                    if h == 0:
                        S.op("dve", lambda e, q=q, w=w, c0=c0: e.tensor_scalar(
                            out=score[:, c0:c0 + w], in0=rl[q][:, 0:w], scalar1=wsc[:, tl, 0:1], scalar2=None, op0=ALU.mult),
                            reads=K_rl[q] + [("wsc", tl)], writes=sc_keys)
                    else:
                        S.op("dve", lambda e, q=q, w=w, c0=c0, h=h: e.scalar_tensor_tensor(
                            out=score[:, c0:c0 + w], in0=rl[q][:, 0:w], scalar=wsc[:, tl, h:h + 1], in1=score[:, c0:c0 + w],
                            op0=ALU.mult, op1=ALU.add),
                            reads=K_rl[q] + [("wsc", tl)], writes=sc_keys)
            sk = K_score
            if t >= 2:
                S.op("dve", lambda e: e.tensor_reduce(out=bs[:, 0:1], in_=score[:, 0:nkeys], axis=AX.X, op=ALU.min),
                     reads=sk, writes=["bs0"])
            S.op("dve", lambda e: e.memset(score[0:64, t * 128 + 64:(t + 1) * 128], -1e30), writes=sk)
            if t >= 2:
                S.op("dve", lambda e: e.tensor_reduce(out=bs[:, 1:2], in_=score[:, 0:nkeys], axis=AX.X, op=ALU.max),
                     reads=sk, writes=["bs1"])
                S.op("dve", lambda e: e.tensor_tensor(out=bs[:, 2:3], in0=bs[:, 1:2], in1=bs[:, 0:1], op=ALU.subtract),
                     reads=["bs0", "bs1"], writes=["bs2"])
                S.op("dve", lambda e: e.tensor_scalar(out=rw[:, :], in0=pw[:, :], scalar1=bs[:, 2:3], scalar2=None, op0=ALU.mult),
                     reads=["pw", "bs2"], writes=["rw"])
                S.op("dve", lambda e: e.tensor_tensor(out=bs[:, 3:4], in0=bs[:, 0:1], in1=rw[:, 0:1], op=ALU.add),
                     reads=["bs0", "rw"], writes=["mid"])
                for it in range(NIT):
                    S.op("dve", lambda e: e.tensor_scalar(out=negm[:, 0:nkeys], in0=score[:, 0:nkeys], scalar1=bs[:, 3:4],
                                                          scalar2=None, op0=ALU.is_gt, op1=ALU.add, accum_out=bs[:, 4:5]),
                         reads=sk + ["mid"], writes=K_negm + ["cnt"])
                    S.op("dve", lambda e: e.tensor_scalar(out=bs[:, 5:6], in0=bs[:, 4:5], scalar1=TOPK - 0.5, scalar2=0.5,
                                                          op0=ALU.is_ge, op1=ALU.subtract),
                         reads=["cnt"], writes=["tt"])
                    S.op("dve", lambda e, it=it: e.scalar_tensor_tensor(out=bs[:, 3:4], in0=bs[:, 5:6], scalar=rw[:, it:it + 1],
                                                                        in1=bs[:, 3:4], op0=ALU.mult, op1=ALU.add),
                         reads=["tt", "rw", "mid"], writes=["mid"])
                S.op("dve", lambda e: e.tensor_tensor(out=bs[:, 6:7], in0=bs[:, 3:4], in1=rw[:, NIT:NIT + 1], op=ALU.subtract),
                     reads=["mid", "rw"], writes=["tau"])
                tau = bs[:, 6:7]
                tk = ["tau"]
            else:
                tau = tauc[:, 0:1]
                tk = ["tauc"]
            S.op("dve", lambda e: e.tensor_scalar(out=negm[:, 0:nkeys], in0=score[:, 0:nkeys], scalar1=tau, scalar2=NEG,
                                                  op0=ALU.is_le, op1=ALU.mult),
                 reads=sk + tk, writes=K_negm)

        ptc = {"i": 0}

        def attn_finish(br, tl, obase):
            av = atok[br][:, :].rearrange("p (i two d) -> p i two d", two=2, d=64)
            for half in range(2):
                ob = obase + half
                ov = pb[ob][:, 0:260].rearrange("p (i d) -> p i d", d=65)
                S.op("dve", lambda e, ov=ov, half=half: e.reciprocal(out=rc[:, br * 2 + half, :], in_=ov[:, :, 64]),
                     writes=[pbk[ob], ("rc", br, half)])
                S.op("dve", lambda e, ov=ov, half=half: e.tensor_tensor(
                    out=av[:, :, half, :], in0=ov[:, :, 0:64],
                    in1=rc[:, br * 2 + half, :].unsqueeze(2).to_broadcast([128, 4, 64]), op=ALU.mult),
                    reads=[("rc", br, half)], writes=[pbk[ob]] + K_atok[br])
            tb = gp()
            tbv = pb[tb][:, :].bitcast(BF16)
            for i in range(4):
                S.op("pe", lambda e, i=i: e.transpose(tbv[:, i * 128:(i + 1) * 128], atok[br][:, i * 128:(i + 1) * 128], ident[:]),
                     reads=K_atok[br] + ["ident"], writes=[pbk[tb]])
            S.op("act", lambda e: e.activation(out=attnT[:, br, :, tl * 128:(tl + 1) * 128],
                                               in_=tbv[:, 0:512].rearrange("p (i q) -> p i q", q=128), func=AF.Copy),
                 writes=[pbk[tb], ("attnT", br, tl)])

        def attn_A(tl, t):
            for j in range(t + 1):
                for half in range(2):
                    ps_ = slice(half * 64, (half + 1) * 64)
                    bank = 2 + half
                    mm(pb[bank][:, :], KAT[ps_, j * 128:(j + 1) * 128], QT[ps_, 0:4, tl * 128:(tl + 1) * 128], True, False,
                       [("KAT", j)] + K_QT_blk(0, 4, tl), bank)
                    mm(pb[bank][:, :], negm[:, j * 128:(j + 1) * 128], I4[:, :], False, True, K_negm + K_I4, bank)
                    pi = ptc["i"] % 4
                    ptc["i"] += 1
                    S.op("act", lambda e, bank=bank, pi=pi: e.activation(out=PT[pi][:, :], in_=pb[bank][:, :], func=AF.Exp),
                         writes=[pbk[bank]] + K_PT[pi])
                    for i in range(4):
                        mm(pb[4 + half][:, i * 65:(i + 1) * 65], PT[pi][:, i * 128:(i + 1) * 128], VA[:, j, :],
                           (j == 0 and i == 0), j == t, K_PT[pi] + [("VA", j), "VA1"], 4 + half, skip=True)
            attn_finish(0, tl, 4)

        def attn_B(tl, t):
            j0 = max(0, t - 4)
            for j in range(j0, t + 1):
                jrel = j - (t - 4)
                slot = j % KSLOTS
                for half in range(2):
                    ps_ = slice(half * 64, (half + 1) * 64)
                    bank = 2 + half
                    for i in range(4):
                        mm(pb[bank][:, i * 128:(i + 1) * 128], KBT[ps_, i, slot, :], QT[ps_, 4 + i, tl * 128:(tl + 1) * 128],
                           i == 0, False, [("KBT", slot)] + K_QT(4 + i, tl), bank, skip=True)
                    mm(pb[bank][:, :], ident[:, :], biasT[:, jrel, half, :], False, True, ["ident", "biasT"], bank, skip=True)
                    pi = ptc["i"] % 4
                    ptc["i"] += 1
                    S.op("act", lambda e, bank=bank, pi=pi: e.activation(out=PT[pi][:, :], in_=pb[bank][:, :], func=AF.Exp),
                         writes=[pbk[bank]] + K_PT[pi])
                    for i in range(4):
                        mm(pb[6 + half][:, i * 65:(i + 1) * 65], PT[pi][:, i * 128:(i + 1) * 128], VB[:, slot, 2 * i + half, :],
                           (j == j0 and i == 0), j == t, K_PT[pi] + [("VB", slot), "VB1"], 6 + half, skip=True)
            attn_finish(1, tl, 6)

        def mixer_group(b, l, g, ci, li):
            gs = slice(g * 512, (g + 1) * 512)
            if DBG_STAGE < 1:
                return
            norm_mod(b, l, g, ci, 0)
            if DBG_STAGE < 6:
                dump(hT[:, :, :].rearrange("p c t -> p (c t)"), 0, 4096, [("hT", c) for c in range(8)])
            if DBG_STAGE < 2:
                return
            for blk in range(6):
                sl = wnext()
                for tl in range(4):
                    zproj_block(blk, sl, tl, g * 4 + tl, li)
            if DBG_STAGE < 6:
              dump(QT[:, :, :].rearrange("p c t -> p (c t)"), 4096, 6144, K_QT_blk(0, 12, 0) + K_QT_blk(0, 12, 1) + K_QT_blk(0, 12, 2) + K_QT_blk(0, 12, 3))
            if DBG_STAGE < 6:
              dump(KAT[:, 0:512], 10240, 512, [("KAT", j) for j in range(4)])
            dump(KIT[:, 0:512], 10752, 512, [("KIT", j) for j in range(4)])
            if DBG_STAGE < 3:
                return
            for tl in range(4):
                t = g * 4 + tl
                indexer(tl, t)
                if tl == 3:
                    dump(score[:, 0:512], 11264, 512, K_score)
                    dump(negm[:, 0:512], 11776, 512, K_negm)
                if DBG_STAGE >= 4:
                    attn_A(tl, t)
                if DBG_STAGE >= 5:
                    attn_B(tl, t)
            dump(attnT[:, :, :, :].rearrange("p a c t -> p (a c t)"), 12288, 4096, [("attnT", br, x) for br in range(2) for x in range(4)])
            if DBG_STAGE < 6:
                return
            for blk in range(4):
                sl = wnext()
                for jj in range(4):
                    c = blk * 4 + jj
                    bank = gp()
                    for k in range(8):
                        mm(pb[bank][:, :], wsl[:, sl, k * 512 + jj * 128:k * 512 + (jj + 1) * 128], hT[:, k, :], k == 0, k == 7,
                           [("w", sl), ("hT", k)], bank)
                    S.op("act", lambda e, c=c, bank=bank: e.activation(out=gates[:, c, :], in_=pb[bank][:, :], func=AF.Sigmoid,
                                                                       bias=bgT[:, li, c:c + 1]),
                         reads=[("bgT", li)], writes=[pbk[bank]] + K_gates(c))
            sla = wnext()
            slb = wnext(keep=1)
            for c in range(8):
                for br, slx in ((0, sla), (1, slb)):
                    bank = gp()
                    for k in range(4):
                        mm(pb[bank][:, :], wsl[:, slx, k * 1024 + c * 128:k * 1024 + (c + 1) * 128], attnT[:, br, k, :], k == 0, k == 3,
                           [("w", slx)] + [("attnT", br, x) for x in range(4)], bank)
                    S.op("dve", lambda e, c=c, br=br, bank=bank: e.tensor_tensor(out=tmpA[:, br, :], in0=pb[bank][:, :],
                                                                                 in1=gates[:, br * 8 + c, :], op=ALU.mult),
                         reads=K_gates(br * 8 + c), writes=[pbk[bank], ("tmpA", br)])
                S.op("pool", lambda e, c=c: e.tensor_tensor(out=merged[:, c, :], in0=tmpA[:, 0, :], in1=tmpA[:, 1, :], op=ALU.add),
                     reads=[("tmpA", 0), ("tmpA", 1)], writes=K_merged(c))
            for i in range(2):
                sl = wnext()
                for cc in range(4):
                    c = i * 4 + cc
                    bank = gp()
                    for k in range(8):
                        mm(pb[bank][:, :], wsl[:, sl, k * 512 + cc * 128:k * 512 + (cc + 1) * 128], merged[:, k, :], k == 0, k == 7,
                           [("w", sl)] + K_merged(k), bank)
                    S.op("dve", lambda e, c=c, bank=bank: e.scalar_tensor_tensor(
                        out=xT[:, c, gs], in0=pb[bank][:, :], scalar=coef[:, ci, 2, c:c + 1], in1=xT[:, c, gs],
                        op0=ALU.mult, op1=ALU.add),
                        reads=[("coef", ci)], writes=[pbk[bank], ("xT", c, g)])
            if 'late' in DBG_SKIP:
                late_dumps.append((gs, g))
            elif DBG_STAGE >= 6:
                if 'mdump' not in DBG_SKIP:
                    dump(merged[:, :, :].rearrange("p c t -> p (c t)"), 0, 4096, sum([K_merged(c) for c in range(8)], []))
                if 'gdump' not in DBG_SKIP:
                    dump(gates[:, 0:8, :].rearrange("p c t -> p (c t)"), 8192, 4096, sum([K_gates(c) for c in range(8)], []))
                for c in range(8):
                    if 'xdump' not in DBG_SKIP:
                        dump(xT[:, c, gs], 4096 + c * 512, 512, [("xT", c, g)])

        late_dumps = []

        def ffn_group(b, l, g, ci):
            gs = slice(g * 512, (g + 1) * 512)
            norm_mod(b, l, g, ci, 1)
            if late_dumps:
                late_dumps.pop()
                for c in range(8):
                    dump(xT[:, c, gs], 4096 + c * 512, 512, [("xT", c, g)])
            pr_i = 0
            for i in range(11):
                sl = wnext()
                for pr in range(2):
                    j = 2 * i + pr
                    bg = 2 * (pr_i % 4)
                    bu = bg + 1
                    pr_i += 1
                    for k in range(8):
                        mm(pb[bg][:, :], wsl[:, sl, k * 512 + (2 * pr) * 128:k * 512 + (2 * pr + 1) * 128], hT[:, k, :], k == 0, k == 7,
                           [("w", sl), ("hT", k)], bg)
                    for k in range(8):
                        mm(pb[bu][:, :], wsl[:, sl, k * 512 + (2 * pr + 1) * 128:k * 512 + (2 * pr + 2) * 128], hT[:, k, :], k == 0, k == 7,
                           [("w", sl), ("hT", k)], bu)
                    q = j % 2
                    S.op("act", lambda e, bg=bg, q=q: e.activation(out=sqc[:, q, :], in_=pb[bg][:, :], func=AF.Silu),
                         writes=[pbk[bg], ("sqc", q)])
                    S.op("dve", lambda e, bu=bu, q=q, j=j: e.tensor_tensor(out=hidden[:, j, :], in0=pb[bu][:, :], in1=sqc[:, q, :], op=ALU.mult),
                         reads=[("sqc", q)], writes=[pbk[bu]] + K_hidden(j))
            for hf in range(2):
                base = 4 if hf == 0 else 0
                for kb in range(3):
                    sl = wnext()
                    nk = 8 if kb < 2 else 6
                    for kk in range(nk):
                        kt = kb * 8 + kk
                        for cc in range(4):
                            mm(pb[base + cc][:, :], wsl[:, sl, kk * 512 + cc * 128:kk * 512 + (cc + 1) * 128], hidden[:, kt, :],
                               kt == 0, kt == 21, [("w", sl)] + K_hidden(kt), base + cc)
                for cc in range(4):
                    c = hf * 4 + cc
                    S.op("dve", lambda e, c=c, bank=base + cc: e.scalar_tensor_tensor(
                        out=xT[:, c, gs], in0=pb[bank][:, :], scalar=coef[:, ci, 5, c:c + 1], in1=xT[:, c, gs],
                        op0=ALU.mult, op1=ALU.add),
                        reads=[("coef", ci)], writes=[pbk[base + cc], ("xT", c, g)])

        out_ops = []
        inst = 0
        for b in range(NB):
            xk_all = [("xT", c, g) for c in range(8) for g in range(NG)]
            for c in range(8):
                S.op("sp", lambda e, b=b, c=c: e.dma_start(out=xT[:, c, :], in_=d_xT[b, c]),
                     writes=[("xT", c, g) for g in range(NG)], dsem="all:x%d" % b)
            S.op("sp", lambda e, b=b: e.dma_start(out=posi[:], in_=d_pos[b]), writes=["posi"], dsem="all:pos%d" % b)
            S.op("dve", lambda e: e.tensor_copy(posf[:], posi[:]), reads=["posi"], writes=["posf"])
            angv = ang[:, :].rearrange("p (t f) -> p t f", f=8)
            S.op("dve", lambda e: e.tensor_tensor(out=angv, in0=posf[:, :].unsqueeze(2).to_broadcast([128, NT, 8]),
                                                  in1=invf[:, :].unsqueeze(1).to_broadcast([128, NT, 8]), op=ALU.mult),
                 reads=["posf", "invf"], writes=["ang"])
            for which, dst in ((0, sinT), (1, cosT)):
                if 'rope' in DBG_SKIP:
                    continue
                r_ = rr[:, which, :]
                wk = ("rr", which)
                S.op("dve", lambda e, which=which, r_=r_: e.tensor_scalar(out=r_, in0=ang[:, :], scalar1=1.0 / (2 * np.pi),
                                                                         scalar2=0.25 * which, op0=ALU.mult, op1=ALU.add),
                     reads=["ang"], writes=[wk])
                S.op("dve", lambda e, r_=r_: e.tensor_copy(rki[:, :], r_), reads=[wk], writes=["rki"])
                S.op("dve", lambda e: e.tensor_copy(rkf[:, :], rki[:, :]), reads=["rki"], writes=["rkf"])
                S.op("dve", lambda e, r_=r_: e.scalar_tensor_tensor(out=r_, in0=rkf[:, :], scalar=-6.28125, in1=ang[:, :],
                                                                    op0=ALU.mult, op1=ALU.add),
                     reads=["rkf", "ang"], writes=[wk])
                S.op("dve", lambda e, r_=r_: e.scalar_tensor_tensor(out=r_, in0=rkf[:, :], scalar=-(2 * np.pi - 6.28125), in1=r_,
                                                                    op0=ALU.mult, op1=ALU.add),
                     reads=["rkf", wk], writes=[wk])
                S.op("dve", lambda e, r_=r_, which=which: e.tensor_scalar(out=r_, in0=r_, scalar1=(np.pi / 2) * which, scalar2=3.1415925,
                                                                         op0=ALU.add, op1=ALU.min),
                     reads=[wk], writes=[wk])
                S.op("dve", lambda e, r_=r_: e.tensor_scalar(out=r_, in0=r_, scalar1=-3.1415925, scalar2=None, op0=ALU.max),
                     reads=[wk], writes=[wk])
                S.op("act", lambda e, r_=r_, dst=dst: e.activation(out=dst[:, :, :].rearrange("p t f -> p (t f)"), in_=r_, func=AF.Sin),
                     reads=[wk], writes=["rope_tab"])
            for l in layers:
                ci = inst % 2
                li = inst % 2
                inst += 1
                ld = "all:ld%d_%d" % (b, l)
                ldp = "all:ldp%d_%d" % (b, l)
                S.op("sp", lambda e, l=l, li=li: e.dma_start(out=bgT[:, li, :], in_=d_bgT[l]), writes=[("bgT", li)], dsem=ld)
                S.op("sp", lambda e, l=l, li=li: e.dma_start(out=gains[:, li, :], in_=d_gains[l].partition_broadcast(128)),
                     writes=[("gains", li)], dsem=ld)
                if 'brow' not in DBG_SKIP:
                    S.op("pool", lambda e, l=l: e.dma_start(out=brow[:, :], in_=d_brow[l]), writes=["brow"], dsem=ldp)
                S.op("pool", lambda e, l=l: e.dma_start(out=biasT[:, :, :, :].rearrange("p a b c -> p (a b c)"), in_=d_biasT[l]),
                     writes=["biasT"], dsem=ldp)
                S.op("dve", lambda e, li=li: e.tensor_scalar(
                    out=gains[:, li, :].rearrange("p (a b d) -> p a b d", b=2, d=64)[:, :, 0, :],
                    in0=gains[:, li, :].rearrange("p (a b d) -> p a b d", b=2, d=64)[:, :, 0, :],
                    scalar1=0.125, scalar2=None, op0=ALU.mult),
                    reads=[("gains", li)], writes=[("gains", li)])
                for which in range(2):
                    if 'coef' in DBG_SKIP:
                        continue
                    o3 = 24 * which
                    S.op("dve", lambda e, l=l, b=b, o3=o3, which=which, ci=ci: e.scalar_tensor_tensor(
                        out=coef[:, ci, 3 * which, :], in0=modT[:, l, o3 + 8:o3 + 16, b], scalar=1.0,
                        in1=gT[:, l, 8 * which:8 * which + 8], op0=ALU.add, op1=ALU.mult),
                        reads=[("modT", l), ("gT", l)], writes=[("coef", ci)])
                    S.op("dve", lambda e, l=l, b=b, o3=o3, which=which, ci=ci: e.tensor_copy(coef[:, ci, 3 * which + 1, :], modT[:, l, o3:o3 + 8, b]),
                         reads=[("modT", l)], writes=[("coef", ci)])
                    S.op("dve", lambda e, l=l, b=b, o3=o3, which=which, ci=ci: e.tensor_copy(coef[:, ci, 3 * which + 2, :], modT[:, l, o3 + 16:o3 + 24, b]),
                         reads=[("modT", l)], writes=[("coef", ci)])
                for g in range(NG):
                    if DBG_STAGE < 99 and (b > 0 or g > 0):
                        continue
                    mixer_group(b, l, g, ci, li)
                    if DBG_STAGE >= 7:
                        ffn_group(b, l, g, ci)
            for c in range(8):
                o = S.op("sp", lambda e, b=b, c=c: e.dma_start(out=d_out[b, c], in_=xT[:, c, :]),
                         reads=[("xT", c, g) for g in range(NG)], dsem="all:o%d" % b)
                out_ops.append(o)
        assert DBG_STAGE < 99 or wstate["cur"] == len(blocks), (wstate, len(blocks))
        S.emit(final_wait_ops=out_ops + dbg_ops)
    return nc


def _blockify(W, ncols_blk=512):
    K, N = W.shape
    assert N % ncols_blk == 0
    kc = K // 128
    out = np.zeros((N // ncols_blk, 128, 8, ncols_blk), np.float32)
    Wr = W.reshape(kc, 128, N // ncols_blk, ncols_blk)
    out[:, :, :kc, :] = Wr.transpose(2, 1, 0, 3)
    return out.reshape(N // ncols_blk, 128, 8 * ncols_blk)


def _prep_shared(inp):
    L = DEPTH
    offs = np.concatenate([[0], np.cumsum(IN_SIZES)])
    rng_ = lambda i: np.arange(offs[i], offs[i + 1])
    qa, ka, va, qi, ki, wi, qb, kb, vb, ga, gb = [rng_(i) for i in range(11)]
    tok_cols = np.concatenate([qa, qb, kb, qi, vb, ka, ka, ki, ki, va, wi])
    gate_cols = np.concatenate([ga, gb])
    gu_cols = []
    for i in range(11):
        for pr in range(2):
            j = 2 * i + pr
            gu_cols.append(np.arange(j * 128, (j + 1) * 128))
            gu_cols.append(D_FF + np.arange(j * 128, (j + 1) * 128))
    gu_cols = np.concatenate(gu_cols)
    wst = np.zeros((L, NWB, 128, 4096), np.float32)
    wada = np.zeros((L, NADA, 128, 4096), np.float32)
    brow = np.zeros((L, 128, 3072), np.float32)
    bgT = np.zeros((L, 128, 16), np.float32)
    badaT = np.zeros((L, 128, 48), np.float32)
    gT = np.zeros((L, 128, 16), np.float32)
    gains = np.zeros((L, 1, 256), np.float32)
    biasT = np.zeros((L, 128, 5120), np.float32)
    k_ = np.arange(128)[:, None, None]
    j_ = np.arange(5)[None, :, None]
    q_ = np.arange(128)[None, None, :]
    tdiff = q_ - k_ + 128 * (4 - j_)
    ridx = np.clip(tdiff, -63, 256) + 63
    cq = q_ // 64
    ck = 2 * j_ + k_ // 64
    vis = (ck >= cq) & (ck <= 8 + cq)
    for l in range(L):
        w_in = np.asarray(inp["w_in"][l])
        wtok = np.zeros((1024, 3072), np.float32)
        wtok[:, :2888] = w_in[:, tok_cols]
        wst[l, 0:6] = _blockify(wtok)
        wst[l, 6:10] = _blockify(w_in[:, gate_cols])
        woa = np.asarray(inp["w_oa"][l]).reshape(4, 128, 1024).transpose(1, 0, 2).reshape(128, 4096)
        wob = np.asarray(inp["w_ob"][l]).reshape(4, 128, 1024).transpose(1, 0, 2).reshape(128, 4096)
        wst[l, 10] = woa
        wst[l, 11] = wob
        wst[l, 12:14] = _blockify(np.asarray(inp["w_out"][l]))
        wst[l, 14:25] = _blockify(np.asarray(inp["w_gu"][l])[:, gu_cols])
        wd = np.zeros((3072, 1024), np.float32)
        wd[:D_FF] = np.asarray(inp["w_down"][l])
        for hf in range(2):
            for kbk in range(3):
                sub = wd[kbk * 1024:(kbk + 1) * 1024, hf * 512:(hf + 1) * 512]
                wst[l, 25 + hf * 3 + kbk] = _blockify(sub)[0]
        wada[l] = _blockify(np.asarray(inp["w_ada"][l]))
        b_in = np.asarray(inp["b_in"][l])
        brow[l, 0, :2888] = b_in[tok_cols]
        bgT[l] = b_in[gate_cols].reshape(16, 128).T
        badaT[l] = np.asarray(inp["b_ada"][l]).reshape(48, 128).T
        gT[l, :, 0:8] = np.asarray(inp["g_mix"][l]).reshape(8, 128).T
        gT[l, :, 8:16] = np.asarray(inp["g_ffn"][l]).reshape(8, 128).T
        gains[l, 0] = np.concatenate([np.asarray(inp[k][l]) for k in ("qn_a", "kn_a", "qn_b", "kn_b")])
        rb = np.asarray(inp["rel_bias"][l])
        g_ = rb[:, ridx]
        g_ = np.where(vis[None], g_, np.float32(NEG)).astype(np.float32)
        g_ = g_.reshape(4, 2, 128, 5, 128)
        biasT[l] = g_.transpose(2, 3, 1, 0, 4).reshape(128, 5120)
    invf = (np.float32(ROPE_THETA) ** (-np.arange(0, 16, 2, dtype=np.float32) / np.float32(16))).astype(np.float32).reshape(1, 8)
    pw = (0.5 ** np.arange(1, NIT + 2)).astype(np.float32).reshape(1, NIT + 1)
    return dict(wst=wst, wada=wada, brow=brow, bgT=bgT, badaT=badaT, gT=gT, gains=gains, biasT=biasT, invf=invf, pw=pw)


def _prep_core(x_t, c, positions, core):
    bsl = slice(core * NB, (core + 1) * NB)
    xT = np.ascontiguousarray(x_t[bsl].transpose(0, 2, 1)).reshape(NB, 8, 128, SEQ)
    cT = np.ascontiguousarray(np.asarray(c)[bsl].reshape(NB, 8, 128).transpose(2, 1, 0)).reshape(128, 8 * NB)
    pos = np.ascontiguousarray(np.asarray(positions)[bsl].reshape(NB, NT, 128).transpose(0, 2, 1)).astype(np.int32)
    return dict(xT=xT, cT=cT, pos=pos)


_PROG_CACHE = {}


def _get_prog(layers):
    key = tuple(layers)
    if key not in _PROG_CACHE:
        _PROG_CACHE[key] = build_program(list(layers))
    return _PROG_CACHE[key]


def _run(x_cur, c, positions, shared, layers):
    nc = _get_prog(layers)
    in_maps = []
    for core in range(N_CORES):
        m = dict(shared)
        m.update(_prep_core(x_cur, c, positions, core))
        in_maps.append(m)
    res = run_bass_kernel_spmd(nc, in_maps, core_ids=list(range(N_CORES)))
    outs = []
    for core in range(N_CORES):
        oT = np.asarray(res.results[core]["oT"]).reshape(NB, 1024, SEQ)
        outs.append(oT.transpose(0, 2, 1))
    return np.ascontiguousarray(np.concatenate(outs, axis=0)).astype(np.float32)


FUSED = True


def kernel(**inputs):
    inp = {k: np.asarray(v) for k, v in inputs.items()}
    shared = _prep_shared(inp)
    x = inp["x"].astype(np.float32)
    if FUSED:
        return _run(x, inp["c"], inp["positions"], shared, list(range(DEPTH)))
    for l in range(DEPTH):
        x = _run(x, inp["c"], inp["positions"], shared, [l])
    return x
```

```python
import contextlib
import numpy as np
import concourse.bass as bass
import concourse.mybir as mybir
from concourse.bass_utils import run_bass_kernel_spmd

F32 = mybir.dt.float32
BF16 = mybir.dt.bfloat16
I32 = mybir.dt.int32
I8 = mybir.dt.int8
ALU = mybir.AluOpType
AF = mybir.ActivationFunctionType
AX = mybir.AxisListType

D_MODEL = 1024
SEQ = 2048
DEPTH = 2
N_CORES = 8
NB = 2
HD = 64
D_FF = 2816
IN_SIZES = (512, 64, 64, 512, 64, 8, 512, 512, 512, 1024, 1024)
TOPK = 256
EPS = 1e-6
INDEX_SCALE = (64 ** -0.5) * (8 ** -0.5)
ROPE_THETA = 500000.0
NT = SEQ // 128
NG = 4
NWB = 31
NADA = 12
NSLOT = 3
NIT = 22
NEG = -30000.0
KSLOTS = 8

SEG = 30000
DBG_STAGE = 99
import os
DBG_SKIP = set(os.environ.get('DBG_SKIP', '').split(','))


class Op:
    __slots__ = ("eng", "fn", "deps", "dsem", "signal", "sig")

    def __init__(self, eng, fn, dsem=None):
        self.eng = eng
        self.fn = fn
        self.deps = []
        self.dsem = dsem
        self.signal = False
        self.sig = None


class Sched:
    ENGS = ("pe", "act", "dve", "pool", "sp")

    def __init__(self, nc):
        self.nc = nc
        self.ops = []
        self.last_w = {}
        self.readers = {}

    def op(self, eng, fn, reads=(), writes=(), dsem=None):
        o = Op(eng, fn, dsem)
        deps = set()
        for k in reads:
            w = self.last_w.get(k)
            if w is not None:
                deps.add(w)
        for k in writes:
            w = self.last_w.get(k)
            if w is not None:
                deps.add(w)
            r = self.readers.get(k)
            if r:
                for x in r[0].values():
                    deps.add(x)
                for x in r[1]:
                    deps.add(x)
        for k in reads:
            r = self.readers.get(k)
            if r is None:
                r = self.readers[k] = ({}, [])
            if dsem is None:
                r[0][eng] = o
            else:
                r[1].append(o)
        for k in writes:
            self.last_w[k] = o
            self.readers[k] = ({}, [])
        deps.discard(o)
        for d in deps:
            if d.eng == "pe" and eng == "pe" and d.dsem is None and dsem is None:
                continue
            o.deps.append(d)
            d.signal = True
        self.ops.append(o)
        return o

    def emit(self, final_wait_ops=()):
        nc = self.nc
        for o in final_wait_ops:
            o.signal = True
        for o in self.ops:
            if o.dsem is not None:
                o.signal = True
        print("sched ops:", len(self.ops), {e: sum(1 for o in self.ops if o.eng == e) for e in self.ENGS})
        cnt = {}
        for o in self.ops:
            if not o.signal:
                continue
            key = ("d", o.dsem) if o.dsem is not None else ("e", o.eng)
            n = cnt.get(key, 0) + 1
            cnt[key] = n
            o.sig = (key, n)
        sems = {}
        with contextlib.ExitStack() as stack:
            def getsem(key, seg):
                k = (key, seg)
                if k not in sems:
                    nm = "s%d" % len(sems)
                    sems[k] = stack.enter_context(nc.semaphore(nm))
                return sems[k]

            for key, n in cnt.items():
                for seg in range((n - 1) // SEG + 1):
                    getsem(key, seg)

            def sem_and_val(sig):
                key, n = sig
                if key[0] == "d" and key[1].startswith("all:"):
                    n = cnt[key]
                seg = (n - 1) // SEG
                v = n - seg * SEG
                if key[0] == "d":
                    v *= 16
                return getsem(key, seg), v, (key, seg)

            by_eng = {e: [] for e in self.ENGS}
            for o in self.ops:
                by_eng[o.eng].append(o)
            block = stack.enter_context(nc.Block())

            def make_section(eng_name, final=False):
                def section(eng):
                    waited = {}
                    for o in by_eng[eng_name]:
                        for d in o.deps:
                            s, v, sk = sem_and_val(d.sig)
                            if waited.get(sk, 0) >= v:
                                continue
                            waited[sk] = v
                            eng.wait_ge(s, v)
                        ins = o.fn(eng)
                        if o.signal:
                            key, n = o.sig
                            seg = (n - 1) // SEG
                            ins.then_inc(getsem(key, seg), 16 if o.dsem is not None else 1)
                    if final:
                        for o in final_wait_ops:
                            s, v, sk = sem_and_val(o.sig)
                            if waited.get(sk, 0) >= v:
                                continue
                            waited[sk] = v
                            eng.wait_ge(s, v)
                return section

            block.sync(make_section("sp", final=True))
            block.scalar(make_section("act"))
            block.vector(make_section("dve"))
            block.gpsimd(make_section("pool"))
            block.tensor(make_section("pe"))


def build_program(layers):
    nc = bass.Bass("TRN2", target_bir_lowering=False)
    L = DEPTH

    def dram(name, shape, dt=F32, kind="ExternalInput"):
        return nc.dram_tensor(name, shape, dt, kind=kind).ap()

    d_xT = dram("xT", [NB, 8, 128, SEQ])
    d_out = dram("oT", [NB, 8, 128, SEQ], kind="ExternalOutput")
    d_cT = dram("cT", [128, 8 * NB])
    d_pos = dram("pos", [NB, 128, NT], I32)
    d_invf = dram("invf", [1, 8])
    d_pw = dram("pw", [1, NIT + 1])
    d_wada = dram("wada", [L, NADA, 128, 4096])
    d_badaT = dram("badaT", [L, 128, 48])
    d_gT = dram("gT", [L, 128, 16])
    d_wst = dram("wst", [L, NWB, 128, 4096])
    d_brow = dram("brow", [L, 128, 3072])
    d_bgT = dram("bgT", [L, 128, 16])
    d_gains = dram("gains", [L, 1, 256])
    d_biasT = dram("biasT", [L, 128, 5120])
    d_dbg = dram("dbg", [128, 16384], kind="ExternalOutput") if DBG_STAGE < 99 else None
    d_dbgb = dram("dbgb", [128, 16384], BF16, kind="ExternalOutput") if DBG_STAGE < 99 else None

    with contextlib.ExitStack() as st:
        def sb(name, shape, dt):
            return st.enter_context(nc.sbuf_tensor(name, shape, dt))

        pb = [st.enter_context(nc.psum_tensor("pb%d" % i, [128, 512], F32)) for i in range(8)]
        pbk = ["pb%d" % i for i in range(8)]

        xT = sb("xT_sb", [128, 8, SEQ], F32)
        KAT = sb("KAT", [128, SEQ], BF16)
        KIT = sb("KIT", [128, SEQ], BF16)
        VA = sb("VA", [128, NT, 65], BF16)
        KBT = sb("KBT", [128, 4, KSLOTS, 128], BF16)
        VB = sb("VB", [128, KSLOTS, 8, 65], BF16)
        wsl = sb("wsl", [128, NSLOT, 4096], BF16)
        hT = sb("hT", [128, 8, 512], BF16)
        rstd = sb("rstd", [128, 512], F32)
        tmpA = sb("tmpA", [128, 2, 512], F32)
        sqc = sb("sqc", [128, 2, 512], BF16)
        attnT = sb("attnT", [128, 2, 4, 512], BF16)
        brow = sb("brow_sb", [128, 3072], BF16)
        biasT = sb("biasT_sb", [128, 5, 2, 512], BF16)
        big = sb("big", [128, 43008], I8)
        ident = sb("ident", [128, 128], BF16)
        I4 = sb("I4", [128, 512], BF16)
        ones_s = sb("ones_s", [128, 128], BF16)
        ones_row = sb("ones_row", [128, 128], BF16)
        cTf = sb("cTf", [128, 8 * NB], F32)
        cact = sb("cact", [128, 8 * NB], BF16)
        modT = sb("modT", [128, L, 48, NB], F32)
        badaT = sb("badaT_sb", [128, L, 48], F32)
        gT = sb("gT_sb", [128, L, 16], F32)
        coef = sb("coef", [128, 2, 6, 8], F32)
        bgT = sb("bgT_sb", [128, 2, 16], F32)
        gains = sb("gains_sb", [128, 2, 256], F32)
        invf = sb("invf_sb", [128, 8], F32)
        pw = sb("pw_sb", [128, NIT + 1], F32)
        posi = sb("posi", [128, NT], I32)
        posf = sb("posf", [128, NT], F32)
        ang = sb("ang", [128, NT * 8], F32)
        rr = sb("rr", [128, 2, NT * 8], F32)
        rki = sb("rki", [128, NT * 8], I32)
        rkf = sb("rkf", [128, NT * 8], F32)
        cosT = sb("cosT", [128, NT, 8], F32)
        sinT = sb("sinT", [128, NT, 8], F32)
        ss = sb("ss", [128, 4, 8], F32)
        rs = sb("rs", [128, 4, 8], F32)
        rp = sb("rp", [128, 4, 80], F32)
        wsc = sb("wsc", [128, 4, 8], F32)
        sqj = sb("sqj", [128, 64], BF16)
        bs = sb("bs", [128, 8], F32)
        rw = sb("rw", [128, NIT + 1], F32)
        tauc = sb("tauc", [128, 1], F32)
        rc = sb("rc", [128, 4, 4], F32)

        def carve(off, nbytes, dt):
            return big[:, off:off + nbytes].bitcast(dt)

        ztok = [carve(0 + i * 2048, 2048, F32) for i in range(4)]
        zb = [carve(8192 + i * 1024, 1024, BF16) for i in range(2)]
        QT = carve(10240, 12288, BF16).rearrange("p (b t) -> p b t", t=512)
        score = carve(22528, 8192, F32)
        negm = carve(30720, 4096, BF16)
        rl = [carve(34816 + i * 1024, 1024, BF16) for i in range(2)]
        PT = [carve(36864 + i * 1024, 1024, BF16) for i in range(4)]
        atok = [carve(40960 + i * 1024, 1024, BF16) for i in range(2)]
        gates = carve(0, 16384, BF16).rearrange("p (c t) -> p c t", t=512)
        merged = carve(16384, 8192, BF16).rearrange("p (c t) -> p c t", t=512)
        hidden = carve(0, 22528, BF16).rearrange("p (c t) -> p c t", t=512)

        def gk(off, nbytes):
            return [("big", i) for i in range(off // 1024, (off + nbytes + 1023) // 1024)]

        K_ztok = [gk(0 + i * 2048, 2048) for i in range(4)]
        K_zb = [gk(8192 + i * 1024, 1024) for i in range(2)]
        K_QT = lambda blk, tl: gk(10240 + blk * 1024 + tl * 256, 256)
        K_QT_blk = lambda b0, b1, tl: sum([K_QT(bb, tl) for bb in range(b0, b1)], [])
        K_score = gk(22528, 8192)
        K_negm = gk(30720, 4096)
        K_rl = [gk(34816 + i * 1024, 1024) for i in range(2)]
        K_PT = [gk(36864 + i * 1024, 1024) for i in range(4)]
        K_atok = [gk(40960 + i * 1024, 1024) for i in range(2)]
        K_gates = lambda c: gk(c * 1024, 1024)
        K_merged = lambda c: gk(16384 + c * 1024, 1024)
        K_hidden = lambda c: gk(c * 1024, 1024)

        S = Sched(nc)
        dbg_ops = []

        def dump(ap, col0, ncols, keys):
            if d_dbg is None:
                return
            dst = d_dbgb if ap.dtype == BF16 else d_dbg
            o = S.op("sp", lambda e: e.dma_start(out=dst[:, col0:col0 + ncols], in_=ap), reads=keys, dsem="dbg%d" % len(dbg_ops))
            dbg_ops.append(o)

        blocks = []
        for l in range(L):
            if l in layers:
                for i in range(NADA):
                    blocks.append(d_wada[l, i])
        for b in range(NB):
            for l in layers:
                for g in range(NG):
                    for i in range(NWB):
                        blocks.append(d_wst[l, i])
        wstate = {"cur": 0, "issued": 0}

        def wnext(keep=0):
            i = wstate["cur"]
            wstate["cur"] += 1
            while wstate["issued"] < min(len(blocks), i - keep + NSLOT):
                j = wstate["issued"]
                sl = j % NSLOT
                S.op("pool", lambda e, j=j, sl=sl: e.dma_start(out=wsl[:, sl, :], in_=blocks[j]),
                     writes=[("w", sl)], dsem="w%d" % sl)
                wstate["issued"] += 1
            return i % NSLOT

        gpstate = {"i": 0}

        def gp():
            gpstate["i"] ^= 1
            return gpstate["i"]

        def mm(out, lhsT, rhs, start, stop, reads, bank, skip=False):
            if skip:
                S.op("pe", lambda e: e.matmul(out, lhsT, rhs, start=start, stop=stop, skip_group_check=True),
                     reads=reads, writes=[pbk[bank]])
            else:
                S.op("pe", lambda e: e.matmul(out, lhsT, rhs, start=start, stop=stop),
                     reads=reads, writes=[pbk[bank]])

        S.op("pool", lambda e: e.memset(ident[:], 1.0), writes=["ident"])
        S.op("pool", lambda e: e.affine_select(out=ident[:], in_=ident[:], pattern=[[-1, 128]],
                                               compare_op=ALU.is_equal, fill=0.0, base=0, channel_multiplier=1),
             reads=["ident"], writes=["ident"])
        for i in range(4):
            S.op("pool", lambda e, i=i: e.tensor_copy(I4[:, i * 128:(i + 1) * 128], ident[:]),
                 reads=["ident"], writes=[("I4", i)])
        K_I4 = [("I4", i) for i in range(4)]
        S.op("pool", lambda e: e.memset(ones_s[:], 1.0 / 1024.0), writes=["ones_s"])
        S.op("pool", lambda e: e.memset(ones_row[:], 1.0), writes=["ones_row"])
        S.op("pool", lambda e: e.memset(VA[:, :, 64:65], 1.0), writes=["VA1"])
        S.op("pool", lambda e: e.memset(VB[:, :, :, 64:65], 1.0), writes=["VB1"])
        S.op("pool", lambda e: e.memset(tauc[:], -1e29), writes=["tauc"])
        pre = "all:pre"
        S.op("sp", lambda e: e.dma_start(out=cTf[:], in_=d_cT), writes=["cTf"], dsem=pre)
        S.op("sp", lambda e: e.dma_start(out=invf[:], in_=d_invf.partition_broadcast(128)), writes=["invf"], dsem=pre)
        S.op("sp", lambda e: e.dma_start(out=pw[:], in_=d_pw.partition_broadcast(128)), writes=["pw"], dsem=pre)
        for l in range(L):
            S.op("sp", lambda e, l=l: e.dma_start(out=badaT[:, l, :], in_=d_badaT[l]), writes=[("badaT", l)], dsem=pre)
            S.op("sp", lambda e, l=l: e.dma_start(out=gT[:, l, :], in_=d_gT[l]), writes=[("gT", l)], dsem=pre)
        S.op("act", lambda e: e.activation(out=cact[:], in_=cTf[:], func=AF.Silu), reads=["cTf"], writes=["cact"])

        for l in range(L):
            if l not in layers:
                continue
            if 'mod' in DBG_SKIP:
                for blk in range(NADA):
                    wnext()
                continue
            for blk in range(NADA):
                sl = wnext()
                for jj in range(4):
                    j = blk * 4 + jj
                    for k in range(8):
                        mm(pb[0][:, j * NB:(j + 1) * NB], wsl[:, sl, k * 512 + jj * 128:k * 512 + (jj + 1) * 128],
                           cact[:, k * NB:(k + 1) * NB], k == 0, k == 7, [("w", sl), "cact"], 0)
            S.op("dve", lambda e, l=l: e.tensor_tensor(
                out=modT[:, l, :, :], in0=pb[0][:, 0:48 * NB].rearrange("p (j b) -> p j b", b=NB),
                in1=badaT[:, l, :].unsqueeze(2).to_broadcast([128, 48, NB]), op=ALU.add),
                reads=[("badaT", l)], writes=[pbk[0], ("modT", l)])

        def norm_mod(b, l, g, ci, which):
            gs = slice(g * 512, (g + 1) * 512)
            bank = gp()
            for c in range(8):
                q = c % 2
                S.op("act", lambda e, c=c, q=q: e.activation(out=sqc[:, q, :], in_=xT[:, c, gs], func=AF.Square),
                     reads=[("xT", c, g)], writes=[("sqc", q)])
                mm(pb[bank][:, :], ones_s[:, :], sqc[:, q, :], c == 0, c == 7, [("sqc", q), "ones_s"], bank)
            S.op("act", lambda e: e.activation(out=rstd[:], in_=pb[bank][:, :], func=AF.Sqrt, bias=epsT[:, 0:1]),
                 reads=["epsT"], writes=[pbk[bank], "rstd"])
            S.op("dve", lambda e: e.reciprocal(out=rstd[:], in_=rstd[:]), reads=["rstd"], writes=["rstd"])
            a0 = 3 * which
            for c in range(8):
                q = c % 2
                S.op("dve", lambda e, c=c, q=q: e.scalar_tensor_tensor(
                    out=tmpA[:, q, :], in0=xT[:, c, gs], scalar=coef[:, ci, a0, c:c + 1], in1=rstd[:],
                    op0=ALU.mult, op1=ALU.mult),
                    reads=[("xT", c, g), "rstd", ("coef", ci)], writes=[("tmpA", q)])
                S.op("act", lambda e, c=c, q=q: e.activation(out=hT[:, c, :], in_=tmpA[:, q, :], func=AF.Identity,
                                                             bias=coef[:, ci, a0 + 1, c:c + 1]),
                     reads=[("tmpA", q), ("coef", ci)], writes=[("hT", c)])

        epsT = sb("epsT", [128, 2], F32)
        S.op("pool", lambda e: e.memset(epsT[:], EPS), writes=["epsT"])

        def rope(z, nh, t, zkeys):
            zv = z[:, 0:nh * 64].rearrange("p (h d) -> p h d", d=64)
            x1 = zv[:, :, 0:8]
            x2 = zv[:, :, 8:16]
            cb = cosT[:, t, :].unsqueeze(1).to_broadcast([128, nh, 8])
            sn = sinT[:, t, :].unsqueeze(1).to_broadcast([128, nh, 8])
            tv = [rp[:, i, 0:nh * 8].rearrange("p (h d) -> p h d", d=8) for i in range(4)]
            S.op("dve", lambda e: e.tensor_tensor(out=tv[0], in0=x1, in1=cb, op=ALU.mult), reads=zkeys + ["rope_tab"], writes=[("rp", 0)])
            S.op("dve", lambda e: e.tensor_tensor(out=tv[1], in0=x2, in1=sn, op=ALU.mult), reads=zkeys + ["rope_tab"], writes=[("rp", 1)])
            S.op("dve", lambda e: e.tensor_tensor(out=tv[2], in0=x2, in1=cb, op=ALU.mult), reads=zkeys + ["rope_tab"], writes=[("rp", 2)])
            S.op("dve", lambda e: e.tensor_tensor(out=tv[3], in0=x1, in1=sn, op=ALU.mult), reads=zkeys + ["rope_tab"], writes=[("rp", 3)])
            S.op("dve", lambda e: e.tensor_tensor(out=x1, in0=tv[0], in1=tv[1], op=ALU.subtract),
                 reads=[("rp", 0), ("rp", 1)], writes=zkeys)
            S.op("dve", lambda e: e.tensor_tensor(out=x2, in0=tv[2], in1=tv[3], op=ALU.add),
                 reads=[("rp", 2), ("rp", 3)], writes=zkeys)

        zrot = {"i": 0}

        def z_stage1(blk, sl, tl, t, li):
            ncols = 512 if blk < 5 else 328
            bank = 2 + (zrot["i"] % 6)
            zrot["i"] += 1
            for k in range(8):
                mm(pb[bank][:, 0:ncols], hT[:, k, tl * 128:(tl + 1) * 128], wsl[:, sl, k * 512:k * 512 + ncols],
                   k == 0, False, [("hT", k), ("w", sl)], bank)
            mm(pb[bank][:, 0:ncols], ones_row[:, :], brow[:, blk * 512:blk * 512 + ncols], False, True,
               ["ones_row", "brow"], bank)
            return bank

        def z_stage2(blk, tl, t, li, bank):
            ncols = 512 if blk < 5 else 328
            zi = (blk * 4 + tl) % 4
            zq = (blk * 4 + tl) % 2
            z = ztok[zi]
            zk = K_ztok[zi]
            nh_norm = {0: 8, 1: 8, 2: 8, 5: 2}.get(blk, 0)
            gidx = {0: 0, 1: 2, 2: 3, 5: 1}.get(blk, 0)
            for h in range(nh_norm):
                S.op("act", lambda e, h=h: e.activation(out=sqj[:, :], in_=pb[bank][:, h * 64:(h + 1) * 64],
                                                        func=AF.Square, accum_out=ss[:, zi, h:h + 1]),
                     writes=[pbk[bank], ("ss", zi, h)])
            S.op("act", lambda e: e.activation(out=z[:, 0:ncols], in_=pb[bank][:, 0:ncols], func=AF.Copy),
                 writes=[pbk[bank]] + zk)
            if nh_norm:
                w = nh_norm * 64
                S.op("act", lambda e: e.activation(out=rs[:, zi, 0:nh_norm], in_=ss[:, zi, 0:nh_norm], func=AF.Sqrt,
                                                   scale=1.0 / 64.0, bias=epsT[:, 0:1]),
                     reads=[("ss", zi, h) for h in range(nh_norm)] + ["epsT"], writes=[("rs", zi)])
                S.op("dve", lambda e: e.reciprocal(out=rs[:, zi, 0:nh_norm], in_=rs[:, zi, 0:nh_norm]),
                     reads=[("rs", zi)], writes=[("rs", zi)])
                zv = z[:, 0:w].rearrange("p (h d) -> p h d", d=64)
                S.op("dve", lambda e: e.tensor_tensor(out=zv, in0=zv,
                                                      in1=rs[:, zi, 0:nh_norm].unsqueeze(2).to_broadcast([128, nh_norm, 64]),
                                                      op=ALU.mult),
                     reads=zk + [("rs", zi)], writes=zk)
                S.op("dve", lambda e: e.tensor_tensor(out=zv, in0=zv,
                                                      in1=gains[:, li, gidx * 64:(gidx + 1) * 64].unsqueeze(1).to_broadcast([128, nh_norm, 64]),
                                                      op=ALU.mult),
                     reads=zk + [("gains", li)], writes=zk)
            if blk in (0, 3):
                rope(z, 8, t, zk)
            if blk == 5:
                rope(z, 4, t, zk)
            if blk in (0, 1, 2, 3):
                S.op("pool", lambda e: e.tensor_copy(zb[zq][:, :], z[:, :]), reads=zk, writes=K_zb[zq])
            if blk == 4:
                slot = t % KSLOTS
                S.op("pool", lambda e: e.tensor_copy(VB[:, slot, :, 0:64], z[:, :].rearrange("p (h d) -> p h d", d=64)),
                     reads=zk + ["VB1"], writes=[("VB", slot)])
            if blk == 5:
                S.op("pool", lambda e: e.tensor_copy(zb[zq][:, 0:256], z[:, 0:256]), reads=zk, writes=K_zb[zq])
                S.op("pool", lambda e: e.tensor_copy(VA[:, t, 0:64], z[:, 256:320]), reads=zk + ["VA1"], writes=[("VA", t)])
                S.op("dve", lambda e: e.tensor_scalar(out=wsc[:, tl, :], in0=z[:, 320:328], scalar1=INDEX_SCALE,
                                                      scalar2=None, op0=ALU.mult),
                     reads=zk, writes=[("wsc", tl)])

        def z_stage3(blk, tl, t):
            zq = (blk * 4 + tl) % 2
            if blk in (0, 1, 2, 3):
                tb = gp()
                tbv = pb[tb][:, :].bitcast(BF16)
                for i in range(4):
                    S.op("pe", lambda e, i=i: e.transpose(tbv[:, i * 128:(i + 1) * 128], zb[zq][:, i * 128:(i + 1) * 128], ident[:]),
                         reads=K_zb[zq] + ["ident"], writes=[pbk[tb]])
                src = tbv[:, 0:512].rearrange("p (i q) -> p i q", q=128)
                if blk == 2:
                    slot = t % KSLOTS
                    S.op("dve", lambda e: e.tensor_copy(KBT[:, :, slot, :], src), writes=[pbk[tb], ("KBT", slot)])
                else:
                    b0 = {0: 0, 1: 4, 3: 8}[blk]
                    S.op("dve", lambda e: e.tensor_copy(QT[:, b0:b0 + 4, tl * 128:(tl + 1) * 128], src),
                         writes=[pbk[tb]] + K_QT_blk(b0, b0 + 4, tl))
            if blk == 5:
                tb = gp()
                tbv = pb[tb][:, :].bitcast(BF16)
                for i in range(2):
                    S.op("pe", lambda e, i=i: e.transpose(tbv[:, i * 128:(i + 1) * 128], zb[zq][:, i * 128:(i + 1) * 128], ident[:]),
                         reads=K_zb[zq] + ["ident"], writes=[pbk[tb]])
                S.op("dve", lambda e: e.tensor_copy(KAT[:, t * 128:(t + 1) * 128], tbv[:, 0:128]),
                     writes=[pbk[tb], ("KAT", t)])
                S.op("dve", lambda e: e.tensor_copy(KIT[:, t * 128:(t + 1) * 128], tbv[:, 128:256]),
                     writes=[pbk[tb], ("KIT", t)])

        def zproj_group(g, li):
            items = [(blk, tl) for blk in range(6) for tl in range(4)]
            banks = {}
            sls = {}
            n = len(items)
            for it in range(n + 2):
                if it < n:
                    blk, tl = items[it]
                    if tl == 0:
                        sls[blk] = wnext()
                    banks[it] = z_stage1(blk, sls[blk], tl, g * 4 + tl, li)
                if 0 <= it - 1 < n:
                    blk, tl = items[it - 1]
                    z_stage2(blk, tl, g * 4 + tl, li, banks[it - 1])
                if 0 <= it - 2 < n:
                    blk, tl = items[it - 2]
                    z_stage3(blk, tl, g * 4 + tl)

        negm2 = carve(4096, 4096, BF16)
        junk = carve(0, 4096, BF16)
        K_negm2 = gk(4096, 4096)
        K_junk = gk(0, 4096)
        NEGM = [(negm, K_negm), (negm2, K_negm2)]

        def idx_units(tl, t):
            nkeys = (t + 1) * 128
            units = []
            for c0 in range(0, nkeys, 512):
                w = min(512, nkeys - c0)
                kit_keys = [("KIT", j) for j in range(c0 // 128, (c0 + w) // 128)]
                sc_keys = gk(22528 + c0 * 4, w * 4)
                for h in range(8):
                    def unit(c0=c0, w=w, h=h, kit_keys=kit_keys, sc_keys=sc_keys):
                        half = h % 2
                        ps_ = slice(half * 64, (half + 1) * 64)
                        bank = gp()
                        mm(pb[bank][:, 0:w], QT[ps_, 8 + h // 2, tl * 128:(tl + 1) * 128], KIT[ps_, c0:c0 + w], True, True,
                           K_QT(8 + h // 2, tl) + kit_keys, bank)
                        q = h % 2
                        S.op("act", lambda e: e.activation(out=rl[q][:, 0:w], in_=pb[bank][:, 0:w], func=AF.Relu),
                             writes=[pbk[bank]] + K_rl[q])
                        if h == 0:
                            S.op("dve", lambda e: e.tensor_scalar(
                                out=score[:, c0:c0 + w], in0=rl[q][:, 0:w], scalar1=wsc[:, tl, 0:1], scalar2=None, op0=ALU.mult),
                                reads=K_rl[q] + [("wsc", tl)], writes=sc_keys)
                        else:
                            S.op("dve", lambda e: e.scalar_tensor_tensor(
                                out=score[:, c0:c0 + w], in0=rl[q][:, 0:w], scalar=wsc[:, tl, h:h + 1], in1=score[:, c0:c0 + w],
                                op0=ALU.mult, op1=ALU.add),
                                reads=K_rl[q] + [("wsc", tl)], writes=sc_keys)
                    units.append(unit)
            return units

        def bisect(tl, t):
            nkeys = (t + 1) * 128
            nm, nmk = NEGM[t % 2]
            sk = K_score
            if t >= 2:
                S.op("dve", lambda e: e.tensor_reduce(out=bs[:, 0:1], in_=score[:, 0:nkeys], axis=AX.X, op=ALU.min),
                     reads=sk, writes=["bs0"])
            S.op("dve", lambda e: e.memset(score[0:64, t * 128 + 64:(t + 1) * 128], -1e30), writes=sk)
            if t >= 2:
                S.op("dve", lambda e: e.tensor_reduce(out=bs[:, 1:2], in_=score[:, 0:nkeys], axis=AX.X, op=ALU.max),
                     reads=sk, writes=["bs1"])
                S.op("dve", lambda e: e.tensor_tensor(out=bs[:, 2:3], in0=bs[:, 1:2], in1=bs[:, 0:1], op=ALU.subtract),
                     reads=["bs0", "bs1"], writes=["bs2"])
                S.op("dve", lambda e: e.tensor_scalar(out=rw[:, :], in0=pw[:, :], scalar1=bs[:, 2:3], scalar2=None, op0=ALU.mult),
                     reads=["pw", "bs2"], writes=["rw"])
                S.op("dve", lambda e: e.tensor_tensor(out=bs[:, 3:4], in0=bs[:, 0:1], in1=rw[:, 0:1], op=ALU.add),
                     reads=["bs0", "rw"], writes=["mid"])
                for it in range(NIT):
                    S.op("dve", lambda e: e.tensor_scalar(out=junk[:, 0:nkeys], in0=score[:, 0:nkeys], scalar1=bs[:, 3:4],
                                                          scalar2=None, op0=ALU.is_gt, op1=ALU.add, accum_out=bs[:, 4:5]),
                         reads=sk + ["mid"], writes=K_junk + ["cnt"])
                    S.op("dve", lambda e: e.tensor_scalar(out=bs[:, 5:6], in0=bs[:, 4:5], scalar1=TOPK - 0.5, scalar2=0.5,
                                                          op0=ALU.is_ge, op1=ALU.subtract),
                         reads=["cnt"], writes=["tt"])
                    S.op("dve", lambda e, it=it: e.scalar_tensor_tensor(out=bs[:, 3:4], in0=bs[:, 5:6], scalar=rw[:, it:it + 1],
                                                                        in1=bs[:, 3:4], op0=ALU.mult, op1=ALU.add),
                         reads=["tt", "rw", "mid"], writes=["mid"])
                S.op("dve", lambda e: e.tensor_tensor(out=bs[:, 6:7], in0=bs[:, 3:4], in1=rw[:, NIT:NIT + 1], op=ALU.subtract),
                     reads=["mid", "rw"], writes=["tau"])
                tau = bs[:, 6:7]
                tk = ["tau"]
            else:
                tau = tauc[:, 0:1]
                tk = ["tauc"]
            S.op("dve", lambda e: e.tensor_scalar(out=nm[:, 0:nkeys], in0=score[:, 0:nkeys], scalar1=tau, scalar2=NEG,
                                                  op0=ALU.is_le, op1=ALU.mult),
                 reads=sk + tk, writes=nmk)

        ptc = {"i": 0}

        def attn_finish(br, tl, obase):
            av = atok[br][:, :].rearrange("p (i two d) -> p i two d", two=2, d=64)
            for half in range(2):
                ob = obase + half
                ov = pb[ob][:, 0:260].rearrange("p (i d) -> p i d", d=65)
                S.op("dve", lambda e, ov=ov, half=half: e.reciprocal(out=rc[:, br * 2 + half, :], in_=ov[:, :, 64]),
                     writes=[pbk[ob], ("rc", br, half)])
                S.op("dve", lambda e, ov=ov, half=half: e.tensor_tensor(
                    out=av[:, :, half, :], in0=ov[:, :, 0:64],
                    in1=rc[:, br * 2 + half, :].unsqueeze(2).to_broadcast([128, 4, 64]), op=ALU.mult),
                    reads=[("rc", br, half)], writes=[pbk[ob]] + K_atok[br])
            tb = gp()
            tbv = pb[tb][:, :].bitcast(BF16)
            for i in range(4):
                S.op("pe", lambda e, i=i: e.transpose(tbv[:, i * 128:(i + 1) * 128], atok[br][:, i * 128:(i + 1) * 128], ident[:]),
                     reads=K_atok[br] + ["ident"], writes=[pbk[tb]])
            S.op("act", lambda e: e.activation(out=attnT[:, br, :, tl * 128:(tl + 1) * 128],
                                               in_=tbv[:, 0:512].rearrange("p (i q) -> p i q", q=128), func=AF.Copy),
                 writes=[pbk[tb], ("attnT", br, tl)])

        def attnA_units(tl, t):
            nm, nmk = NEGM[t % 2]
            units = []
            for j in range(t + 1):
                for half in range(2):
                    def unit(j=j, half=half):
                        ps_ = slice(half * 64, (half + 1) * 64)
                        bank = 2 + half
                        mm(pb[bank][:, :], KAT[ps_, j * 128:(j + 1) * 128], QT[ps_, 0:4, tl * 128:(tl + 1) * 128], True, False,
                           [("KAT", j)] + K_QT_blk(0, 4, tl), bank)
                        mm(pb[bank][:, :], nm[:, j * 128:(j + 1) * 128], I4[:, :], False, True, nmk + K_I4, bank)
                        pi = ptc["i"] % 4
                        ptc["i"] += 1
                        S.op("act", lambda e: e.activation(out=PT[pi][:, :], in_=pb[bank][:, :], func=AF.Exp),
                             writes=[pbk[bank]] + K_PT[pi])
                        for i in range(4):
                            mm(pb[4 + half][:, i * 65:(i + 1) * 65], PT[pi][:, i * 128:(i + 1) * 128], VA[:, j, :],
                               (j == 0 and i == 0), j == t, K_PT[pi] + [("VA", j), "VA1"], 4 + half, skip=True)
                    units.append(unit)
            return units

        def attnB_units(tl, t):
            j0 = max(0, t - 4)
            units = []
            for j in range(j0, t + 1):
                for half in range(2):
                    def unit(j=j, half=half):
                        jrel = j - (t - 4)
                        slot = j % KSLOTS
                        ps_ = slice(half * 64, (half + 1) * 64)
                        bank = 2 + half
                        for i in range(4):
                            mm(pb[bank][:, i * 128:(i + 1) * 128], KBT[ps_, i, slot, :], QT[ps_, 4 + i, tl * 128:(tl + 1) * 128],
                               i == 0, False, [("KBT", slot)] + K_QT(4 + i, tl), bank, skip=True)
                        mm(pb[bank][:, :], ident[:, :], biasT[:, jrel, half, :], False, True, ["ident", "biasT"], bank, skip=True)
                        pi = ptc["i"] % 4
                        ptc["i"] += 1
                        S.op("act", lambda e: e.activation(out=PT[pi][:, :], in_=pb[bank][:, :], func=AF.Exp),
                             writes=[pbk[bank]] + K_PT[pi])
                        for i in range(4):
                            mm(pb[6 + half][:, i * 65:(i + 1) * 65], PT[pi][:, i * 128:(i + 1) * 128], VB[:, slot, 2 * i + half, :],
                               (j == j0 and i == 0), j == t, K_PT[pi] + [("VB", slot), "VB1"], 6 + half, skip=True)
                    units.append(unit)
            return units

        def interleave(ua, ub):
            na, nb_ = len(ua), len(ub)
            ia = ib = 0
            while ia < na or ib < nb_:
                if ib >= nb_ or (ia < na and ia * nb_ <= ib * na):
                    ua[ia]()
                    ia += 1
                else:
                    ub[ib]()
                    ib += 1

        def attention_group(g):
            t0 = g * 4
            for u in idx_units(0, t0):
                u()
            for u in attnB_units(0, t0):
                u()
            bisect(0, t0)
            attn_finish(1, 0, 6)
            for k in range(4):
                t = t0 + k
                if k < 3:
                    interleave(idx_units(k + 1, t + 1), attnA_units(k, t))
                    for u in attnB_units(k + 1, t + 1):
                        u()
                    bisect(k + 1, t + 1)
                    attn_finish(0, k, 4)
                    attn_finish(1, k + 1, 6)
                else:
                    for u in attnA_units(k, t):
                        u()
                    attn_finish(0, k, 4)

        def mixer_group(b, l, g, ci, li):
            gs = slice(g * 512, (g + 1) * 512)
            if DBG_STAGE < 1:
                return
            norm_mod(b, l, g, ci, 0)
            if DBG_STAGE < 6:
                dump(hT[:, :, :].rearrange("p c t -> p (c t)"), 0, 4096, [("hT", c) for c in range(8)])
            if DBG_STAGE < 2:
                return
            zproj_group(g, li)
            if DBG_STAGE < 6:
              dump(QT[:, :, :].rearrange("p c t -> p (c t)"), 4096, 6144, K_QT_blk(0, 12, 0) + K_QT_blk(0, 12, 1) + K_QT_blk(0, 12, 2) + K_QT_blk(0, 12, 3))
            if DBG_STAGE < 6:
              dump(KAT[:, 0:512], 10240, 512, [("KAT", j) for j in range(4)])
            dump(KIT[:, 0:512], 10752, 512, [("KIT", j) for j in range(4)])
            if DBG_STAGE < 3:
                return
            attention_group(g)
            dump(attnT[:, :, :, :].rearrange("p a c t -> p (a c t)"), 12288, 4096, [("attnT", br, x) for br in range(2) for x in range(4)])
            if DBG_STAGE < 6:
                return
            for blk in range(4):
                sl = wnext()
                for jj in range(4):
                    c = blk * 4 + jj
                    bank = gp()
                    for k in range(8):
                        mm(pb[bank][:, :], wsl[:, sl, k * 512 + jj * 128:k * 512 + (jj + 1) * 128], hT[:, k, :], k == 0, k == 7,
                           [("w", sl), ("hT", k)], bank)
                    S.op("act", lambda e, c=c, bank=bank: e.activation(out=gates[:, c, :], in_=pb[bank][:, :], func=AF.Sigmoid,
                                                                       bias=bgT[:, li, c:c + 1]),
                         reads=[("bgT", li)], writes=[pbk[bank]] + K_gates(c))
            sla = wnext()
            slb = wnext(keep=1)
            for c in range(8):
                for br, slx in ((0, sla), (1, slb)):
                    bank = gp()
                    for k in range(4):
                        mm(pb[bank][:, :], wsl[:, slx, k * 1024 + c * 128:k * 1024 + (c + 1) * 128], attnT[:, br, k, :], k == 0, k == 3,
                           [("w", slx)] + [("attnT", br, x) for x in range(4)], bank)
                    S.op("dve", lambda e, c=c, br=br, bank=bank: e.tensor_tensor(out=tmpA[:, br, :], in0=pb[bank][:, :],
                                                                                 in1=gates[:, br * 8 + c, :], op=ALU.mult),
                         reads=K_gates(br * 8 + c), writes=[pbk[bank], ("tmpA", br)])
                S.op("pool", lambda e, c=c: e.tensor_tensor(out=merged[:, c, :], in0=tmpA[:, 0, :], in1=tmpA[:, 1, :], op=ALU.add),
                     reads=[("tmpA", 0), ("tmpA", 1)], writes=K_merged(c))
            for i in range(2):
                sl = wnext()
                for cc in range(4):
                    c = i * 4 + cc
                    bank = gp()
                    for k in range(8):
                        mm(pb[bank][:, :], wsl[:, sl, k * 512 + cc * 128:k * 512 + (cc + 1) * 128], merged[:, k, :], k == 0, k == 7,
                           [("w", sl)] + K_merged(k), bank)
                    S.op("dve", lambda e, c=c, bank=bank: e.scalar_tensor_tensor(
                        out=xT[:, c, gs], in0=pb[bank][:, :], scalar=coef[:, ci, 2, c:c + 1], in1=xT[:, c, gs],
                        op0=ALU.mult, op1=ALU.add),
                        reads=[("coef", ci)], writes=[pbk[bank], ("xT", c, g)])
            if 'late' in DBG_SKIP:
                late_dumps.append((gs, g))
            elif DBG_STAGE >= 6:
                if 'mdump' not in DBG_SKIP:
                    dump(merged[:, :, :].rearrange("p c t -> p (c t)"), 0, 4096, sum([K_merged(c) for c in range(8)], []))
                if 'gdump' not in DBG_SKIP:
                    dump(gates[:, 0:8, :].rearrange("p c t -> p (c t)"), 8192, 4096, sum([K_gates(c) for c in range(8)], []))
                for c in range(8):
                    if 'xdump' not in DBG_SKIP:
                        dump(xT[:, c, gs], 4096 + c * 512, 512, [("xT", c, g)])

        late_dumps = []

        def ffn_group(b, l, g, ci):
            gs = slice(g * 512, (g + 1) * 512)
            norm_mod(b, l, g, ci, 1)
            if late_dumps:
                late_dumps.pop()
                for c in range(8):
                    dump(xT[:, c, gs], 4096 + c * 512, 512, [("xT", c, g)])
            pr_i = 0
            for i in range(11):
                sl = wnext()
                for pr in range(2):
                    j = 2 * i + pr
                    bg = 2 * (pr_i % 4)
                    bu = bg + 1
                    pr_i += 1
                    for k in range(8):
                        mm(pb[bg][:, :], wsl[:, sl, k * 512 + (2 * pr) * 128:k * 512 + (2 * pr + 1) * 128], hT[:, k, :], k == 0, k == 7,
                           [("w", sl), ("hT", k)], bg)
                    for k in range(8):
                        mm(pb[bu][:, :], wsl[:, sl, k * 512 + (2 * pr + 1) * 128:k * 512 + (2 * pr + 2) * 128], hT[:, k, :], k == 0, k == 7,
                           [("w", sl), ("hT", k)], bu)
                    q = j % 2
                    S.op("act", lambda e, bg=bg, q=q: e.activation(out=sqc[:, q, :], in_=pb[bg][:, :], func=AF.Silu),
                         writes=[pbk[bg], ("sqc", q)])
                    S.op("dve", lambda e, bu=bu, q=q, j=j: e.tensor_tensor(out=hidden[:, j, :], in0=pb[bu][:, :], in1=sqc[:, q, :], op=ALU.mult),
                         reads=[("sqc", q)], writes=[pbk[bu]] + K_hidden(j))
            for hf in range(2):
                base = 4 if hf == 0 else 0
                for kb in range(3):
                    sl = wnext()
                    nk = 8 if kb < 2 else 6
                    for kk in range(nk):
                        kt = kb * 8 + kk
                        for cc in range(4):
                            mm(pb[base + cc][:, :], wsl[:, sl, kk * 512 + cc * 128:kk * 512 + (cc + 1) * 128], hidden[:, kt, :],
                               kt == 0, kt == 21, [("w", sl)] + K_hidden(kt), base + cc)
                for cc in range(4):
                    c = hf * 4 + cc
                    S.op("dve", lambda e, c=c, bank=base + cc: e.scalar_tensor_tensor(
                        out=xT[:, c, gs], in0=pb[bank][:, :], scalar=coef[:, ci, 5, c:c + 1], in1=xT[:, c, gs],
                        op0=ALU.mult, op1=ALU.add),
                        reads=[("coef", ci)], writes=[pbk[base + cc], ("xT", c, g)])

        out_ops = []
        inst = 0
        for b in range(NB):
            xk_all = [("xT", c, g) for c in range(8) for g in range(NG)]
            for c in range(8):
                S.op("sp", lambda e, b=b, c=c: e.dma_start(out=xT[:, c, :], in_=d_xT[b, c]),
                     writes=[("xT", c, g) for g in range(NG)], dsem="all:x%d" % b)
            S.op("sp", lambda e, b=b: e.dma_start(out=posi[:], in_=d_pos[b]), writes=["posi"], dsem="all:pos%d" % b)
            S.op("dve", lambda e: e.tensor_copy(posf[:], posi[:]), reads=["posi"], writes=["posf"])
            angv = ang[:, :].rearrange("p (t f) -> p t f", f=8)
            S.op("dve", lambda e: e.tensor_tensor(out=angv, in0=posf[:, :].unsqueeze(2).to_broadcast([128, NT, 8]),
                                                  in1=invf[:, :].unsqueeze(1).to_broadcast([128, NT, 8]), op=ALU.mult),
                 reads=["posf", "invf"], writes=["ang"])
            for which, dst in ((0, sinT), (1, cosT)):
                if 'rope' in DBG_SKIP:
                    continue
                r_ = rr[:, which, :]
                wk = ("rr", which)
                S.op("dve", lambda e, which=which, r_=r_: e.tensor_scalar(out=r_, in0=ang[:, :], scalar1=1.0 / (2 * np.pi),
                                                                         scalar2=0.25 * which, op0=ALU.mult, op1=ALU.add),
                     reads=["ang"], writes=[wk])
                S.op("dve", lambda e, r_=r_: e.tensor_copy(rki[:, :], r_), reads=[wk], writes=["rki"])
                S.op("dve", lambda e: e.tensor_copy(rkf[:, :], rki[:, :]), reads=["rki"], writes=["rkf"])
                S.op("dve", lambda e, r_=r_: e.scalar_tensor_tensor(out=r_, in0=rkf[:, :], scalar=-6.28125, in1=ang[:, :],
                                                                    op0=ALU.mult, op1=ALU.add),
                     reads=["rkf", "ang"], writes=[wk])
                S.op("dve", lambda e, r_=r_: e.scalar_tensor_tensor(out=r_, in0=rkf[:, :], scalar=-(2 * np.pi - 6.28125), in1=r_,
                                                                    op0=ALU.mult, op1=ALU.add),
                     reads=["rkf", wk], writes=[wk])
                S.op("dve", lambda e, r_=r_, which=which: e.tensor_scalar(out=r_, in0=r_, scalar1=(np.pi / 2) * which, scalar2=3.1415925,
                                                                         op0=ALU.add, op1=ALU.min),
                     reads=[wk], writes=[wk])
                S.op("dve", lambda e, r_=r_: e.tensor_scalar(out=r_, in0=r_, scalar1=-3.1415925, scalar2=None, op0=ALU.max),
                     reads=[wk], writes=[wk])
                S.op("act", lambda e, r_=r_, dst=dst: e.activation(out=dst[:, :, :].rearrange("p t f -> p (t f)"), in_=r_, func=AF.Sin),
                     reads=[wk], writes=["rope_tab"])
            for l in layers:
                ci = inst % 2
                li = inst % 2
                inst += 1
                ld = "all:ld%d_%d" % (b, l)
                ldp = "all:ldp%d_%d" % (b, l)
                S.op("sp", lambda e, l=l, li=li: e.dma_start(out=bgT[:, li, :], in_=d_bgT[l]), writes=[("bgT", li)], dsem=ld)
                S.op("sp", lambda e, l=l, li=li: e.dma_start(out=gains[:, li, :], in_=d_gains[l].partition_broadcast(128)),
                     writes=[("gains", li)], dsem=ld)
                if 'brow' not in DBG_SKIP:
                    S.op("pool", lambda e, l=l: e.dma_start(out=brow[:, :], in_=d_brow[l]), writes=["brow"], dsem=ldp)
                S.op("pool", lambda e, l=l: e.dma_start(out=biasT[:, :, :, :].rearrange("p a b c -> p (a b c)"), in_=d_biasT[l]),
                     writes=["biasT"], dsem=ldp)
                S.op("dve", lambda e, li=li: e.tensor_scalar(
                    out=gains[:, li, :].rearrange("p (a b d) -> p a b d", b=2, d=64)[:, :, 0, :],
                    in0=gains[:, li, :].rearrange("p (a b d) -> p a b d", b=2, d=64)[:, :, 0, :],
                    scalar1=0.125, scalar2=None, op0=ALU.mult),
                    reads=[("gains", li)], writes=[("gains", li)])
                for which in range(2):
                    if 'coef' in DBG_SKIP:
                        continue
                    o3 = 24 * which
                    S.op("dve", lambda e, l=l, b=b, o3=o3, which=which, ci=ci: e.scalar_tensor_tensor(
                        out=coef[:, ci, 3 * which, :], in0=modT[:, l, o3 + 8:o3 + 16, b], scalar=1.0,
                        in1=gT[:, l, 8 * which:8 * which + 8], op0=ALU.add, op1=ALU.mult),
                        reads=[("modT", l), ("gT", l)], writes=[("coef", ci)])
                    S.op("dve", lambda e, l=l, b=b, o3=o3, which=which, ci=ci: e.tensor_copy(coef[:, ci, 3 * which + 1, :], modT[:, l, o3:o3 + 8, b]),
                         reads=[("modT", l)], writes=[("coef", ci)])
                    S.op("dve", lambda e, l=l, b=b, o3=o3, which=which, ci=ci: e.tensor_copy(coef[:, ci, 3 * which + 2, :], modT[:, l, o3 + 16:o3 + 24, b]),
                         reads=[("modT", l)], writes=[("coef", ci)])
                for g in range(NG):
                    if DBG_STAGE < 99 and (b > 0 or g > 0):
                        continue
                    mixer_group(b, l, g, ci, li)
                    if DBG_STAGE >= 7:
                        ffn_group(b, l, g, ci)
            for c in range(8):
                o = S.op("sp", lambda e, b=b, c=c: e.dma_start(out=d_out[b, c], in_=xT[:, c, :]),
                         reads=[("xT", c, g) for g in range(NG)], dsem="all:o%d" % b)
                out_ops.append(o)
        assert DBG_STAGE < 99 or wstate["cur"] == len(blocks), (wstate, len(blocks))
        S.emit(final_wait_ops=out_ops + dbg_ops)
    return nc


def _blockify(W, ncols_blk=512):
    K, N = W.shape
    assert N % ncols_blk == 0
    kc = K // 128
    out = np.zeros((N // ncols_blk, 128, 8, ncols_blk), np.float32)
    Wr = W.reshape(kc, 128, N // ncols_blk, ncols_blk)
    out[:, :, :kc, :] = Wr.transpose(2, 1, 0, 3)
    return out.reshape(N // ncols_blk, 128, 8 * ncols_blk)


def _prep_shared(inp):
    L = DEPTH
    offs = np.concatenate([[0], np.cumsum(IN_SIZES)])
    rng_ = lambda i: np.arange(offs[i], offs[i + 1])
    qa, ka, va, qi, ki, wi, qb, kb, vb, ga, gb = [rng_(i) for i in range(11)]
    tok_cols = np.concatenate([qa, qb, kb, qi, vb, ka, ka, ki, ki, va, wi])
    gate_cols = np.concatenate([ga, gb])
    gu_cols = []
    for i in range(11):
        for pr in range(2):
            j = 2 * i + pr
            gu_cols.append(np.arange(j * 128, (j + 1) * 128))
            gu_cols.append(D_FF + np.arange(j * 128, (j + 1) * 128))
    gu_cols = np.concatenate(gu_cols)
    wst = np.zeros((L, NWB, 128, 4096), np.float32)
    wada = np.zeros((L, NADA, 128, 4096), np.float32)
    brow = np.zeros((L, 128, 3072), np.float32)
    bgT = np.zeros((L, 128, 16), np.float32)
    badaT = np.zeros((L, 128, 48), np.float32)
    gT = np.zeros((L, 128, 16), np.float32)
    gains = np.zeros((L, 1, 256), np.float32)
    biasT = np.zeros((L, 128, 5120), np.float32)
    k_ = np.arange(128)[:, None, None]
    j_ = np.arange(5)[None, :, None]
    q_ = np.arange(128)[None, None, :]
    tdiff = q_ - k_ + 128 * (4 - j_)
    ridx = np.clip(tdiff, -63, 256) + 63
    cq = q_ // 64
    ck = 2 * j_ + k_ // 64
    vis = (ck >= cq) & (ck <= 8 + cq)
    for l in range(L):
        w_in = np.asarray(inp["w_in"][l])
        wtok = np.zeros((1024, 3072), np.float32)
        wtok[:, :2888] = w_in[:, tok_cols]
        wst[l, 0:6] = _blockify(wtok)
        wst[l, 6:10] = _blockify(w_in[:, gate_cols])
        woa = np.asarray(inp["w_oa"][l]).reshape(4, 128, 1024).transpose(1, 0, 2).reshape(128, 4096)
        wob = np.asarray(inp["w_ob"][l]).reshape(4, 128, 1024).transpose(1, 0, 2).reshape(128, 4096)
        wst[l, 10] = woa
        wst[l, 11] = wob
        wst[l, 12:14] = _blockify(np.asarray(inp["w_out"][l]))
        wst[l, 14:25] = _blockify(np.asarray(inp["w_gu"][l])[:, gu_cols])
        wd = np.zeros((3072, 1024), np.float32)
        wd[:D_FF] = np.asarray(inp["w_down"][l])
        for hf in range(2):
            for kbk in range(3):
                sub = wd[kbk * 1024:(kbk + 1) * 1024, hf * 512:(hf + 1) * 512]
                wst[l, 25 + hf * 3 + kbk] = _blockify(sub)[0]
        wada[l] = _blockify(np.asarray(inp["w_ada"][l]))
        b_in = np.asarray(inp["b_in"][l])
        brow[l, 0, :2888] = b_in[tok_cols]
        bgT[l] = b_in[gate_cols].reshape(16, 128).T
        badaT[l] = np.asarray(inp["b_ada"][l]).reshape(48, 128).T
        gT[l, :, 0:8] = np.asarray(inp["g_mix"][l]).reshape(8, 128).T
        gT[l, :, 8:16] = np.asarray(inp["g_ffn"][l]).reshape(8, 128).T
        gains[l, 0] = np.concatenate([np.asarray(inp[k][l]) for k in ("qn_a", "kn_a", "qn_b", "kn_b")])
        rb = np.asarray(inp["rel_bias"][l])
        g_ = rb[:, ridx]
        g_ = np.where(vis[None], g_, np.float32(NEG)).astype(np.float32)
        g_ = g_.reshape(4, 2, 128, 5, 128)
        biasT[l] = g_.transpose(2, 3, 1, 0, 4).reshape(128, 5120)
    invf = (np.float32(ROPE_THETA) ** (-np.arange(0, 16, 2, dtype=np.float32) / np.float32(16))).astype(np.float32).reshape(1, 8)
    pw = (0.5 ** np.arange(1, NIT + 2)).astype(np.float32).reshape(1, NIT + 1)
    return dict(wst=wst, wada=wada, brow=brow, bgT=bgT, badaT=badaT, gT=gT, gains=gains, biasT=biasT, invf=invf, pw=pw)


def _prep_core(x_t, c, positions, core):
    bsl = slice(core * NB, (core + 1) * NB)
    xT = np.ascontiguousarray(x_t[bsl].transpose(0, 2, 1)).reshape(NB, 8, 128, SEQ)
    cT = np.ascontiguousarray(np.asarray(c)[bsl].reshape(NB, 8, 128).transpose(2, 1, 0)).reshape(128, 8 * NB)
    pos = np.ascontiguousarray(np.asarray(positions)[bsl].reshape(NB, NT, 128).transpose(0, 2, 1)).astype(np.int32)
    return dict(xT=xT, cT=cT, pos=pos)


_PROG_CACHE = {}


def _get_prog(layers):
    key = tuple(layers)
    if key not in _PROG_CACHE:
        _PROG_CACHE[key] = build_program(list(layers))
    return _PROG_CACHE[key]


def _run(x_cur, c, positions, shared, layers):
    nc = _get_prog(layers)
    in_maps = []
    for core in range(N_CORES):
        m = dict(shared)
        m.update(_prep_core(x_cur, c, positions, core))
        in_maps.append(m)
    res = run_bass_kernel_spmd(nc, in_maps, core_ids=list(range(N_CORES)))
    outs = []
    for core in range(N_CORES):
        oT = np.asarray(res.results[core]["oT"]).reshape(NB, 1024, SEQ)
        outs.append(oT.transpose(0, 2, 1))
    return np.ascontiguousarray(np.concatenate(outs, axis=0)).astype(np.float32)


FUSED = True


def kernel(**inputs):
    inp = {k: np.asarray(v) for k, v in inputs.items()}
    shared = _prep_shared(inp)
    x = inp["x"].astype(np.float32)
    if FUSED:
        return _run(x, inp["c"], inp["positions"], shared, list(range(DEPTH)))
    for l in range(DEPTH):
        x = _run(x, inp["c"], inp["positions"], shared, [l])
    return x
```

```python
import contextlib
import numpy as np
import concourse.bass as bass
import concourse.mybir as mybir
from concourse.bass_utils import run_bass_kernel_spmd

F32 = mybir.dt.float32
BF16 = mybir.dt.bfloat16
I32 = mybir.dt.int32
I8 = mybir.dt.int8
ALU = mybir.AluOpType
AF = mybir.ActivationFunctionType
AX = mybir.AxisListType

D_MODEL = 1024
SEQ = 2048
DEPTH = 2
N_CORES = 8
NB = 2
HD = 64
D_FF = 2816
IN_SIZES = (512, 64, 64, 512, 64, 8, 512, 512, 512, 1024, 1024)
TOPK = 256
EPS = 1e-6
INDEX_SCALE = (64 ** -0.5) * (8 ** -0.5)
ROPE_THETA = 500000.0
NT = SEQ // 128
NG = 4
NWB = 31
NADA = 12
NSLOT = 3
NIT = 17
NEG = -30000.0
KSLOTS = 8

SEG = 30000
DBG_STAGE = 99
A_FRAC = 0.6
import os
DBG_SKIP = set(os.environ.get('DBG_SKIP', '').split(','))


class Op:
    __slots__ = ("eng", "fn", "deps", "dsem", "signal", "sig")

    def __init__(self, eng, fn, dsem=None):
        self.eng = eng
        self.fn = fn
        self.deps = []
        self.dsem = dsem
        self.signal = False
        self.sig = None


class Sched:
    ENGS = ("pe", "act", "dve", "pool", "sp")

    def __init__(self, nc):
        self.nc = nc
        self.ops = []
        self.last_w = {}
        self.readers = {}

    def op(self, eng, fn, reads=(), writes=(), dsem=None):
        o = Op(eng, fn, dsem)
        deps = set()
        for k in reads:
            w = self.last_w.get(k)
            if w is not None:
                deps.add(w)
        for k in writes:
            w = self.last_w.get(k)
            if w is not None:
                deps.add(w)
            r = self.readers.get(k)
            if r:
                for x in r[0].values():
                    deps.add(x)
                for x in r[1]:
                    deps.add(x)
        for k in reads:
            r = self.readers.get(k)
            if r is None:
                r = self.readers[k] = ({}, [])
            if dsem is None:
                r[0][eng] = o
            else:
                r[1].append(o)
        for k in writes:
            self.last_w[k] = o
            self.readers[k] = ({}, [])
        deps.discard(o)
        for d in deps:
            if d.eng == "pe" and eng == "pe" and d.dsem is None and dsem is None:
                continue
            o.deps.append(d)
            d.signal = True
        self.ops.append(o)
        return o

    def emit(self, final_wait_ops=()):
        nc = self.nc
        for o in final_wait_ops:
            o.signal = True
        for o in self.ops:
            if o.dsem is not None:
                o.signal = True
        print("sched ops:", len(self.ops), {e: sum(1 for o in self.ops if o.eng == e) for e in self.ENGS})
        cnt = {}
        for o in self.ops:
            if not o.signal:
                continue
            key = ("d", o.dsem) if o.dsem is not None else ("e", o.eng)
            n = cnt.get(key, 0) + 1
            cnt[key] = n
            o.sig = (key, n)
        sems = {}
        with contextlib.ExitStack() as stack:
            def getsem(key, seg):
                k = (key, seg)
                if k not in sems:
                    nm = "s%d" % len(sems)
                    sems[k] = stack.enter_context(nc.semaphore(nm))
                return sems[k]

            for key, n in cnt.items():
                for seg in range((n - 1) // SEG + 1):
                    getsem(key, seg)

            def sem_and_val(sig):
                key, n = sig
                if key[0] == "d" and key[1].startswith("all:"):
                    n = cnt[key]
                seg = (n - 1) // SEG
                v = n - seg * SEG
                if key[0] == "d":
                    v *= 16
                return getsem(key, seg), v, (key, seg)

            by_eng = {e: [] for e in self.ENGS}
            for o in self.ops:
                by_eng[o.eng].append(o)
            block = stack.enter_context(nc.Block())

            def make_section(eng_name, final=False):
                def section(eng):
                    waited = {}
                    for o in by_eng[eng_name]:
                        for d in o.deps:
                            s, v, sk = sem_and_val(d.sig)
                            if waited.get(sk, 0) >= v:
                                continue
                            waited[sk] = v
                            eng.wait_ge(s, v)
                        ins = o.fn(eng)
                        if o.signal:
                            key, n = o.sig
                            seg = (n - 1) // SEG
                            ins.then_inc(getsem(key, seg), 16 if o.dsem is not None else 1)
                    if final:
                        for o in final_wait_ops:
                            s, v, sk = sem_and_val(o.sig)
                            if waited.get(sk, 0) >= v:
                                continue
                            waited[sk] = v
                            eng.wait_ge(s, v)
                return section

            block.sync(make_section("sp", final=True))
            block.scalar(make_section("act"))
            block.vector(make_section("dve"))
            block.gpsimd(make_section("pool"))
            block.tensor(make_section("pe"))


def build_program(layers):
    nc = bass.Bass("TRN2", target_bir_lowering=False)
    L = DEPTH

    def dram(name, shape, dt=F32, kind="ExternalInput"):
        return nc.dram_tensor(name, shape, dt, kind=kind).ap()

    d_xT = dram("xT", [NB, 8, 128, SEQ])
    d_out = dram("oT", [NB, 8, 128, SEQ], kind="ExternalOutput")
    d_cT = dram("cT", [128, 8 * NB])
    d_pos = dram("pos", [NB, 128, NT], I32)
    d_invf = dram("invf", [1, 8])
    d_pw = dram("pw", [1, NIT + 1])
    d_wada = dram("wada", [L, NADA, 128, 4096])
    d_badaT = dram("badaT", [L, 128, 48])
    d_gT = dram("gT", [L, 128, 16])
    d_wst = dram("wst", [L, NWB, 128, 4096])
    d_brow = dram("brow", [L, 128, 3072])
    d_bgT = dram("bgT", [L, 128, 16])
    d_gains = dram("gains", [L, 1, 256])
    d_biasT = dram("biasT", [L, 128, 5120])
    d_dbg = dram("dbg", [128, 16384], kind="ExternalOutput") if DBG_STAGE < 99 else None
    d_dbgb = dram("dbgb", [128, 16384], BF16, kind="ExternalOutput") if DBG_STAGE < 99 else None

    with contextlib.ExitStack() as st:
        def sb(name, shape, dt):
            return st.enter_context(nc.sbuf_tensor(name, shape, dt))

        pb = [st.enter_context(nc.psum_tensor("pb%d" % i, [128, 512], F32)) for i in range(8)]
        pbk = ["pb%d" % i for i in range(8)]

        xT = sb("xT_sb", [128, 8, SEQ], F32)
        KAT = sb("KAT", [128, SEQ], BF16)
        KIT = sb("KIT", [128, SEQ], BF16)
        VA = sb("VA", [128, NT, 65], BF16)
        KBT = sb("KBT", [128, 4, KSLOTS, 128], BF16)
        VB = sb("VB", [128, KSLOTS, 8, 65], BF16)
        wsl = sb("wsl", [128, NSLOT, 4096], BF16)
        hT = sb("hT", [128, 8, 512], BF16)
        rstd = sb("rstd", [128, 512], F32)
        tmpA = sb("tmpA", [128, 2, 512], F32)
        sqc = sb("sqc", [128, 2, 512], BF16)
        attnT = sb("attnT", [128, 2, 4, 512], BF16)
        brow = sb("brow_sb", [128, 3072], BF16)
        biasT = sb("biasT_sb", [128, 5, 2, 512], BF16)
        big = sb("big", [128, 43008], I8)
        ident = sb("ident", [128, 128], BF16)
        I4 = sb("I4", [128, 512], BF16)
        ones_s = sb("ones_s", [128, 128], BF16)
        ones_row = sb("ones_row", [128, 128], BF16)
        cTf = sb("cTf", [128, 8 * NB], F32)
        cact = sb("cact", [128, 8 * NB], BF16)
        modT = sb("modT", [128, L, 48, NB], F32)
        badaT = sb("badaT_sb", [128, L, 48], F32)
        gT = sb("gT_sb", [128, L, 16], F32)
        coef = sb("coef", [128, 2, 6, 8], F32)
        bgT = sb("bgT_sb", [128, 2, 16], F32)
        gains = sb("gains_sb", [128, 2, 256], F32)
        invf = sb("invf_sb", [128, 8], F32)
        pw = sb("pw_sb", [128, NIT + 1], F32)
        posi = sb("posi", [128, NT], I32)
        posf = sb("posf", [128, NT], F32)
        ang = sb("ang", [128, NT * 8], F32)
        rr = sb("rr", [128, 2, NT * 8], F32)
        rki = sb("rki", [128, NT * 8], I32)
        rkf = sb("rkf", [128, NT * 8], F32)
        cosT = sb("cosT", [128, NT, 8], F32)
        sinT = sb("sinT", [128, NT, 8], F32)
        ss = sb("ss", [128, 4, 8], F32)
        rs = sb("rs", [128, 4, 8], F32)
        rp = sb("rp", [128, 4, 80], F32)
        wsc = sb("wsc", [128, 4, 8], F32)
        sqj = sb("sqj", [128, 64], BF16)
        bs = sb("bs", [128, 16], F32)
        rw = sb("rw", [128, NIT + 1], F32)
        tauc = sb("tauc", [128, 1], F32)
        rc = sb("rc", [128, 4, 4], F32)

        def carve(off, nbytes, dt):
            return big[:, off:off + nbytes].bitcast(dt)

        ztok = [carve(0 + i * 2048, 2048, F32) for i in range(4)]
        zb = [carve(8192 + i * 1024, 1024, BF16) for i in range(2)]
        QT = carve(10240, 12288, BF16).rearrange("p (b t) -> p b t", t=512)
        score = carve(22528, 8192, F32)
        negm = carve(30720, 4096, BF16)
        rl = [carve(34816 + i * 1024, 1024, BF16) for i in range(2)]
        PT = [carve(36864 + i * 1024, 1024, BF16) for i in range(4)]
        atok = [carve(40960 + i * 1024, 1024, BF16) for i in range(2)]
        gates = carve(14336, 16384, BF16).rearrange("p (c t) -> p c t", t=512)
        merged = carve(0, 8192, BF16).rearrange("p (c t) -> p c t", t=512)
        hidden = carve(0, 22528, BF16).rearrange("p (c t) -> p c t", t=512)

        def gk(off, nbytes):
            return [("big", i) for i in range(off // 1024, (off + nbytes + 1023) // 1024)]

        K_ztok = [gk(0 + i * 2048, 2048) for i in range(4)]
        K_zb = [gk(8192 + i * 1024, 1024) for i in range(2)]
        K_QT = lambda blk, tl: gk(10240 + blk * 1024 + tl * 256, 256)
        K_QT_blk = lambda b0, b1, tl: sum([K_QT(bb, tl) for bb in range(b0, b1)], [])
        K_score = gk(22528, 8192)
        K_negm = gk(30720, 4096)
        K_rl = [gk(34816 + i * 1024, 1024) for i in range(2)]
        K_PT = [gk(36864 + i * 1024, 1024) for i in range(4)]
        K_atok = [gk(40960 + i * 1024, 1024) for i in range(2)]
        K_gates = lambda c: gk(14336 + c * 1024, 1024)
        K_merged = lambda c: gk(c * 1024, 1024)
        K_hidden = lambda c: gk(c * 1024, 1024)

        S = Sched(nc)
        dbg_ops = []

        def dump(ap, col0, ncols, keys):
            if d_dbg is None:
                return
            dst = d_dbgb if ap.dtype == BF16 else d_dbg
            o = S.op("sp", lambda e: e.dma_start(out=dst[:, col0:col0 + ncols], in_=ap), reads=keys, dsem="dbg%d" % len(dbg_ops))
            dbg_ops.append(o)

        blocks = []
        for l in range(L):
            if l in layers:
                for i in range(NADA):
                    blocks.append(d_wada[l, i])
        for b in range(NB):
            for l in layers:
                for g in range(NG):
                    for i in range(NWB):
                        blocks.append(d_wst[l, i])
        wstate = {"cur": 0, "issued": 0}

        def wnext(keep=0):
            i = wstate["cur"]
            wstate["cur"] += 1
            while wstate["issued"] < min(len(blocks), i - keep + NSLOT):
                j = wstate["issued"]
                sl = j % NSLOT
                S.op("pool", lambda e, j=j, sl=sl: e.dma_start(out=wsl[:, sl, :], in_=blocks[j]),
                     writes=[("w", sl)], dsem="w%d" % sl)
                wstate["issued"] += 1
            return i % NSLOT

        gpstate = {"i": 0}

        def gp():
            gpstate["i"] ^= 1
            return gpstate["i"]

        def mm(out, lhsT, rhs, start, stop, reads, bank, skip=False):
            if skip:
                S.op("pe", lambda e: e.matmul(out, lhsT, rhs, start=start, stop=stop, skip_group_check=True),
                     reads=reads, writes=[pbk[bank]])
            else:
                S.op("pe", lambda e: e.matmul(out, lhsT, rhs, start=start, stop=stop),
                     reads=reads, writes=[pbk[bank]])

        S.op("pool", lambda e: e.memset(ident[:], 1.0), writes=["ident"])
        S.op("pool", lambda e: e.affine_select(out=ident[:], in_=ident[:], pattern=[[-1, 128]],
                                               compare_op=ALU.is_equal, fill=0.0, base=0, channel_multiplier=1),
             reads=["ident"], writes=["ident"])
        for i in range(4):
            S.op("pool", lambda e, i=i: e.tensor_copy(I4[:, i * 128:(i + 1) * 128], ident[:]),
                 reads=["ident"], writes=[("I4", i)])
        K_I4 = [("I4", i) for i in range(4)]
        S.op("pool", lambda e: e.memset(ones_s[:], 1.0 / 1024.0), writes=["ones_s"])
        S.op("pool", lambda e: e.memset(ones_row[:], 1.0), writes=["ones_row"])
        S.op("pool", lambda e: e.memset(VA[:, :, 64:65], 1.0), writes=["VA1"])
        S.op("pool", lambda e: e.memset(VB[:, :, :, 64:65], 1.0), writes=["VB1"])
        S.op("pool", lambda e: e.memset(tauc[:], -1e29), writes=["tauc"])
        pre = "all:pre"
        S.op("sp", lambda e: e.dma_start(out=cTf[:], in_=d_cT), writes=["cTf"], dsem=pre)
        S.op("sp", lambda e: e.dma_start(out=invf[:], in_=d_invf.partition_broadcast(128)), writes=["invf"], dsem=pre)
        S.op("sp", lambda e: e.dma_start(out=pw[:], in_=d_pw.partition_broadcast(128)), writes=["pw"], dsem=pre)
        for l in range(L):
            S.op("sp", lambda e, l=l: e.dma_start(out=badaT[:, l, :], in_=d_badaT[l]), writes=[("badaT", l)], dsem=pre)
            S.op("sp", lambda e, l=l: e.dma_start(out=gT[:, l, :], in_=d_gT[l]), writes=[("gT", l)], dsem=pre)
        S.op("act", lambda e: e.activation(out=cact[:], in_=cTf[:], func=AF.Silu), reads=["cTf"], writes=["cact"])

        for l in range(L):
            if l not in layers:
                continue
            if 'mod' in DBG_SKIP:
                for blk in range(NADA):
                    wnext()
                continue
            for blk in range(NADA):
                sl = wnext()
                for jj in range(4):
                    j = blk * 4 + jj
                    for k in range(8):
                        mm(pb[0][:, j * NB:(j + 1) * NB], wsl[:, sl, k * 512 + jj * 128:k * 512 + (jj + 1) * 128],
                           cact[:, k * NB:(k + 1) * NB], k == 0, k == 7, [("w", sl), "cact"], 0)
            S.op("dve", lambda e, l=l: e.tensor_tensor(
                out=modT[:, l, :, :], in0=pb[0][:, 0:48 * NB].rearrange("p (j b) -> p j b", b=NB),
                in1=badaT[:, l, :].unsqueeze(2).to_broadcast([128, 48, NB]), op=ALU.add),
                reads=[("badaT", l)], writes=[pbk[0], ("modT", l)])

        def norm_mod(b, l, g, ci, which):
            gs = slice(g * 512, (g + 1) * 512)
            bank = gp()
            for c in range(8):
                q = c % 2
                S.op("act", lambda e, c=c, q=q: e.activation(out=sqc[:, q, :], in_=xT[:, c, gs], func=AF.Square),
                     reads=[("xT", c, g)], writes=[("sqc", q)])
                mm(pb[bank][:, :], ones_s[:, :], sqc[:, q, :], c == 0, c == 7, [("sqc", q), "ones_s"], bank)
            S.op("act", lambda e: e.activation(out=rstd[:], in_=pb[bank][:, :], func=AF.Sqrt, bias=epsT[:, 0:1]),
                 reads=["epsT"], writes=[pbk[bank], "rstd"])
            S.op("dve", lambda e: e.reciprocal(out=rstd[:], in_=rstd[:]), reads=["rstd"], writes=["rstd"])
            a0 = 3 * which
            for c in range(8):
                q = c % 2
                S.op("dve", lambda e, c=c, q=q: e.scalar_tensor_tensor(
                    out=tmpA[:, q, :], in0=xT[:, c, gs], scalar=coef[:, ci, a0, c:c + 1], in1=rstd[:],
                    op0=ALU.mult, op1=ALU.mult),
                    reads=[("xT", c, g), "rstd", ("coef", ci)], writes=[("tmpA", q)])
                S.op("act", lambda e, c=c, q=q: e.activation(out=hT[:, c, :], in_=tmpA[:, q, :], func=AF.Identity,
                                                             bias=coef[:, ci, a0 + 1, c:c + 1]),
                     reads=[("tmpA", q), ("coef", ci)], writes=[("hT", c)])

        epsT = sb("epsT", [128, 2], F32)
        S.op("pool", lambda e: e.memset(epsT[:], EPS), writes=["epsT"])

        def rope(z, nh, t, zkeys):
            zv = z[:, 0:nh * 64].rearrange("p (h d) -> p h d", d=64)
            x1 = zv[:, :, 0:8]
            x2 = zv[:, :, 8:16]
            cb = cosT[:, t, :].unsqueeze(1).to_broadcast([128, nh, 8])
            sn = sinT[:, t, :].unsqueeze(1).to_broadcast([128, nh, 8])
            tv = [rp[:, i, 0:nh * 8].rearrange("p (h d) -> p h d", d=8) for i in range(4)]
            S.op("dve", lambda e: e.tensor_tensor(out=tv[0], in0=x1, in1=cb, op=ALU.mult), reads=zkeys + ["rope_tab"], writes=[("rp", 0)])
            S.op("dve", lambda e: e.tensor_tensor(out=tv[1], in0=x2, in1=sn, op=ALU.mult), reads=zkeys + ["rope_tab"], writes=[("rp", 1)])
            S.op("dve", lambda e: e.tensor_tensor(out=tv[2], in0=x2, in1=cb, op=ALU.mult), reads=zkeys + ["rope_tab"], writes=[("rp", 2)])
            S.op("dve", lambda e: e.tensor_tensor(out=tv[3], in0=x1, in1=sn, op=ALU.mult), reads=zkeys + ["rope_tab"], writes=[("rp", 3)])
            S.op("dve", lambda e: e.tensor_tensor(out=x1, in0=tv[0], in1=tv[1], op=ALU.subtract),
                 reads=[("rp", 0), ("rp", 1)], writes=zkeys)
            S.op("dve", lambda e: e.tensor_tensor(out=x2, in0=tv[2], in1=tv[3], op=ALU.add),
                 reads=[("rp", 2), ("rp", 3)], writes=zkeys)

        zrot = {"i": 0}

        def z_stage1(blk, sl, tl, t, li):
            ncols = 512 if blk < 5 else 328
            bank = 2 + (zrot["i"] % 6)
            zrot["i"] += 1
            for k in range(8):
                mm(pb[bank][:, 0:ncols], hT[:, k, tl * 128:(tl + 1) * 128], wsl[:, sl, k * 512:k * 512 + ncols],
                   k == 0, False, [("hT", k), ("w", sl)], bank)
            mm(pb[bank][:, 0:ncols], ones_row[:, :], brow[:, blk * 512:blk * 512 + ncols], False, True,
               ["ones_row", "brow"], bank)
            return bank

        def z_stage2(blk, tl, t, li, bank):
            ncols = 512 if blk < 5 else 328
            zi = (blk * 4 + tl) % 4
            zq = (blk * 4 + tl) % 2
            z = ztok[zi]
            zk = K_ztok[zi]
            nh_norm = {0: 8, 1: 8, 2: 8, 5: 2}.get(blk, 0)
            gidx = {0: 0, 1: 2, 2: 3, 5: 1}.get(blk, 0)
            for h in range(nh_norm):
                S.op("act", lambda e, h=h: e.activation(out=sqj[:, :], in_=pb[bank][:, h * 64:(h + 1) * 64],
                                                        func=AF.Square, accum_out=ss[:, zi, h:h + 1]),
                     writes=[pbk[bank], ("ss", zi, h)])
            S.op("act", lambda e: e.activation(out=z[:, 0:ncols], in_=pb[bank][:, 0:ncols], func=AF.Copy),
                 writes=[pbk[bank]] + zk)
            if nh_norm:
                w = nh_norm * 64
                S.op("act", lambda e: e.activation(out=rs[:, zi, 0:nh_norm], in_=ss[:, zi, 0:nh_norm], func=AF.Sqrt,
                                                   scale=1.0 / 64.0, bias=epsT[:, 0:1]),
                     reads=[("ss", zi, h) for h in range(nh_norm)] + ["epsT"], writes=[("rs", zi)])
                S.op("dve", lambda e: e.reciprocal(out=rs[:, zi, 0:nh_norm], in_=rs[:, zi, 0:nh_norm]),
                     reads=[("rs", zi)], writes=[("rs", zi)])
                zv = z[:, 0:w].rearrange("p (h d) -> p h d", d=64)
                S.op("dve", lambda e: e.tensor_tensor(out=zv, in0=zv,
                                                      in1=rs[:, zi, 0:nh_norm].unsqueeze(2).to_broadcast([128, nh_norm, 64]),
                                                      op=ALU.mult),
                     reads=zk + [("rs", zi)], writes=zk)
                S.op("dve", lambda e: e.tensor_tensor(out=zv, in0=zv,
                                                      in1=gains[:, li, gidx * 64:(gidx + 1) * 64].unsqueeze(1).to_broadcast([128, nh_norm, 64]),
                                                      op=ALU.mult),
                     reads=zk + [("gains", li)], writes=zk)
            if blk in (0, 3):
                rope(z, 8, t, zk)
            if blk == 5:
                rope(z, 4, t, zk)
            if blk in (0, 1, 2, 3):
                S.op("pool", lambda e: e.tensor_copy(zb[zq][:, :], z[:, :]), reads=zk, writes=K_zb[zq])
            if blk == 4:
                slot = t % KSLOTS
                S.op("pool", lambda e: e.tensor_copy(VB[:, slot, :, 0:64], z[:, :].rearrange("p (h d) -> p h d", d=64)),
                     reads=zk + ["VB1"], writes=[("VB", slot)])
            if blk == 5:
                S.op("pool", lambda e: e.tensor_copy(zb[zq][:, 0:256], z[:, 0:256]), reads=zk, writes=K_zb[zq])
                S.op("pool", lambda e: e.tensor_copy(VA[:, t, 0:64], z[:, 256:320]), reads=zk + ["VA1"], writes=[("VA", t)])
                S.op("dve", lambda e: e.tensor_scalar(out=wsc[:, tl, :], in0=z[:, 320:328], scalar1=INDEX_SCALE,
                                                      scalar2=None, op0=ALU.mult),
                     reads=zk, writes=[("wsc", tl)])

        def z_stage3(blk, tl, t):
            zq = (blk * 4 + tl) % 2
            if blk in (0, 1, 2, 3):
                tb = gp()
                tbv = pb[tb][:, :].bitcast(BF16)
                for i in range(4):
                    S.op("pe", lambda e, i=i: e.transpose(tbv[:, i * 128:(i + 1) * 128], zb[zq][:, i * 128:(i + 1) * 128], ident[:]),
                         reads=K_zb[zq] + ["ident"], writes=[pbk[tb]])
                src = tbv[:, 0:512].rearrange("p (i q) -> p i q", q=128)
                if blk == 2:
                    slot = t % KSLOTS
                    S.op("dve", lambda e: e.tensor_copy(KBT[:, :, slot, :], src), writes=[pbk[tb], ("KBT", slot)])
                else:
                    b0 = {0: 0, 1: 4, 3: 8}[blk]
                    S.op("dve", lambda e: e.tensor_copy(QT[:, b0:b0 + 4, tl * 128:(tl + 1) * 128], src),
                         writes=[pbk[tb]] + K_QT_blk(b0, b0 + 4, tl))
            if blk == 5:
                tb = gp()
                tbv = pb[tb][:, :].bitcast(BF16)
                for i in range(2):
                    S.op("pe", lambda e, i=i: e.transpose(tbv[:, i * 128:(i + 1) * 128], zb[zq][:, i * 128:(i + 1) * 128], ident[:]),
                         reads=K_zb[zq] + ["ident"], writes=[pbk[tb]])
                S.op("dve", lambda e: e.tensor_copy(KAT[:, t * 128:(t + 1) * 128], tbv[:, 0:128]),
                     writes=[pbk[tb], ("KAT", t)])
                S.op("dve", lambda e: e.tensor_copy(KIT[:, t * 128:(t + 1) * 128], tbv[:, 128:256]),
                     writes=[pbk[tb], ("KIT", t)])

        def zproj_group(g, li):
            items = [(blk, tl) for blk in range(6) for tl in range(4)]
            banks = {}
            sls = {}
            n = len(items)
            for it in range(n + 2):
                if it < n:
                    blk, tl = items[it]
                    if tl == 0:
                        sls[blk] = wnext()
                    banks[it] = z_stage1(blk, sls[blk], tl, g * 4 + tl, li)
                if 0 <= it - 1 < n:
                    blk, tl = items[it - 1]
                    z_stage2(blk, tl, g * 4 + tl, li, banks[it - 1])
                if 0 <= it - 2 < n:
                    blk, tl = items[it - 2]
                    z_stage3(blk, tl, g * 4 + tl)

        negm2 = carve(4096, 4096, BF16)
        junk = carve(0, 4096, BF16)
        K_negm2 = gk(4096, 4096)
        K_junk = gk(0, 4096)
        NEGM = [(negm, K_negm), (negm2, K_negm2)]

        def idx_units(tl, t):
            nkeys = (t + 1) * 128
            units = []
            for c0 in range(0, nkeys, 512):
                w = min(512, nkeys - c0)
                kit_keys = [("KIT", j) for j in range(c0 // 128, (c0 + w) // 128)]
                sc_keys = gk(22528 + c0 * 4, w * 4)
                for hp in range(4):
                    def unit(c0=c0, w=w, hp=hp, kit_keys=kit_keys, sc_keys=sc_keys):
                        for half in range(2):
                            ps_ = slice(half * 64, (half + 1) * 64)
                            mm(pb[half][:, 0:w], QT[ps_, 8 + hp, tl * 128:(tl + 1) * 128], KIT[ps_, c0:c0 + w], True, True,
                               K_QT(8 + hp, tl) + kit_keys, half)
                        for half in range(2):
                            S.op("act", lambda e, half=half: e.activation(out=rl[half][:, 0:w], in_=pb[half][:, 0:w], func=AF.Relu),
                                 writes=[pbk[half]] + K_rl[half])
                        for half in range(2):
                            h = 2 * hp + half
                            if h == 0:
                                S.op("dve", lambda e, half=half: e.tensor_scalar(
                                    out=score[:, c0:c0 + w], in0=rl[half][:, 0:w], scalar1=wsc[:, tl, 0:1], scalar2=None, op0=ALU.mult),
                                    reads=K_rl[half] + [("wsc", tl)], writes=sc_keys)
                            else:
                                S.op("dve", lambda e, half=half, h=h: e.scalar_tensor_tensor(
                                    out=score[:, c0:c0 + w], in0=rl[half][:, 0:w], scalar=wsc[:, tl, h:h + 1], in1=score[:, c0:c0 + w],
                                    op0=ALU.mult, op1=ALU.add),
                                    reads=K_rl[half] + [("wsc", tl)], writes=sc_keys)
                    units.append(unit)
            return units

        junkA = carve(8192, 2048, BF16)
        K_junkA = gk(8192, 2048)

        def bisect_units(tl, t):
            nkeys = (t + 1) * 128
            nm, nmk = NEGM[t % 2]
            sk = K_score
            units = []
            if t >= 2:
                nA = min(1024, (nkeys * 7 // 16) // 64 * 64)
                n1 = nkeys - nA
                thr = TOPK - 0.5 - nA / 2.0

                def prologue():
                    S.op("dve", lambda e: e.tensor_reduce(out=bs[:, 0:1], in_=score[:, 0:nkeys], axis=AX.X, op=ALU.min),
                         reads=sk, writes=["bs0"])
                    S.op("dve", lambda e: e.memset(score[0:64, t * 128 + 64:(t + 1) * 128], -1e30), writes=sk)
                    S.op("dve", lambda e: e.tensor_reduce(out=bs[:, 1:2], in_=score[:, 0:nkeys], axis=AX.X, op=ALU.max),
                         reads=sk, writes=["bs1"])
                    S.op("dve", lambda e: e.tensor_tensor(out=bs[:, 2:3], in0=bs[:, 1:2], in1=bs[:, 0:1], op=ALU.subtract),
                         reads=["bs0", "bs1"], writes=["bs2"])
                    S.op("dve", lambda e: e.tensor_scalar(out=rw[:, :], in0=pw[:, :], scalar1=bs[:, 2:3], scalar2=None, op0=ALU.mult),
                         reads=["pw", "bs2"], writes=["rw"])
                    S.op("dve", lambda e: e.tensor_tensor(out=bs[:, 3:4], in0=bs[:, 0:1], in1=rw[:, 0:1], op=ALU.add),
                         reads=["bs0", "rw"], writes=["mid"])
                units.append(prologue)
                for it in range(NIT):
                    def iteration(it=it):
                        S.op("dve", lambda e: e.tensor_scalar(out=junk[:, 0:n1], in0=score[:, 0:n1], scalar1=bs[:, 3:4],
                                                              scalar2=None, op0=ALU.is_gt, op1=ALU.add, accum_out=bs[:, 4:5]),
                             reads=sk + ["mid"], writes=K_junk + ["cnt"])
                        S.op("act", lambda e: e.activation(out=junkA[:, 0:nA], in_=score[:, n1:nkeys], func=AF.Sign,
                                                           bias=bs[:, 3:4], scale=-1.0, accum_out=bs[:, 7:8]),
                             reads=sk + ["mid"], writes=K_junkA + ["sA"])
                        S.op("dve", lambda e: e.scalar_tensor_tensor(out=bs[:, 8:9], in0=bs[:, 7:8], scalar=-0.5, in1=bs[:, 4:5],
                                                                     op0=ALU.mult, op1=ALU.add),
                             reads=["cnt", "sA"], writes=["cu"])
                        S.op("dve", lambda e: e.tensor_scalar(out=bs[:, 5:6], in0=bs[:, 8:9], scalar1=thr, scalar2=0.5,
                                                              op0=ALU.is_ge, op1=ALU.subtract),
                             reads=["cu"], writes=["tt"])
                        S.op("dve", lambda e: e.scalar_tensor_tensor(out=bs[:, 3:4], in0=bs[:, 5:6], scalar=rw[:, it:it + 1],
                                                                     in1=bs[:, 3:4], op0=ALU.mult, op1=ALU.add),
                             reads=["tt", "rw", "mid"], writes=["mid"])
                    units.append(iteration)

                def epilogue():
                    S.op("dve", lambda e: e.tensor_tensor(out=bs[:, 6:7], in0=bs[:, 3:4], in1=rw[:, NIT:NIT + 1], op=ALU.subtract),
                         reads=["mid", "rw"], writes=["tau"])
                    S.op("dve", lambda e: e.tensor_scalar(out=nm[:, 0:nkeys], in0=score[:, 0:nkeys], scalar1=bs[:, 6:7], scalar2=NEG,
                                                          op0=ALU.is_le, op1=ALU.mult),
                         reads=sk + ["tau"], writes=nmk)
                units.append(epilogue)
            else:
                def simple():
                    S.op("dve", lambda e: e.memset(score[0:64, t * 128 + 64:(t + 1) * 128], -1e30), writes=sk)
                    S.op("dve", lambda e: e.tensor_scalar(out=nm[:, 0:nkeys], in0=score[:, 0:nkeys], scalar1=tauc[:, 0:1], scalar2=NEG,
                                                          op0=ALU.is_le, op1=ALU.mult),
                         reads=sk + ["tauc"], writes=nmk)
                units.append(simple)
            return units

        ptc = {"i": 0}

        def attn_finish(br, tl, obase):
            av = atok[br][:, :].rearrange("p (i two d) -> p i two d", two=2, d=64)
            for half in range(2):
                ob = obase + half
                ov = pb[ob][:, 0:260].rearrange("p (i d) -> p i d", d=65)
                S.op("dve", lambda e, ov=ov, half=half: e.reciprocal(out=rc[:, br * 2 + half, :], in_=ov[:, :, 64]),
                     writes=[pbk[ob], ("rc", br, half)])
                S.op("dve", lambda e, ov=ov, half=half: e.tensor_tensor(
                    out=av[:, :, half, :], in0=ov[:, :, 0:64],
                    in1=rc[:, br * 2 + half, :].unsqueeze(2).to_broadcast([128, 4, 64]), op=ALU.mult),
                    reads=[("rc", br, half)], writes=[pbk[ob]] + K_atok[br])
            tb = gp()
            tbv = pb[tb][:, :].bitcast(BF16)
            for i in range(4):
                S.op("pe", lambda e, i=i: e.transpose(tbv[:, i * 128:(i + 1) * 128], atok[br][:, i * 128:(i + 1) * 128], ident[:]),
                     reads=K_atok[br] + ["ident"], writes=[pbk[tb]])
            S.op("act", lambda e: e.activation(out=attnT[:, br, :, tl * 128:(tl + 1) * 128],
                                               in_=tbv[:, 0:512].rearrange("p (i q) -> p i q", q=128), func=AF.Copy),
                 writes=[pbk[tb], ("attnT", br, tl)])

        def attnA_units(tl, t, banks_of=None):
            nm, nmk = NEGM[t % 2]
            units = []
            for j in range(t + 1):
                def unit(j=j):
                    bset = banks_of(j)
                    if bset is None:
                        bk = st_pair()
                    else:
                        bk = [bset[(2 * j + half) % len(bset)] for half in range(2)]
                    for half in range(2):
                        ps_ = slice(half * 64, (half + 1) * 64)
                        mm(pb[bk[half]][:, :], KAT[ps_, j * 128:(j + 1) * 128], QT[ps_, 0:4, tl * 128:(tl + 1) * 128], True, False,
                           [("KAT", j)] + K_QT_blk(0, 4, tl), bk[half])
                    for half in range(2):
                        mm(pb[bk[half]][:, :], nm[:, j * 128:(j + 1) * 128], I4[:, :], False, True, nmk + K_I4, bk[half])
                    pis = []
                    for half in range(2):
                        pi = ptc["i"] % 4
                        ptc["i"] += 1
                        pis.append(pi)
                        S.op("act", lambda e, pi=pi, half=half: e.activation(out=PT[pi][:, :], in_=pb[bk[half]][:, :], func=AF.Exp),
                             writes=[pbk[bk[half]]] + K_PT[pi])
                    for half in range(2):
                        pi = pis[half]
                        for i in range(4):
                            mm(pb[4 + half][:, i * 65:(i + 1) * 65], PT[pi][:, i * 128:(i + 1) * 128], VA[:, j, :],
                               (j == 0 and i == 0), j == t, K_PT[pi] + [("VA", j), "VA1"], 4 + half, skip=True)
                units.append(unit)
            return units

        strot = {"i": 0}

        def st_pair():
            strot["i"] ^= 1
            return (2, 3) if strot["i"] else (0, 1)

        def attnB_units(tl, t):
            j0 = max(0, t - 4)
            units = []
            for j in range(j0, t + 1):
                def unit(j=j):
                    jrel = j - (t - 4)
                    slot = j % KSLOTS
                    bk = st_pair()
                    for half in range(2):
                        ps_ = slice(half * 64, (half + 1) * 64)
                        for i in range(4):
                            mm(pb[bk[half]][:, i * 128:(i + 1) * 128], KBT[ps_, i, slot, :], QT[ps_, 4 + i, tl * 128:(tl + 1) * 128],
                               i == 0, False, [("KBT", slot)] + K_QT(4 + i, tl), bk[half], skip=True)
                    for half in range(2):
                        mm(pb[bk[half]][:, :], ident[:, :], biasT[:, jrel, half, :], False, True, ["ident", "biasT"], bk[half], skip=True)
                    pis = []
                    for half in range(2):
                        pi = ptc["i"] % 4
                        ptc["i"] += 1
                        pis.append(pi)
                        S.op("act", lambda e, pi=pi, half=half: e.activation(out=PT[pi][:, :], in_=pb[bk[half]][:, :], func=AF.Exp),
                             writes=[pbk[bk[half]]] + K_PT[pi])
                    for half in range(2):
                        pi = pis[half]
                        for i in range(4):
                            mm(pb[6 + half][:, i * 65:(i + 1) * 65], PT[pi][:, i * 128:(i + 1) * 128], VB[:, slot, 2 * i + half, :],
                               (j == j0 and i == 0), j == t, K_PT[pi] + [("VB", slot), "VB1"], 6 + half, skip=True)
                units.append(unit)
            return units

        def interleave(ua, ub):
            na, nb_ = len(ua), len(ub)
            ia = ib = 0
            while ia < na or ib < nb_:
                if ib >= nb_ or (ia < na and ia * nb_ <= ib * na):
                    ua[ia]()
                    ia += 1
                else:
                    ub[ib]()
                    ib += 1

        def attention_group(g, tail_units=()):
            t0 = g * 4
            for u in idx_units(0, t0):
                u()
            interleave(bisect_units(0, t0), attnB_units(0, t0))
            attn_finish(1, 0, 6)
            for k in range(4):
                t = t0 + k
                if k < 3:
                    nu = t + 1
                    n1 = int(nu * A_FRAC)
                    ua = attnA_units(k, t, lambda u, n1=n1: (2, 3, 6, 7) if u < n1 else None)
                    interleave(idx_units(k + 1, t + 1), ua[:n1])
                    interleave(bisect_units(k + 1, t + 1), attnB_units(k + 1, t + 1) + ua[n1:])
                    attn_finish(0, k, 4)
                    attn_finish(1, k + 1, 6)
                else:
                    interleave(attnA_units(k, t, lambda u: (2, 3, 6, 7)), list(tail_units))
                    attn_finish(0, k, 4)

        def mixer_group(b, l, g, ci, li, skip_norm=False):
            gs = slice(g * 512, (g + 1) * 512)
            if DBG_STAGE < 1:
                return
            if not skip_norm:
                norm_mod(b, l, g, ci, 0)
            if DBG_STAGE < 6:
                dump(hT[:, :, :].rearrange("p c t -> p (c t)"), 0, 4096, [("hT", c) for c in range(8)])
            if DBG_STAGE < 2:
                return
            zproj_group(g, li)
            if DBG_STAGE < 6:
              dump(QT[:, :, :].rearrange("p c t -> p (c t)"), 4096, 6144, K_QT_blk(0, 12, 0) + K_QT_blk(0, 12, 1) + K_QT_blk(0, 12, 2) + K_QT_blk(0, 12, 3))
            if DBG_STAGE < 6:
              dump(KAT[:, 0:512], 10240, 512, [("KAT", j) for j in range(4)])
            dump(KIT[:, 0:512], 10752, 512, [("KIT", j) for j in range(4)])
            if DBG_STAGE < 3:
                return
            gsl = {}
            gate_units = []
            for blk in range(4):
                for jj in range(4):
                    def gunit(blk=blk, jj=jj):
                        if jj == 0:
                            gsl[blk] = wnext()
                        sl = gsl[blk]
                        c = blk * 4 + jj
                        bank = gp()
                        for k in range(8):
                            mm(pb[bank][:, :], wsl[:, sl, k * 512 + jj * 128:k * 512 + (jj + 1) * 128], hT[:, k, :], k == 0, k == 7,
                               [("w", sl), ("hT", k)], bank)
                        S.op("act", lambda e: e.activation(out=gates[:, c, :], in_=pb[bank][:, :], func=AF.Tanh,
                                                           bias=bgT[:, li, c:c + 1], scale=0.5),
                             reads=[("bgT", li)], writes=[pbk[bank]] + K_gates(c))
                    gate_units.append(gunit)
            if DBG_STAGE >= 6:
                attention_group(g, gate_units)
            else:
                attention_group(g)
            dump(attnT[:, :, :, :].rearrange("p a c t -> p (a c t)"), 12288, 4096, [("attnT", br, x) for br in range(2) for x in range(4)])
            if DBG_STAGE < 6:
                return
            sla = wnext()
            slb = wnext(keep=1)
            for c in range(8):
                for br, slx in ((0, sla), (1, slb)):
                    bank = gp()
                    for k in range(4):
                        mm(pb[bank][:, :], wsl[:, slx, k * 1024 + c * 128:k * 1024 + (c + 1) * 128], attnT[:, br, k, :], k == 0, k == 3,
                           [("w", slx)] + [("attnT", br, x) for x in range(4)], bank)
                    S.op("dve", lambda e, c=c, br=br, bank=bank: e.scalar_tensor_tensor(
                        out=tmpA[:, br, :], in0=gates[:, br * 8 + c, :], scalar=1.0, in1=pb[bank][:, :],
                        op0=ALU.add, op1=ALU.mult),
                         reads=K_gates(br * 8 + c), writes=[pbk[bank], ("tmpA", br)])
                S.op("pool", lambda e, c=c: e.tensor_tensor(out=merged[:, c, :], in0=tmpA[:, 0, :], in1=tmpA[:, 1, :], op=ALU.add),
                     reads=[("tmpA", 0), ("tmpA", 1)], writes=K_merged(c))
            for i in range(2):
                sl = wnext()
                for cc in range(4):
                    c = i * 4 + cc
                    bank = gp()
                    for k in range(8):
                        mm(pb[bank][:, :], wsl[:, sl, k * 512 + cc * 128:k * 512 + (cc + 1) * 128], merged[:, k, :], k == 0, k == 7,
                           [("w", sl)] + K_merged(k), bank)
                    S.op("dve", lambda e, c=c, bank=bank: e.scalar_tensor_tensor(
                        out=xT[:, c, gs], in0=pb[bank][:, :], scalar=coef[:, ci, 2, c:c + 1], in1=xT[:, c, gs],
                        op0=ALU.mult, op1=ALU.add),
                        reads=[("coef", ci)], writes=[pbk[bank], ("xT", c, g)])
            if 'late' in DBG_SKIP:
                late_dumps.append((gs, g))
            elif DBG_STAGE >= 6:
                if 'mdump' not in DBG_SKIP:
                    dump(merged[:, :, :].rearrange("p c t -> p (c t)"), 0, 4096, sum([K_merged(c) for c in range(8)], []))
                if 'gdump' not in DBG_SKIP:
                    dump(gates[:, 0:8, :].rearrange("p c t -> p (c t)"), 8192, 4096, sum([K_gates(c) for c in range(8)], []))
                for c in range(8):
                    if 'xdump' not in DBG_SKIP:
                        dump(xT[:, c, gs], 4096 + c * 512, 512, [("xT", c, g)])

        late_dumps = []

        def ffn_group(b, l, g, ci, mid_hook=None):
            gs = slice(g * 512, (g + 1) * 512)
            norm_mod(b, l, g, ci, 1)
            if late_dumps:
                late_dumps.pop()
                for c in range(8):
                    dump(xT[:, c, gs], 4096 + c * 512, 512, [("xT", c, g)])
            pr_i = 0
            for i in range(11):
                sl = wnext()
                for pr in range(2):
                    j = 2 * i + pr
                    bg = 2 * (pr_i % 4)
                    bu = bg + 1
                    pr_i += 1
                    for k in range(8):
                        mm(pb[bg][:, :], wsl[:, sl, k * 512 + (2 * pr) * 128:k * 512 + (2 * pr + 1) * 128], hT[:, k, :], k == 0, k == 7,
                           [("w", sl), ("hT", k)], bg)
                    for k in range(8):
                        mm(pb[bu][:, :], wsl[:, sl, k * 512 + (2 * pr + 1) * 128:k * 512 + (2 * pr + 2) * 128], hT[:, k, :], k == 0, k == 7,
                           [("w", sl), ("hT", k)], bu)
                    q = j % 2
                    S.op("act", lambda e, bg=bg, q=q: e.activation(out=sqc[:, q, :], in_=pb[bg][:, :], func=AF.Silu),
                         writes=[pbk[bg], ("sqc", q)])
                    S.op("dve", lambda e, bu=bu, q=q, j=j: e.tensor_tensor(out=hidden[:, j, :], in0=pb[bu][:, :], in1=sqc[:, q, :], op=ALU.mult),
                         reads=[("sqc", q)], writes=[pbk[bu]] + K_hidden(j))
            for hf in range(2):
                if hf == 1 and mid_hook is not None:
                    mid_hook()
                base = 4 if hf == 0 else 0
                for kb in range(3):
                    sl = wnext()
                    nk = 8 if kb < 2 else 6
                    for kk in range(nk):
                        kt = kb * 8 + kk
                        for cc in range(4):
                            mm(pb[base + cc][:, :], wsl[:, sl, kk * 512 + cc * 128:kk * 512 + (cc + 1) * 128], hidden[:, kt, :],
                               kt == 0, kt == 21, [("w", sl)] + K_hidden(kt), base + cc)
                for cc in range(4):
                    c = hf * 4 + cc
                    S.op("dve", lambda e, c=c, bank=base + cc: e.scalar_tensor_tensor(
                        out=xT[:, c, gs], in0=pb[bank][:, :], scalar=coef[:, ci, 5, c:c + 1], in1=xT[:, c, gs],
                        op0=ALU.mult, op1=ALU.add),
                        reads=[("coef", ci)], writes=[pbk[base + cc], ("xT", c, g)])

        out_ops = []
        inst = 0
        for b in range(NB):
            xk_all = [("xT", c, g) for c in range(8) for g in range(NG)]
            for c in range(8):
                S.op("sp", lambda e, b=b, c=c: e.dma_start(out=xT[:, c, :], in_=d_xT[b, c]),
                     writes=[("xT", c, g) for g in range(NG)], dsem="all:x%d" % b)
            S.op("sp", lambda e, b=b: e.dma_start(out=posi[:], in_=d_pos[b]), writes=["posi"], dsem="all:pos%d" % b)
            S.op("dve", lambda e: e.tensor_copy(posf[:], posi[:]), reads=["posi"], writes=["posf"])
            angv = ang[:, :].rearrange("p (t f) -> p t f", f=8)
            S.op("dve", lambda e: e.tensor_tensor(out=angv, in0=posf[:, :].unsqueeze(2).to_broadcast([128, NT, 8]),
                                                  in1=invf[:, :].unsqueeze(1).to_broadcast([128, NT, 8]), op=ALU.mult),
                 reads=["posf", "invf"], writes=["ang"])
            for which, dst in ((0, sinT), (1, cosT)):
                if 'rope' in DBG_SKIP:
                    continue
                r_ = rr[:, which, :]
                wk = ("rr", which)
                S.op("dve", lambda e, which=which, r_=r_: e.tensor_scalar(out=r_, in0=ang[:, :], scalar1=1.0 / (2 * np.pi),
                                                                         scalar2=0.25 * which, op0=ALU.mult, op1=ALU.add),
                     reads=["ang"], writes=[wk])
                S.op("dve", lambda e, r_=r_: e.tensor_copy(rki[:, :], r_), reads=[wk], writes=["rki"])
                S.op("dve", lambda e: e.tensor_copy(rkf[:, :], rki[:, :]), reads=["rki"], writes=["rkf"])
                S.op("dve", lambda e, r_=r_: e.scalar_tensor_tensor(out=r_, in0=rkf[:, :], scalar=-6.28125, in1=ang[:, :],
                                                                    op0=ALU.mult, op1=ALU.add),
                     reads=["rkf", "ang"], writes=[wk])
                S.op("dve", lambda e, r_=r_: e.scalar_tensor_tensor(out=r_, in0=rkf[:, :], scalar=-(2 * np.pi - 6.28125), in1=r_,
                                                                    op0=ALU.mult, op1=ALU.add),
                     reads=["rkf", wk], writes=[wk])
                S.op("dve", lambda e, r_=r_, which=which: e.tensor_scalar(out=r_, in0=r_, scalar1=(np.pi / 2) * which, scalar2=3.1415925,
                                                                         op0=ALU.add, op1=ALU.min),
                     reads=[wk], writes=[wk])
                S.op("dve", lambda e, r_=r_: e.tensor_scalar(out=r_, in0=r_, scalar1=-3.1415925, scalar2=None, op0=ALU.max),
                     reads=[wk], writes=[wk])
                S.op("act", lambda e, r_=r_, dst=dst: e.activation(out=dst[:, :, :].rearrange("p t f -> p (t f)"), in_=r_, func=AF.Sin),
                     reads=[wk], writes=["rope_tab"])
            for l in layers:
                ci = inst % 2
                li = inst % 2
                inst += 1
                ld = "all:ld%d_%d" % (b, l)
                ldp = "all:ldp%d_%d" % (b, l)
                S.op("sp", lambda e, l=l, li=li: e.dma_start(out=bgT[:, li, :], in_=d_bgT[l]), writes=[("bgT", li)], dsem=ld)
                S.op("sp", lambda e, l=l, li=li: e.dma_start(out=gains[:, li, :], in_=d_gains[l].partition_broadcast(128)),
                     writes=[("gains", li)], dsem=ld)
                if 'brow' not in DBG_SKIP:
                    S.op("pool", lambda e, l=l: e.dma_start(out=brow[:, :], in_=d_brow[l]), writes=["brow"], dsem=ldp)
                S.op("pool", lambda e, l=l: e.dma_start(out=biasT[:, :, :, :].rearrange("p a b c -> p (a b c)"), in_=d_biasT[l]),
                     writes=["biasT"], dsem=ldp)
                S.op("dve", lambda e, li=li: e.tensor_scalar(
                    out=gains[:, li, :].rearrange("p (a b d) -> p a b d", b=2, d=64)[:, :, 0, :],
                    in0=gains[:, li, :].rearrange("p (a b d) -> p a b d", b=2, d=64)[:, :, 0, :],
                    scalar1=0.125, scalar2=None, op0=ALU.mult),
                    reads=[("gains", li)], writes=[("gains", li)])
                S.op("dve", lambda e, li=li: e.tensor_scalar(out=bgT[:, li, :], in0=bgT[:, li, :], scalar1=0.5, scalar2=None, op0=ALU.mult),
                     reads=[("bgT", li)], writes=[("bgT", li)])
                for which in range(2):
                    if 'coef' in DBG_SKIP:
                        continue
                    o3 = 24 * which
                    S.op("dve", lambda e, l=l, b=b, o3=o3, which=which, ci=ci: e.scalar_tensor_tensor(
                        out=coef[:, ci, 3 * which, :], in0=modT[:, l, o3 + 8:o3 + 16, b], scalar=1.0,
                        in1=gT[:, l, 8 * which:8 * which + 8], op0=ALU.add, op1=ALU.mult),
                        reads=[("modT", l), ("gT", l)], writes=[("coef", ci)])
                    S.op("dve", lambda e, l=l, b=b, o3=o3, which=which, ci=ci: e.tensor_copy(coef[:, ci, 3 * which + 1, :], modT[:, l, o3:o3 + 8, b]),
                         reads=[("modT", l)], writes=[("coef", ci)])
                    S.op("dve", lambda e, l=l, b=b, o3=o3, which=which, ci=ci: e.tensor_scalar(
                        out=coef[:, ci, 3 * which + 2, :], in0=modT[:, l, o3 + 16:o3 + 24, b],
                        scalar1=(0.5 if which == 0 else 1.0), scalar2=None, op0=ALU.mult),
                         reads=[("modT", l)], writes=[("coef", ci)])
                for g in range(NG):
                    if DBG_STAGE < 99 and (b > 0 or g > 0):
                        continue
                    mixer_group(b, l, g, ci, li, skip_norm=(g > 0 and DBG_STAGE >= 99))
                    if DBG_STAGE >= 7:
                        hook = None
                        if g < NG - 1 and DBG_STAGE >= 99:
                            hook = (lambda b=b, l=l, g=g, ci=ci: norm_mod(b, l, g + 1, ci, 0))
                        ffn_group(b, l, g, ci, hook)
            for c in range(8):
                o = S.op("sp", lambda e, b=b, c=c: e.dma_start(out=d_out[b, c], in_=xT[:, c, :]),
                         reads=[("xT", c, g) for g in range(NG)], dsem="all:o%d" % b)
                out_ops.append(o)
        assert DBG_STAGE < 99 or wstate["cur"] == len(blocks), (wstate, len(blocks))
        S.emit(final_wait_ops=out_ops + dbg_ops)
    return nc


def _blockify(W, ncols_blk=512):
    K, N = W.shape
    assert N % ncols_blk == 0
    kc = K // 128
    out = np.zeros((N // ncols_blk, 128, 8, ncols_blk), np.float32)
    Wr = W.reshape(kc, 128, N // ncols_blk, ncols_blk)
    out[:, :, :kc, :] = Wr.transpose(2, 1, 0, 3)
    return out.reshape(N // ncols_blk, 128, 8 * ncols_blk)


def _prep_shared(inp):
    L = DEPTH
    offs = np.concatenate([[0], np.cumsum(IN_SIZES)])
    rng_ = lambda i: np.arange(offs[i], offs[i + 1])
    qa, ka, va, qi, ki, wi, qb, kb, vb, ga, gb = [rng_(i) for i in range(11)]
    tok_cols = np.concatenate([qa, qb, kb, qi, vb, ka, ka, ki, ki, va, wi])
    gate_cols = np.concatenate([ga, gb])
    gu_cols = []
    for i in range(11):
        for pr in range(2):
            j = 2 * i + pr
            gu_cols.append(np.arange(j * 128, (j + 1) * 128))
            gu_cols.append(D_FF + np.arange(j * 128, (j + 1) * 128))
    gu_cols = np.concatenate(gu_cols)
    wst = np.zeros((L, NWB, 128, 4096), np.float32)
    wada = np.zeros((L, NADA, 128, 4096), np.float32)
    brow = np.zeros((L, 128, 3072), np.float32)
    bgT = np.zeros((L, 128, 16), np.float32)
    badaT = np.zeros((L, 128, 48), np.float32)
    gT = np.zeros((L, 128, 16), np.float32)
    gains = np.zeros((L, 1, 256), np.float32)
    biasT = np.zeros((L, 128, 5120), np.float32)
    k_ = np.arange(128)[:, None, None]
    j_ = np.arange(5)[None, :, None]
    q_ = np.arange(128)[None, None, :]
    tdiff = q_ - k_ + 128 * (4 - j_)
    ridx = np.clip(tdiff, -63, 256) + 63
    cq = q_ // 64
    ck = 2 * j_ + k_ // 64
    vis = (ck >= cq) & (ck <= 8 + cq)
    for l in range(L):
        w_in = np.asarray(inp["w_in"][l])
        wtok = np.zeros((1024, 3072), np.float32)
        wtok[:, :2888] = w_in[:, tok_cols]
        wst[l, 0:6] = _blockify(wtok)
        wst[l, 6:10] = _blockify(w_in[:, gate_cols])
        woa = np.asarray(inp["w_oa"][l]).reshape(4, 128, 1024).transpose(1, 0, 2).reshape(128, 4096)
        wob = np.asarray(inp["w_ob"][l]).reshape(4, 128, 1024).transpose(1, 0, 2).reshape(128, 4096)
        wst[l, 10] = woa
        wst[l, 11] = wob
        wst[l, 12:14] = _blockify(np.asarray(inp["w_out"][l]))
        wst[l, 14:25] = _blockify(np.asarray(inp["w_gu"][l])[:, gu_cols])
        wd = np.zeros((3072, 1024), np.float32)
        wd[:D_FF] = np.asarray(inp["w_down"][l])
        for hf in range(2):
            for kbk in range(3):
                sub = wd[kbk * 1024:(kbk + 1) * 1024, hf * 512:(hf + 1) * 512]
                wst[l, 25 + hf * 3 + kbk] = _blockify(sub)[0]
        wada[l] = _blockify(np.asarray(inp["w_ada"][l]))
        b_in = np.asarray(inp["b_in"][l])
        brow[l, 0, :2888] = b_in[tok_cols]
        bgT[l] = b_in[gate_cols].reshape(16, 128).T
        badaT[l] = np.asarray(inp["b_ada"][l]).reshape(48, 128).T
        gT[l, :, 0:8] = np.asarray(inp["g_mix"][l]).reshape(8, 128).T
        gT[l, :, 8:16] = np.asarray(inp["g_ffn"][l]).reshape(8, 128).T
        gains[l, 0] = np.concatenate([np.asarray(inp[k][l]) for k in ("qn_a", "kn_a", "qn_b", "kn_b")])
        rb = np.asarray(inp["rel_bias"][l])
        g_ = rb[:, ridx]
        g_ = np.where(vis[None], g_, np.float32(NEG)).astype(np.float32)
        g_ = g_.reshape(4, 2, 128, 5, 128)
        biasT[l] = g_.transpose(2, 3, 1, 0, 4).reshape(128, 5120)
    invf = (np.float32(ROPE_THETA) ** (-np.arange(0, 16, 2, dtype=np.float32) / np.float32(16))).astype(np.float32).reshape(1, 8)
    pw = (0.5 ** np.arange(1, NIT + 2)).astype(np.float32).reshape(1, NIT + 1)
    return dict(wst=wst, wada=wada, brow=brow, bgT=bgT, badaT=badaT, gT=gT, gains=gains, biasT=biasT, invf=invf, pw=pw)


def _prep_core(x_t, c, positions, core):
    bsl = slice(core * NB, (core + 1) * NB)
    xT = np.ascontiguousarray(x_t[bsl].transpose(0, 2, 1)).reshape(NB, 8, 128, SEQ)
    cT = np.ascontiguousarray(np.asarray(c)[bsl].reshape(NB, 8, 128).transpose(2, 1, 0)).reshape(128, 8 * NB)
    pos = np.ascontiguousarray(np.asarray(positions)[bsl].reshape(NB, NT, 128).transpose(0, 2, 1)).astype(np.int32)
    return dict(xT=xT, cT=cT, pos=pos)


_PROG_CACHE = {}


def _get_prog(layers):
    key = tuple(layers)
    if key not in _PROG_CACHE:
        _PROG_CACHE[key] = build_program(list(layers))
    return _PROG_CACHE[key]


def _run(x_cur, c, positions, shared, layers):
    nc = _get_prog(layers)
    in_maps = []
    for core in range(N_CORES):
        m = dict(shared)
        m.update(_prep_core(x_cur, c, positions, core))
        in_maps.append(m)
    res = run_bass_kernel_spmd(nc, in_maps, core_ids=list(range(N_CORES)))
    outs = []
    for core in range(N_CORES):
        oT = np.asarray(res.results[core]["oT"]).reshape(NB, 1024, SEQ)
        outs.append(oT.transpose(0, 2, 1))
    return np.ascontiguousarray(np.concatenate(outs, axis=0)).astype(np.float32)


FUSED = True


def kernel(**inputs):
    inp = {k: np.asarray(v) for k, v in inputs.items()}
    shared = _prep_shared(inp)
    x = inp["x"].astype(np.float32)
    if FUSED:
        return _run(x, inp["c"], inp["positions"], shared, list(range(DEPTH)))
    for l in range(DEPTH):
        x = _run(x, inp["c"], inp["positions"], shared, [l])
    return x
```
